# Optimizing a Trainium2 kernel written in Bass

```python
import math
import jax
import jax.numpy as jnp
from jax import lax
import numpy as np

D_MODEL = 1024
BATCH = 32
SEQ = 2048
DEPTH = 4

GRID_W = 64
CTX_LEN = 256
N_MIXERS = 3
EXPAND = 2
E_WIDTH = EXPAND * D_MODEL
HEAD_DIM = 128
N_Q_HEADS = E_WIDTH // HEAD_DIM
N_KV_HEADS = max(1, N_Q_HEADS // 4)
Q_PER_KV = N_Q_HEADS // N_KV_HEADS
KV_WIDTH = N_KV_HEADS * HEAD_DIM
WINDOW = 128
ATTN_BLOCK = 128
ROPE_THETA = 10000.0
S5_GROUP = 16
S5_GROUPS = E_WIDTH // S5_GROUP
S5_STATE = 64
S5_CHUNK = 128
HG_HEAD_DIM = 128
HG_HEADS = E_WIDTH // HG_HEAD_DIM
HG_CHUNK = 64
ATTN_IN = 2 * E_WIDTH + 2 * KV_WIDTH
S5_IN = 2 * E_WIDTH
HG_IN = 5 * E_WIDTH
NORM_EPS = 1e-5
NEG_INF = -1e30

kernel_name = 'hybrid_diffusion_gqa_s5_hgrn2'


def layer_norm(x, g, b):
    xf = x.astype(jnp.float32)
    mu = jnp.mean(xf, -1, keepdims=True)
    var = jnp.mean(jnp.square(xf - mu), -1, keepdims=True)
    y = (xf - mu) * lax.rsqrt(var + NORM_EPS) * g.astype(jnp.float32) + b.astype(jnp.float32)
    return y.astype(x.dtype)


def grid_positions(n_tokens):
    rows = n_tokens // GRID_W
    row = jnp.repeat(jnp.arange(rows, dtype=jnp.float32), GRID_W)
    col = jnp.tile(jnp.arange(GRID_W, dtype=jnp.float32), rows)
    return row, col


def rope_2d(x, row, col):
    half = x.shape[-1] // 2
    nf = half // 2
    inv_freq = jnp.power(ROPE_THETA, -jnp.arange(nf, dtype=jnp.float32) / nf)
    xf = x.astype(jnp.float32)

    def rotate(xp, pos):
        ang = pos[:, None] * inv_freq[None, :]
        cos = jnp.cos(ang)[None, :, None, :]
        sin = jnp.sin(ang)[None, :, None, :]
        x1, x2 = xp[..., :nf], xp[..., nf:]
        return jnp.concatenate([x1 * cos - x2 * sin, x2 * cos + x1 * sin], -1)

    return jnp.concatenate([rotate(xf[..., :half], row), rotate(xf[..., half:], col)], -1).astype(x.dtype)


def softmax_with_sink(s, sink):
    sk = jnp.broadcast_to(sink[None, :, :, None, None], s.shape[:-1] + (1,))
    return jax.nn.softmax(jnp.concatenate([s, sk], -1), axis=-1)[..., :-1]


def attention_mixer(p, pc, sink, row, col, ctx_out):
    bsz, n_lat, _ = p.shape
    n_ctx = pc.shape[1]

    def split(t):
        b_, l_ = t.shape[0], t.shape[1]
        q = t[..., :E_WIDTH].reshape(b_, l_, N_Q_HEADS, HEAD_DIM)
        k = t[..., E_WIDTH:E_WIDTH + KV_WIDTH].reshape(b_, l_, N_KV_HEADS, HEAD_DIM)
        v = t[..., E_WIDTH + KV_WIDTH:].reshape(b_, l_, N_KV_HEADS, HEAD_DIM)
        return q, k, v

    q, k, v = split(p)
    qc, kc, vc = split(pc)
    q = rope_2d(q, row, col).reshape(bsz, n_lat, N_KV_HEADS, Q_PER_KV, HEAD_DIM)
    k = rope_2d(k, row, col)
    qc = qc.reshape(bsz, n_ctx, N_KV_HEADS, Q_PER_KV, HEAD_DIM)
    scale = HEAD_DIM ** -0.5
    sink_r = sink.astype(jnp.float32).reshape(N_KV_HEADS, Q_PER_KV)

    n_blk = n_lat // ATTN_BLOCK
    n_key = ATTN_BLOCK + 2 * WINDOW
    qb = q.reshape(bsz, n_blk, ATTN_BLOCK, N_KV_HEADS, Q_PER_KV, HEAD_DIM).swapaxes(0, 1)
    pad = ((0, 0), (WINDOW, WINDOW), (0, 0), (0, 0))
    kp = jnp.pad(k, pad)
    vp = jnp.pad(v, pad)
    q_off = jnp.arange(ATTN_BLOCK)
    k_off = jnp.arange(n_key) - WINDOW

    def block(args):
        j, qj = args
        start = j * ATTN_BLOCK
        kj = lax.dynamic_slice_in_dim(kp, start, n_key, axis=1)
        vj = lax.dynamic_slice_in_dim(vp, start, n_key, axis=1)
        qpos = start + q_off
        kpos = start + k_off
        valid = ((jnp.abs(qpos[:, None] - kpos[None, :]) <= WINDOW)
                 & (kpos >= 0)[None, :] & (kpos < n_lat)[None, :])
        s_win = jnp.einsum('bqhgd,bkhd->bhgqk', qj, kj, preferred_element_type=jnp.float32) * scale
        s_win = jnp.where(valid, s_win, NEG_INF)
        s_ctx = jnp.einsum('bqhgd,bchd->bhgqc', qj, kc, preferred_element_type=jnp.float32) * scale
        pr = softmax_with_sink(jnp.concatenate([s_win, s_ctx], -1), sink_r).astype(vj.dtype)
        return (jnp.einsum('bhgqk,bkhd->bqhgd', pr[..., :n_key], vj)
                + jnp.einsum('bhgqc,bchd->bqhgd', pr[..., n_key:], vc))

    o = lax.map(block, (jnp.arange(n_blk), qb)).swapaxes(0, 1).reshape(bsz, n_lat, E_WIDTH)
    if not ctx_out:
        return o, None
    sc = jnp.einsum('bqhgd,bkhd->bhgqk', qc, kc, preferred_element_type=jnp.float32) * scale
    prc = softmax_with_sink(sc, sink_r).astype(vc.dtype)
    oc = jnp.einsum('bhgqk,bkhd->bqhgd', prc, vc).reshape(bsz, n_ctx, E_WIDTH)
    return o, oc


def flip_seq(t, rev):
    return jnp.flip(t, axis=1) if rev else t


def _linrec_combine(e1, e2):
    a1, b1 = e1
    a2, b2 = e2
    return a1 * a2, a2 * b1 + b2


def s5_discretise(lam_re, lam_im, log_step, b_re, b_im, c_re, c_im):
    lam = lax.complex(lam_re.astype(jnp.float32), lam_im.astype(jnp.float32))
    dt = jnp.exp(log_step.astype(jnp.float32))[:, None]
    abar = jnp.exp(lam * dt)
    bmat = lax.complex(b_re.astype(jnp.float32), b_im.astype(jnp.float32))
    bbar = ((abar - 1.0) / lam)[..., None] * bmat
    cmat = lax.complex(c_re.astype(jnp.float32), c_im.astype(jnp.float32))
    return abar, bbar, cmat


def s5_scan(u, abar, bbar, cmat, h0):
    bsz, n_tok, n_g, n_c = u.shape
    n_chunk = n_tok // S5_CHUNK
    uc = u.reshape(bsz, n_chunk, S5_CHUNK, n_g, n_c).swapaxes(0, 1)

    def step(h, u_blk):
        bu = jnp.einsum('btgn,gpn->btgp', u_blk.astype(jnp.complex64), bbar)
        bu = bu.at[:, 0].add(abar[None] * h)
        a = jnp.broadcast_to(abar, bu.shape)
        _, hs = lax.associative_scan(_linrec_combine, (a, bu), axis=1)
        y = jnp.einsum('btgp,gnp->btgn', hs, cmat).real
        return hs[:, -1], y

    h_last, ys = lax.scan(step, h0, uc)
    return ys.swapaxes(0, 1).reshape(bsz, n_tok, n_g, n_c), h_last


def s5_mixer(p, pc, lam_re, lam_im, log_step, b_re, b_im, c_re, c_im, d_skip, glu_w, glu_b):
    bsz, n_lat, _ = p.shape
    n_ctx = pc.shape[1]
    u = p.astype(jnp.float32).reshape(bsz, n_lat, S5_GROUPS, S5_GROUP)
    uc = pc.astype(jnp.float32).reshape(bsz, n_ctx, S5_GROUPS, S5_GROUP)
    d = d_skip.astype(jnp.float32).reshape(S5_GROUPS, S5_GROUP)
    y = d * u
    yc = d * uc
    for rev in (0, 1):
        abar, bbar, cmat = s5_discretise(lam_re[rev], lam_im[rev], log_step[rev],
                                         b_re[rev], b_im[rev], c_re[rev], c_im[rev])
        h0 = jnp.zeros((bsz, S5_GROUPS, S5_STATE), jnp.complex64)
        yc_dir, h_ctx = s5_scan(flip_seq(uc, rev), abar, bbar, cmat, h0)
        y_dir, _ = s5_scan(flip_seq(u, rev), abar, bbar, cmat, h_ctx)
        y = y + flip_seq(y_dir, rev)
        yc = yc + flip_seq(yc_dir, rev)
    w = glu_w.astype(jnp.float32)
    bias = glu_b.astype(jnp.float32)

    def glu(t, n):
        g = jax.nn.gelu(t.reshape(bsz, n, E_WIDTH))
        return (g * jax.nn.sigmoid(g @ w + bias)).astype(p.dtype)

    return glu(y, n_lat), glu(yc, n_ctx)


def hgrn2_scan(q, k, v, logf, s0):
    bsz, n_tok, n_h, _ = q.shape
    n_chunk = n_tok // HG_CHUNK
    mid = HG_CHUNK // 2
    order_mask = jnp.tril(jnp.ones((HG_CHUNK, HG_CHUNK), dtype=bool))

    def blocks(t):
        return t.reshape(bsz, n_chunk, HG_CHUNK, n_h, t.shape[-1]).transpose(1, 0, 3, 2, 4)

    def step(s, blk):
        qb, kb, vb, gb = blk
        b = jnp.cumsum(gb, axis=2)
        ref = b[:, :, mid:mid + 1]
        att = jnp.einsum('bhtk,bhsk->bhts', qb * jnp.exp(b - ref), kb * jnp.exp(ref - b))
        att = jnp.where(order_mask, att, 0.0)
        o = (jnp.einsum('bhts,bhsv->bhtv', att, vb)
             + jnp.einsum('bhtk,bhkv->bhtv', qb * jnp.exp(b), s))
        b_last = b[:, :, -1:]
        s = (jnp.exp(b_last[:, :, 0])[..., None] * s
             + jnp.einsum('bhsk,bhsv->bhkv', kb * jnp.exp(b_last - b), vb))
        return s, o

    s_last, outs = lax.scan(step, s0, (blocks(q), blocks(k), blocks(v), blocks(logf)))
    return outs.transpose(1, 0, 3, 2, 4).reshape(bsz, n_tok, n_h, v.shape[-1]), s_last


def hgrn2_mixer(p, pc, lb, norm_g):
    bsz, n_lat, _ = p.shape
    n_ctx = pc.shape[1]
    lbh = lb.astype(jnp.float32).reshape(HG_HEADS, HG_HEAD_DIM)

    def heads(t):
        t = t.astype(jnp.float32).reshape(t.shape[0], t.shape[1], 4, HG_HEADS, HG_HEAD_DIM)
        return t[:, :, 0], t[:, :, 1], t[:, :, 2], t[:, :, 3]

    q, f_fw, f_bw, v = heads(p)
    qc, fc_fw, fc_bw, vc = heads(pc)
    outs, outs_c = [], []
    for rev, (fl, flc) in enumerate(((f_fw, fc_fw), (f_bw, fc_bw))):
        f = lbh + (1.0 - lbh) * jax.nn.sigmoid(fl)
        fc = lbh + (1.0 - lbh) * jax.nn.sigmoid(flc)
        s0 = jnp.zeros((bsz, HG_HEADS, HG_HEAD_DIM, HG_HEAD_DIM), jnp.float32)
        oc_dir, s_ctx = hgrn2_scan(flip_seq(qc, rev), flip_seq(1.0 - fc, rev),
                                   flip_seq(vc, rev), flip_seq(jnp.log(fc), rev), s0)
        o_dir, _ = hgrn2_scan(flip_seq(q, rev), flip_seq(1.0 - f, rev),
                              flip_seq(v, rev), flip_seq(jnp.log(f), rev), s_ctx)
        outs.append(flip_seq(o_dir, rev))
        outs_c.append(flip_seq(oc_dir, rev))
    g = norm_g.astype(jnp.float32).reshape(HG_HEADS, HG_HEAD_DIM)

    def head_norm(t, n):
        t = t * lax.rsqrt(jnp.mean(jnp.square(t), -1, keepdims=True) + NORM_EPS) * g
        return t.reshape(bsz, n, E_WIDTH).astype(p.dtype)

    return head_norm(outs[0] + outs[1], n_lat), head_norm(outs_c[0] + outs_c[1], n_ctx)


def setup_inputs(seed: int = 0) -> dict:
    key = jax.random.key(seed)
    ks = jax.random.split(key, 25)
    f32 = jnp.float32
    n_attn = len(range(0, DEPTH, N_MIXERS))
    n_s5 = len(range(1, DEPTH, N_MIXERS))
    n_hg = len(range(2, DEPTH, N_MIXERS))
    beta = (8.0 * DEPTH) ** -0.25

    def nrm(k, shape, std):
        return std * jax.random.normal(k, shape, f32)

    s5_shape = (n_s5, 2, S5_GROUPS, S5_STATE)
    lam_im_init = math.pi * jnp.arange(S5_STATE, dtype=f32)
    return {
        'x': nrm(ks[0], (BATCH, SEQ, D_MODEL), 1.0),
        'c': nrm(ks[1], (BATCH, D_MODEL), 1.0),
        'ctx': nrm(ks[2], (BATCH, CTX_LEN, D_MODEL), 1.0),
        'c_ctx': nrm(ks[3], (D_MODEL,), 1.0),
        'ada_w': nrm(ks[4], (DEPTH, D_MODEL, 3 * D_MODEL), 0.5 * D_MODEL ** -0.5),
        'ada_b': nrm(ks[5], (DEPTH, 3 * D_MODEL), 0.02),
        'ln_g': 1.0 + nrm(ks[6], (DEPTH, D_MODEL), 0.02),
        'ln_b': nrm(ks[7], (DEPTH, D_MODEL), 0.02),
        'w_out': nrm(ks[8], (DEPTH, E_WIDTH, D_MODEL), beta * E_WIDTH ** -0.5),
        'attn_w_in': nrm(ks[9], (n_attn, D_MODEL, ATTN_IN), D_MODEL ** -0.5),
        'attn_sink': nrm(ks[10], (n_attn, N_Q_HEADS), 0.5),
        's5_w_in': nrm(ks[11], (n_s5, D_MODEL, S5_IN), D_MODEL ** -0.5),
        's5_lam_re': -0.5 + nrm(ks[12], s5_shape, 0.01),
        's5_lam_im': lam_im_init + nrm(ks[13], s5_shape, 0.01),
        's5_log_step': jax.random.uniform(ks[14], (n_s5, 2, S5_GROUPS), f32, math.log(1e-3), math.log(1e-1)),
        's5_b_re': nrm(ks[15], (n_s5, 2, S5_GROUPS, S5_STATE, S5_GROUP), (2 * S5_GROUP) ** -0.5),
        's5_b_im': nrm(ks[16], (n_s5, 2, S5_GROUPS, S5_STATE, S5_GROUP), (2 * S5_GROUP) ** -0.5),
        's5_c_re': nrm(ks[17], (n_s5, 2, S5_GROUPS, S5_GROUP, S5_STATE), S5_STATE ** -0.5),
        's5_c_im': nrm(ks[18], (n_s5, 2, S5_GROUPS, S5_GROUP, S5_STATE), S5_STATE ** -0.5),
        's5_d': nrm(ks[19], (n_s5, E_WIDTH), 0.5),
        's5_glu_w': nrm(ks[20], (n_s5, E_WIDTH, E_WIDTH), E_WIDTH ** -0.5),
        's5_glu_b': nrm(ks[21], (n_s5, E_WIDTH), 0.02),
        'hg_w_in': nrm(ks[22], (n_hg, D_MODEL, HG_IN), D_MODEL ** -0.5),
        'hg_lb': nrm(ks[23], (DEPTH, E_WIDTH), 0.1),
        'hg_norm_g': 1.0 + nrm(ks[24], (n_hg, E_WIDTH), 0.02),
    }


def reference(x, c, ctx, c_ctx, ada_w, ada_b, ln_g, ln_b, w_out, attn_w_in, attn_sink,
              s5_w_in, s5_lam_re, s5_lam_im, s5_log_step, s5_b_re, s5_b_im, s5_c_re, s5_c_im,
              s5_d, s5_glu_w, s5_glu_b, hg_w_in, hg_lb, hg_norm_g):
    n_lat = x.shape[1]
    row, col = grid_positions(n_lat)
    alpha = (2.0 * DEPTH) ** 0.25
    lb_w = jax.nn.softmax(hg_lb.astype(jnp.float32), axis=0)
    lb_all = jnp.cumsum(lb_w, axis=0) - lb_w[0:1]
    xc = ctx
    for i in range(DEPTH):
        kind, j = i % N_MIXERS, i // N_MIXERS
        last = i == DEPTH - 1
        mod = jax.nn.silu(c) @ ada_w[i] + ada_b[i]
        mod_c = jax.nn.silu(c_ctx) @ ada_w[i] + ada_b[i]
        shift, scale, gate = jnp.split(mod[:, None, :], 3, axis=-1)
        shift_c, scale_c, gate_c = jnp.split(mod_c, 3, axis=-1)
        h = x * (1.0 + scale) + shift
        hc = xc * (1.0 + scale_c) + shift_c
        w_in = (attn_w_in, s5_w_in, hg_w_in)[kind][j]
        pz = h @ w_in
        pzc = hc @ w_in
        p, z = pz[..., :-E_WIDTH], pz[..., -E_WIDTH:]
        pc, zc = pzc[..., :-E_WIDTH], pzc[..., -E_WIDTH:]
        if kind == 0:
            y, yc = attention_mixer(p, pc, attn_sink[j], row, col, not last)
        elif kind == 1:
            y, yc = s5_mixer(p, pc, s5_lam_re[j], s5_lam_im[j], s5_log_step[j], s5_b_re[j], s5_b_im[j],
                             s5_c_re[j], s5_c_im[j], s5_d[j], s5_glu_w[j], s5_glu_b[j])
        else:
            y, yc = hgrn2_mixer(p, pc, lb_all[i], hg_norm_g[j])
        x = layer_norm(alpha * x + gate * ((y * jax.nn.silu(z)) @ w_out[i]), ln_g[i], ln_b[i])
        if not last:
            xc = layer_norm(alpha * xc + gate_c * ((yc * jax.nn.silu(zc)) @ w_out[i]), ln_g[i], ln_b[i])
    return x
```

```python
import contextlib
import math
import numpy as np
import concourse.bass as bass
import concourse.mybir as mybir
from concourse.bass_utils import run_bass_kernel_spmd

F32 = mybir.dt.float32
BF16 = mybir.dt.bfloat16
I32 = mybir.dt.int32
ALU = mybir.AluOpType
AF = mybir.ActivationFunctionType

D = 1024
E = 2048
LC = 256
LL = 2048
LT = LC + LL
DEPTH = 4
ALPHA = (2.0 * DEPTH) ** 0.25
EPS = 1e-5
NCORES = 8
STOP = [None]
DBG = {}
TWO_PI = 2.0 * math.pi


class Prog:
    NSLOT = 8
    SAME_ENGINE_SYNC = True

    def __init__(self, nc, stack):
        self.nc = nc
        self.eng = {'pe': nc.tensor, 'act': nc.scalar, 'dve': nc.vector,
                    'pool': nc.gpsimd, 'sp': nc.sync}
        self.ops = {e: [] for e in self.eng}
        self.sem = {e: stack.enter_context(nc.semaphore('s_' + e)) for e in self.eng}
        self.cnt = {e: 0 for e in self.eng}
        self.dsem = {}
        self.dcnt = {}
        self.dnext = {}
        for q in ('sp', 'pool'):
            self.dsem[q] = [stack.enter_context(nc.semaphore('d_%s%d' % (q, i)))
                            for i in range(self.NSLOT)]
            self.dcnt[q] = [0] * self.NSLOT
            self.dnext[q] = 0
        self.lastw = {}
        self.readers = {}
        self.waited = {e: {} for e in self.eng}
        self.pending = {e: [] for e in self.eng}
        self.nops = 0

    def _deps(self, e, reads, writes, is_dma=False):
        toks = []
        for r in reads:
            w = self.lastw.get(r)
            if w is not None:
                toks.append(w)
        for r in writes:
            w = self.lastw.get(r)
            if w is not None:
                toks.append(w)
            toks.extend(self.readers.get(r, ()))
        need = {}
        for (te, sem, val, tdma) in toks:
            if te == e and not tdma and not is_dma and (e == 'pe' or not self.SAME_ENGINE_SYNC):
                continue
            k = id(sem)
            if need.get(k, (None, 0))[1] < val:
                need[k] = (sem, val)
        waits = self.pending[e]
        self.pending[e] = []
        wd = self.waited[e]
        for k, (sem, val) in need.items():
            if wd.get(k, 0) >= val:
                continue
            wd[k] = val
            waits.append((sem, val))
        return waits

    def _commit(self, tok, reads, writes):
        for r in writes:
            self.lastw[r] = tok
            self.readers[r] = []
        for r in reads:
            if r in writes:
                continue
            self.readers.setdefault(r, []).append(tok)

    def op(self, e, fn, reads=(), writes=()):
        waits = self._deps(e, reads, writes)
        self.cnt[e] += 1
        tok = (e, self.sem[e], self.cnt[e], False)
        self.ops[e].append((waits, fn, self.sem[e], 1))
        self._commit(tok, reads, writes)
        self.nops += 1

    def dma(self, q, out, in_, reads=(), writes=(), **kw):
        waits = self._deps(q, reads, writes, is_dma=True)
        slot = self.dnext[q]
        self.dnext[q] = (slot + 1) % self.NSLOT
        sem = self.dsem[q][slot]
        prev = self.dcnt[q][slot]
        if prev > 0 and self.waited[q].get(id(sem), 0) < prev:
            self.waited[q][id(sem)] = prev
            waits.append((sem, prev))
        self.dcnt[q][slot] = prev + 16
        tok = (q, sem, prev + 16, True)
        self.ops[q].append((waits, lambda eng: eng.dma_start(out=out, in_=in_, **kw), sem, 16))
        self._commit(tok, reads, writes)
        self.nops += 1

    def _all_tokens(self):
        fin = []
        for e in self.eng:
            if self.cnt[e] > 0:
                fin.append((self.sem[e], self.cnt[e]))
        for q in self.dsem:
            for s, c in zip(self.dsem[q], self.dcnt[q]):
                if c > 0:
                    fin.append((s, c))
        return fin

    def barrier(self):
        fin = self._all_tokens()
        for e in self.eng:
            wd = self.waited[e]
            for (s, v) in fin:
                if s is self.sem[e]:
                    continue
                if wd.get(id(s), 0) >= v:
                    continue
                wd[id(s)] = v
                self.pending[e].append((s, v))
        self.lastw = {}
        self.readers = {}

    def emit(self):
        nc = self.nc
        fin = self._all_tokens()
        ops = self.ops
        pending = self.pending
        with nc.Block() as block:
            def mk(ename):
                def body(eng):
                    for (waits, fn, sem, inc) in ops[ename]:
                        for (s, v) in waits:
                            eng.wait_ge(s, v)
                        fn(eng).then_inc(sem, inc)
                    for (s, v) in pending[ename]:
                        eng.wait_ge(s, v)
                    if ename == 'sp':
                        for (s, v) in fin:
                            eng.wait_ge(s, v)
                return body
            block.sync(mk('sp'))
            block.tensor(mk('pe'))
            block.scalar(mk('act'))
            block.vector(mk('dve'))
            block.gpsimd(mk('pool'))

    def mm(self, out, lhsT, rhs, start, stop, reads=(), writes=()):
        self.op('pe', lambda e: e.matmul(out, lhsT, rhs, start=start, stop=stop), reads, writes)

    def tr(self, out, in_, ident, reads=(), writes=()):
        self.op('pe', lambda e: e.transpose(out, in_, ident), reads, writes)

    def act(self, out, in_, func, reads=(), writes=(), **kw):
        self.op('act', lambda e: e.activation(out=out, in_=in_, func=func, **kw), reads, writes)

    def tt(self, eng, out, in0, in1, op, reads=(), writes=()):
        self.op(eng, lambda e: e.tensor_tensor(out=out, in0=in0, in1=in1, op=op), reads, writes)

    def ts(self, eng, out, in0, s1, s2, op0, op1=None, reads=(), writes=()):
        if op1 is None:
            self.op(eng, lambda e: e.tensor_scalar(out=out, in0=in0, scalar1=s1, scalar2=None, op0=op0),
                    reads, writes)
        else:
            self.op(eng, lambda e: e.tensor_scalar(out=out, in0=in0, scalar1=s1, scalar2=s2, op0=op0, op1=op1),
                    reads, writes)

    def stt(self, eng, out, in0, scalar, in1, op0, op1, reads=(), writes=()):
        self.op(eng, lambda e: e.scalar_tensor_tensor(out=out, in0=in0, scalar=scalar, in1=in1, op0=op0, op1=op1),
                reads, writes)

    def cp(self, eng, out, in_, reads=(), writes=()):
        if eng == 'act':
            self.op('act', lambda e: e.copy(out=out, in_=in_), reads, writes)
        else:
            self.op(eng, lambda e: e.tensor_copy(out=out, in_=in_), reads, writes)

    def memset(self, eng, ap, val, writes=()):
        self.op(eng, lambda e: e.memset(ap, val), (), writes)

    def recip(self, out, in_, reads=(), writes=()):
        self.op('dve', lambda e: e.reciprocal(out=out, in_=in_), reads, writes)


class Rot:
    def __init__(self, tiles, name):
        self.tiles = tiles
        self.name = name
        self.i = 0

    def next(self):
        k = self.i % len(self.tiles)
        self.i += 1
        return self.tiles[k], (self.name, k)


C_IDENT = 0
C_MPREV = 128
C_MNEXT = 256
C_ROPEP = 384
C_POS = 512
C_FIDX = C_POS + 0
C_HGRF = C_FIDX + 1
C_HGRR = C_HGRF + 512
C_HGTF = C_HGRR + 512
C_HGTR = C_HGTF + 64
C_HGM4 = C_HGTR + 64
C_GM = C_HGM4 + 256
C_MM = C_GM + 8
C_IM = C_MM + 128
C_MF = C_IM + 1024
C_MR = C_MF + 128
C_NCOL = C_MR + 128


def make_consts():
    c = np.zeros((128, C_NCOL), np.float32)
    c[:, C_IDENT:C_IDENT + 128] = np.eye(128, dtype=np.float32)
    kk = np.arange(128)[:, None]
    qq = np.arange(128)[None, :]
    c[:, C_MPREV:C_MPREV + 128] = (qq <= kk)
    c[:, C_MNEXT:C_MNEXT + 128] = (kk <= qq)
    pm = np.zeros((128, 128), np.float32)
    for base in (0, 64):
        for f in range(32):
            pm[base + f + 32, base + f] = -1.0
            pm[base + f, base + f + 32] = 1.0
    c[:, C_ROPEP:C_ROPEP + 128] = pm
    t = np.arange(2048)
    c[:, C_FIDX] = np.arange(128) % 32
    tt = np.arange(512)
    c[:, C_HGRF:C_HGRF + 512] = (tt % 64 != 0)[None, :]
    c[:, C_HGRR:C_HGRR + 512] = (tt % 64 != 63)[None, :]
    s = (np.arange(128) % 64)[:, None]
    t64 = np.arange(64)[None, :]
    c[:, C_HGTF:C_HGTF + 64] = (s <= t64)
    c[:, C_HGTR:C_HGTR + 64] = (s >= t64)
    pp_ = np.arange(128)[:, None]
    lo = (pp_ < 64)
    hi = (pp_ >= 64)
    c[:, C_HGM4 + 0:C_HGM4 + 64] = (s <= t64) * lo
    c[:, C_HGM4 + 64:C_HGM4 + 128] = (s <= t64) * hi
    c[:, C_HGM4 + 128:C_HGM4 + 192] = (s >= t64) * lo
    c[:, C_HGM4 + 192:C_HGM4 + 256] = (s >= t64) * hi
    p = np.arange(128)[:, None]
    q = np.arange(128)[None, :]
    c[:, C_GM:C_GM + 8] = (p // 16 == np.arange(8)[None, :])
    c[:, C_MM:C_MM + 128] = (q % 16 == p % 16)
    im = (np.arange(128)[None, :] // 16 == np.arange(8)[:, None]).astype(np.float32)
    c[:, C_IM:C_IM + 1024] = im.reshape(1, 1024)
    c[:, C_MF:C_MF + 128] = (q // 16 >= p // 16)
    c[:, C_MR:C_MR + 128] = (p // 16 >= q // 16)
    return c


def make_pos():
    t = np.arange(2048)
    c = np.zeros((128, 2048), np.float32)
    c[:64] = (t // 64)[None, :]
    c[64:] = (t % 64)[None, :]
    return c


def token_tiles(nb, b, with_ctx=True):
    tl = []
    if with_ctx:
        tl.append((0, LC, nb))
    for i in range(LL // 512):
        tl.append((LC + 512 * i, 512, b))
    return tl


def build(nb, layers=(0, 1, 2, 3), final_out=True, debug=False):
    nc = bass.Bass("TRN2", target_bir_lowering=False)
    NR = nb + 1 + ((nb + 1) % 2)
    _uid = [0]

    def sbt(name, shape, dt):
        _uid[0] += 1
        return nc.sbuf_tensor("%s_u%d" % (name, _uid[0]), shape, dt)

    def din(name, shape, dt=F32):
        return nc.dram_tensor(name, list(shape), dt, kind="ExternalInput").ap()

    def dscr(name, shape, dt):
        return nc.dram_tensor(name, list(shape), dt, kind="Internal").ap()

    x_in = din("x", [nb, LL, D])
    c_in = din("c", [nb, D])
    ctx_in = din("ctx", [nb, LC, D])
    cctx_in = din("c_ctx", [D])
    ada_w = din("ada_w", [DEPTH, D, 3 * D])
    ada_b = din("ada_b", [DEPTH, 3 * D])
    ln_g = din("ln_g", [DEPTH, D])
    ln_b = din("ln_b", [DEPTH, D])
    w_out = din("w_out", [DEPTH, E, D])
    attn_w_in = din("attn_w_in", [2, D, 5120])
    attn_sink = din("attn_sink", [2, 16])
    s5_w_in = din("s5_w_in", [1, D, 4096])
    s5_lam_re = din("s5_lam_re", [1, 2, 128, 64])
    s5_lam_im = din("s5_lam_im", [1, 2, 128, 64])
    s5_log_step = din("s5_log_step", [1, 2, 128])
    s5_b_re = din("s5_b_re", [1, 2, 128, 64, 16])
    s5_b_im = din("s5_b_im", [1, 2, 128, 64, 16])
    s5_c_re = din("s5_c_re", [1, 2, 128, 16, 64])
    s5_c_im = din("s5_c_im", [1, 2, 128, 16, 64])
    s5_d = din("s5_d", [1, E])
    s5_glu_w = din("s5_glu_w", [1, E, E])
    s5_glu_b = din("s5_glu_b", [1, E])
    hg_w_in = din("hg_w_in", [1, D, 10240])
    hg_lb = din("hg_lb", [DEPTH, E])
    hg_norm_g = din("hg_norm_g", [1, E])
    consts = din("consts", [128, C_NCOL])
    posc = din("posc", [128, LL])
    out = nc.dram_tensor("out", [nb, LL, D], F32, kind="ExternalOutput").ap()
    dbg = nc.dram_tensor("dbg", [nb, D, LT], F32, kind="ExternalOutput").ap() if debug else None
    dbgy = nc.dram_tensor("dbgy", [E, LT], BF16, kind="ExternalOutput").ap() if debug else None
    dbgz = nc.dram_tensor("dbgz", [E, LT], BF16, kind="ExternalOutput").ap() if debug else None
    dbgws = nc.dram_tensor("dbgws", [2, 2, 128, 128, 64], BF16, kind="ExternalOutput").ap() if debug else None
    dbgwn = nc.dram_tensor("dbgwn", [2, 2, 128, 64, 128], BF16, kind="ExternalOutput").ap() if debug else None
    dbgwi = nc.dram_tensor("dbgwi", [128, 128, 128], BF16, kind="ExternalOutput").ap() if debug else None
    dbgg = nc.dram_tensor("dbgg", [E, LT], BF16, kind="ExternalOutput").ap() if debug else None

    xT_s = dscr("xT_s", [nb, D, LT], F32)
    wb_attn = dscr("wb_attn", [2, D, 5120], BF16)
    wb_s5 = dscr("wb_s5", [D, 4096], BF16)
    wb_glu = dscr("wb_glu", [E, E], BF16)
    wb_hg = dscr("wb_hg", [D, 10240], BF16)
    wb_out = dscr("wb_out", [DEPTH, E, D], BF16)
    qT_s = dscr("qT_s", [E, LT], BF16)
    kT_s = dscr("kT_s", [512, LT], BF16)
    v_s = dscr("v_s", [LT, 512], BF16)
    zT_s = dscr("zT_s", [E, LT], BF16)
    yT_s = dscr("yT_s", [E, LT], BF16)
    uT_s = dscr("uT_s", [E, LT], BF16)
    rope_s = dscr("rope_s", [2, 128, LL], F32)
    sel_s = dscr("sel_s", [128, 8192], BF16)
    gT_s = dscr("gT_s", [E, LT], BF16)
    WS_s = dscr("WS_s", [2, 2, 128, 128, 64], BF16)
    Winter_s = dscr("Winter_s", [2, 2, 128, 64, 128], BF16)
    Wintra_s = dscr("Wintra_s", [128, 128, 128], BF16)
    a8_s = dscr("a8_s", [2, 2, 64, 128], F32)

    with contextlib.ExitStack() as gs:
        P = Prog(nc, gs)

        def galloc(name, shape, dt):
            return gs.enter_context(sbt(name, list(shape), dt))

        def finish():
            if debug:
                for b in range(nb):
                    P.dma('sp', dbg[b], xT_s[b], writes=[('dbg', b)])
                P.dma('sp', dbgy, yT_s, writes=['dbgy'])
                P.dma('sp', dbgz, zT_s, writes=['dbgz'])
                if 1 in [l_ % 3 for l_ in layers]:
                    for d_ in range(2):
                        for c_ in range(2):
                            P.dma('sp', dbgws[d_, c_], WS_s[d_, c_], writes=[('dbgws', d_, c_)])
                            P.dma('sp', dbgwn[d_, c_], Winter_s[d_, c_], writes=[('dbgwn', d_, c_)])
                    P.dma('sp', dbgwi, Wintra_s, writes=['dbgwi'])
                    P.dma('sp', dbgg, gT_s, writes=['dbgg'])
            P.emit()
            return nc

        cst = galloc("cst", [128, C_NCOL], F32)
        ident32 = cst[:, C_IDENT:C_IDENT + 128]
        ones32 = galloc("ones32", [128, 128], F32)
        onesbf = galloc("onesbf", [128, 128], BF16)
        identbf = galloc("identbf", [128, 128], BF16)
        ropeP = galloc("ropeP", [128, 128], BF16)
        mprev = galloc("mprev", [128, 128], BF16)
        mnext = galloc("mnext", [128, 128], BF16)
        modT = galloc("modT", [128, DEPTH, 24, NR], F32)
        lng = galloc("lng", [128, DEPTH, 8], F32)
        lnb = galloc("lnb", [128, DEPTH, 8], F32)
        esink = galloc("esink", [128, 2, 16], F32)

        psum = [gs.enter_context(nc.psum_tensor("ps%d" % i, [128, 512], F32)) for i in range(8)]

        P.dma('sp', cst[:], consts, writes=['cst'])
        P.memset('dve', ones32[:], 1.0, writes=['ones32'])
        P.memset('dve', onesbf[:], 1.0, writes=['onesbf'])
        P.cp('dve', identbf[:], ident32, reads=['cst'], writes=['identbf'])
        P.cp('dve', ropeP[:], cst[:, C_ROPEP:C_ROPEP + 128], reads=['cst'], writes=['ropeP'])
        P.cp('dve', mprev[:], cst[:, C_MPREV:C_MPREV + 128], reads=['cst'], writes=['mprev'])
        P.cp('dve', mnext[:], cst[:, C_MNEXT:C_MNEXT + 128], reads=['cst'], writes=['mnext'])
        P.dma('sp', lng[:], ln_g.rearrange("l (c p) -> p l c", p=128), writes=['lng'],
              allow_slow_non_contiguous=True)
        P.dma('sp', lnb[:], ln_b.rearrange("l (c p) -> p l c", p=128), writes=['lnb'],
              allow_slow_non_contiguous=True)
        P.dma('sp', esink[:].rearrange("p a h -> p (a h)"),
              attn_sink.rearrange("a h -> (a h)").partition_broadcast(128), writes=['esink'])
        P.act(esink[:], esink[:], AF.Exp, reads=['esink'], writes=['esink'])

        def conv_weights(pairs):
            with contextlib.ExitStack() as ph:
                n = 3
                srcs = Rot([ph.enter_context(sbt("cw_s%d" % i, [128, 2048], F32)) for i in range(n)], 'cw_s')
                dsts = Rot([ph.enter_context(sbt("cw_d%d" % i, [128, 2048], BF16)) for i in range(n)], 'cw_d')
                engs = ['dve', 'pool', 'act']
                k = 0
                for (src, dst, dkey) in pairs:
                    R, Cc = src.shape
                    for r0 in range(0, R, 128):
                        for c0 in range(0, Cc, 2048):
                            cw = min(2048, Cc - c0)
                            st, sk = srcs.next()
                            dt_, dk = dsts.next()
                            P.dma('sp', st[:, :cw], src[r0:r0 + 128, c0:c0 + cw], writes=[sk])
                            P.cp(engs[k % 3], dt_[:, :cw], st[:, :cw], reads=[sk], writes=[dk])
                            P.dma('pool', dst[r0:r0 + 128, c0:c0 + cw], dt_[:, :cw], reads=[dk], writes=[('wbw', k)])
                            k += 1
            P.barrier()

        pairs = []
        used_kinds = set(l % 3 for l in layers)
        for l in layers:
            pairs.append((w_out[l], wb_out[l], 'wb'))
            if l % 3 == 0:
                pairs.append((attn_w_in[l // 3], wb_attn[l // 3], 'wb'))
        if 1 in used_kinds:
            pairs.append((s5_w_in[0], wb_s5, 'wb'))
            pairs.append((s5_glu_w[0], wb_glu, 'wb'))
        if 2 in used_kinds:
            pairs.append((hg_w_in[0], wb_hg, 'wb'))
        if STOP[0] == 'init':
            return finish()
        if DBG.get('dmacast', False):
            kcw = 0
            for (src, dst, dkey) in pairs:
                R, Cc = src.shape
                for r0 in range(0, R, 128):
                    P.dma('pool', dst[r0:r0 + 128, :], src[r0:r0 + 128, :], writes=[('wbw', kcw)])
                    kcw += 1
        else:
            conv_weights(pairs)
        if STOP[0] == 'conv':
            return finish()

        def to_feature_major():
            with contextlib.ExitStack() as ph:
                xin = Rot([ph.enter_context(sbt("tf_i%d" % i, [128, D], F32)) for i in range(3)], 'tf_i')
                xst = Rot([ph.enter_context(sbt("tf_o%d" % i, [128, 8, 512], F32)) for i in range(2)], 'tf_o')
                pk = 0
                for b in range(nb):
                    for (t0, nt, _r) in token_tiles(nb, b):
                        so, sok = xst.next()
                        for tb in range(nt // 128):
                            xi, xik = xin.next()
                            if t0 < LC:
                                src = ctx_in[b, tb * 128:(tb + 1) * 128, :]
                            else:
                                src = x_in[b, t0 - LC + tb * 128:t0 - LC + (tb + 1) * 128, :]
                            P.dma('sp', xi[:], src, writes=[xik])
                            for half in range(2):
                                pt = psum[pk % 8]
                                pkey = ('ps', pk % 8)
                                pk += 1
                                for cc in range(4):
                                    c = half * 4 + cc
                                    P.tr(pt[:, cc * 128:(cc + 1) * 128], xi[:, c * 128:(c + 1) * 128], ident32,
                                         reads=[xik, 'cst'], writes=[pkey])
                                P.cp('act' if half else 'dve',
                                     so[:, half * 4:(half + 1) * 4, tb * 128:(tb + 1) * 128],
                                     pt[:].rearrange("p (c t) -> p c t", c=4), reads=[pkey], writes=[sok])
                        P.dma('pool', xT_s[b].rearrange("(c p) t -> p c t", p=128)[:, :, t0:t0 + nt],
                              so[:, :, :nt], reads=[sok], writes=[('xT', b, t0)])
            P.barrier()

        to_feature_major()
        if STOP[0] == 'tfm':
            return finish()

        def compute_mod():
            with contextlib.ExitStack() as ph:
                cT = ph.enter_context(sbt("cT", [128, 8, NR], F32))
                P.memset('dve', cT[:], 0.0, writes=['cT'])
                adab = ph.enter_context(sbt("adab", [128, DEPTH, 24], F32))
                wts = Rot([ph.enter_context(sbt("adaw%d" % i, [128, 8, 1024], F32)) for i in range(2)], 'adaw')
                for r in range(nb):
                    P.dma('sp', cT[:, :, r:r + 1], c_in[r].rearrange("(c p o) -> p c o", p=128, o=1),
                          writes=['cT'], allow_slow_non_contiguous=True)
                P.dma('sp', cT[:, :, nb:nb + 1], cctx_in.rearrange("(c p o) -> p c o", p=128, o=1),
                      writes=['cT'], allow_slow_non_contiguous=True)
                P.dma('sp', adab[:], ada_b.rearrange("l (o p) -> p l o", p=128), writes=['adab'],
                      allow_slow_non_contiguous=True)
                P.act(cT[:], cT[:], AF.Silu, reads=['cT'], writes=['cT'])
                for l in layers:
                    for part in range(3):
                        wt, wk = wts.next()
                        P.dma('sp', wt[:], ada_w[l].rearrange("(c p) n -> p c n", p=128)[:, :, part * 1024:(part + 1) * 1024],
                              writes=[wk])
                        pt = psum[part]
                        pkey = ('ps', part)
                        for oc in range(8):
                            for kc in range(8):
                                P.mm(pt[:, oc * NR:(oc + 1) * NR], wt[:, kc, oc * 128:(oc + 1) * 128],
                                     cT[:, kc, :], kc == 0, kc == 7, reads=[wk, 'cT'], writes=[pkey])
                        P.tt('dve', modT[:, l, part * 8:(part + 1) * 8, :],
                             pt[:, :8 * NR].rearrange("p (o r) -> p o r", o=8),
                             adab[:, l, part * 8:(part + 1) * 8].unsqueeze(2).to_broadcast([128, 8, NR]),
                             ALU.add, reads=[pkey, 'adab'], writes=['modT'])
                    P.ts('dve', modT[:, l, 8:16, :], modT[:, l, 8:16, :], 1.0, None, ALU.add,
                         reads=['modT'], writes=['modT'])
                    P.ts('dve', modT[:, l, 16:24, :], modT[:, l, 16:24, :], 1.0 / ALPHA, None, ALU.mult,
                         reads=['modT'], writes=['modT'])
            P.barrier()

        compute_mod()
        if STOP[0] == 'mod':
            return finish()


        def rope_tables():
            with contextlib.ExitStack() as ph:
                ropecos = ph.enter_context(sbt("ropecos", [128, LL], F32))
                ropesin = ph.enter_context(sbt("ropesin", [128, LL], F32))
                invf = ph.enter_context(sbt("invf", [128, 1], F32))
                ang = ph.enter_context(sbt("ang", [128, LL], F32))
                a2 = ph.enter_context(sbt("a2", [128, LL], F32))
                ki = ph.enter_context(sbt("ki", [128, LL], I32))
                kf = ph.enter_context(sbt("kf", [128, LL], F32))
                P.act(invf[:], cst[:, C_FIDX:C_FIDX + 1], AF.Exp, reads=['cst'], writes=['invf'],
                      scale=-math.log(10000.0) / 32.0)
                P.dma('sp', a2[:], posc, writes=['a2'])
                P.ts('dve', ang[:], a2[:], invf[:, 0:1], None, ALU.mult,
                     reads=['a2', 'invf'], writes=['ang'])
                for (dst, shift, dkey) in ((ropesin, 0.0, 'ropesin'), (ropecos, math.pi / 2, 'ropecos')):
                    P.ts('dve', a2[:], ang[:], shift, 1.0 / TWO_PI, ALU.add, ALU.mult, reads=['ang'], writes=['a2'])
                    P.cp('dve', ki[:], a2[:], reads=['a2'], writes=['ki'])
                    P.cp('dve', kf[:], ki[:], reads=['ki'], writes=['kf'])
                    P.stt('dve', a2[:], kf[:], -TWO_PI, ang[:], ALU.mult, ALU.add, reads=['kf', 'ang'], writes=['a2'])
                    P.ts('dve', a2[:], a2[:], shift, 3.14159, ALU.add, ALU.min, reads=['a2'], writes=['a2'])
                    P.ts('dve', a2[:], a2[:], -3.14159, None, ALU.max, reads=['a2'], writes=['a2'])
                    P.act(dst[:], a2[:], AF.Sin, reads=['a2'], writes=[dkey])
                    P.dma('pool', rope_s[0 if dkey == 'ropesin' else 1], dst[:], reads=[dkey], writes=[('rope_s', dkey)])
            P.barrier()

        if 0 in used_kinds:
            rope_tables()
        if STOP[0] == 'rope':
            return finish()

        def load_hT(ph, l, b, tiles):
            hT = ph.enter_context(sbt("hT", [128, 8, LT], BF16))
            with contextlib.ExitStack() as ph2:
                xin = Rot([ph2.enter_context(sbt("hx%d" % i, [128, 8, 512], F32)) for i in range(2)], 'hx')
                for ti, (t0, nt, r) in enumerate(tiles):
                    xt, xk = xin.next()
                    P.dma('sp', xt[:, :, :nt], xT_s[b].rearrange("(c p) t -> p c t", p=128)[:, :, t0:t0 + nt],
                          reads=[('xT', b, t0)], writes=[xk])
                    for c in range(8):
                        if c % 2 == 0:
                            P.act(hT[:, c, t0:t0 + nt], xt[:, c, :nt], AF.Identity, reads=[xk, 'modT'],
                                  writes=[('hT', t0, c)], scale=modT[:, l, 8 + c, r:r + 1], bias=modT[:, l, c, r:r + 1])
                        else:
                            P.ts('dve', hT[:, c, t0:t0 + nt], xt[:, c, :nt], modT[:, l, 8 + c, r:r + 1],
                                 modT[:, l, c, r:r + 1], ALU.mult, ALU.add, reads=[xk, 'modT'], writes=[('hT', t0, c)])
                P.barrier()
            return hT

        pcount = [0]

        def next_ps():
            k = pcount[0] % 8
            pcount[0] += 1
            return psum[k], ('ps', k)

        def phase_C(l, b, last):
            tiles = token_tiles(nb, b, with_ctx=not last)
            with contextlib.ExitStack() as ph:
                wo = ph.enter_context(sbt("wo", [128, 16, D], BF16))
                P.dma('sp', wo[:], wb_out[l].rearrange("(k p) n -> p k n", p=128), writes=['wo'])
                yts = Rot([ph.enter_context(sbt("c_y%d" % i, [128, 16, 512], BF16)) for i in range(2)], 'c_y')
                zts = Rot([ph.enter_context(sbt("c_z%d" % i, [128, 16, 512], BF16)) for i in range(1 if last else 2)], 'c_z')
                xts = Rot([ph.enter_context(sbt("c_x%d" % i, [128, 8, 512], F32)) for i in range(2)], 'c_x')
                tps = Rot([ph.enter_context(sbt("c_t%d" % i, [128, 8, 512], F32)) for i in range(2)], 'c_t')
                sqs = Rot([ph.enter_context(sbt("c_q%d" % i, [128, 8, 512], F32)) for i in range(1)], 'c_q')
                sts = Rot([ph.enter_context(sbt("c_s%d" % i, [128, 3, 512], F32)) for i in range(1)], 'c_s')
                ots = Rot([ph.enter_context(sbt("c_o%d" % i, [128, D], F32)) for i in range(2)], 'c_o') if last else None

                def front(tile):
                    t0, nt, r = tile
                    yt, yk = yts.next()
                    zt, zk = zts.next()
                    xt, xk = xts.next()
                    tp, tk = tps.next()
                    P.dma('sp', yt[:, :, :nt], yT_s.rearrange("(k p) t -> p k t", p=128)[:, :, t0:t0 + nt],
                          writes=[yk, (yk, 0), (yk, 1)])
                    P.dma('sp', zt[:, :, :nt], zT_s.rearrange("(k p) t -> p k t", p=128)[:, :, t0:t0 + nt],
                          writes=[zk])
                    P.dma('sp', xt[:, :, :nt], xT_s[b].rearrange("(c p) t -> p c t", p=128)[:, :, t0:t0 + nt],
                          reads=[('xT', b, t0)], writes=[xk])
                    P.tt('dve', yt[:, 0:10, :nt], yt[:, 0:10, :nt], zt[:, 0:10, :nt], ALU.mult, reads=[yk, zk], writes=[(yk, 0)])
                    P.tt('pool', yt[:, 10:16, :nt], yt[:, 10:16, :nt], zt[:, 10:16, :nt], ALU.mult, reads=[yk, zk], writes=[(yk, 1)])
                    for oc in range(8):
                        pt, pkey = next_ps()
                        for kc in range(16):
                            P.mm(pt[:, :nt], wo[:, kc, oc * 128:(oc + 1) * 128], yt[:, kc, :nt], kc == 0, kc == 15,
                                 reads=['wo', (yk, 0 if kc < 10 else 1)], writes=[pkey])
                        P.stt('dve', tp[:, oc, :nt], pt[:, :nt], modT[:, l, 16 + oc, r:r + 1], xt[:, oc, :nt],
                              ALU.mult, ALU.add, reads=[pkey, xk, 'modT'], writes=[(tk, oc)])
                    return (tile, xt, xk, tp, tk, yk)

                def tail(fr):
                    (t0, nt, r), xt, xk, tp, tk, yk = fr
                    sq, qk = sqs.next()
                    st, sk = sts.next()
                    tks = [(tk, oc) for oc in range(8)]
                    P.act(sq[:, :, :nt], tp[:, :, :nt], AF.Square, reads=tks, writes=[qk])
                    pm, pmk = next_ps()
                    for oc in range(8):
                        P.mm(pm[:, :nt], ones32[:], tp[:, oc, :nt], oc == 0, oc == 7, reads=[(tk, oc), 'ones32'], writes=[pmk])
                    pq, pqk = next_ps()
                    for oc in range(8):
                        P.mm(pq[:, :nt], ones32[:], sq[:, oc, :nt], oc == 0, oc == 7, reads=[qk, 'ones32'], writes=[pqk])
                    mean = st[:, 0, :nt]
                    msq = st[:, 1, :nt]
                    rstd = st[:, 2, :nt]
                    P.act(mean, pm[:, :nt], AF.Copy, reads=[pmk], writes=[sk], scale=1.0 / D)
                    P.tt('dve', msq, mean, mean, ALU.mult, reads=[sk], writes=[sk])
                    P.stt('dve', msq, pq[:, :nt], 1.0 / D, msq, ALU.mult, ALU.subtract, reads=[pqk, sk], writes=[sk])
                    P.act(msq, msq, AF.Ln, reads=[sk], writes=[sk], bias=EPS / (ALPHA * ALPHA))
                    P.act(rstd, msq, AF.Exp, reads=[sk], writes=[sk], scale=-0.5)
                    for (eng, c0, c1) in (('dve', 0, 5), ('pool', 5, 8)):
                        kk = [(tk, oc) for oc in range(c0, c1)]
                        P.tt(eng, tp[:, c0:c1, :nt], tp[:, c0:c1, :nt], mean.unsqueeze(1).to_broadcast([128, c1 - c0, nt]),
                             ALU.subtract, reads=kk + [sk], writes=kk)
                        P.tt(eng, tp[:, c0:c1, :nt], tp[:, c0:c1, :nt], rstd.unsqueeze(1).to_broadcast([128, c1 - c0, nt]),
                             ALU.mult, reads=kk + [sk], writes=kk)
                    for oc in range(8):
                        P.act(xt[:, oc, :nt], tp[:, oc, :nt], AF.Identity, reads=[(tk, oc), 'lng', 'lnb'], writes=[xk],
                              scale=lng[:, l, oc:oc + 1], bias=lnb[:, l, oc:oc + 1])
                    if not (last and final_out):
                        P.dma('pool', xT_s[b].rearrange("(c p) t -> p c t", p=128)[:, :, t0:t0 + nt], xt[:, :, :nt],
                              reads=[xk], writes=[('xT', b, t0)])
                    if last and final_out:
                        for tb in range(nt // 128):
                            ot, ok = ots.next()
                            for half in range(2):
                                pt, pkey = next_ps()
                                for cc in range(4):
                                    c = half * 4 + cc
                                    P.tr(pt[:, cc * 128:(cc + 1) * 128], xt[:, c, tb * 128:(tb + 1) * 128], ident32,
                                         reads=[xk, 'cst'], writes=[pkey])
                                P.cp('act' if half else 'dve', ot[:, half * 512:(half + 1) * 512], pt[:],
                                     reads=[pkey], writes=[ok])
                            tok0 = t0 - LC + tb * 128
                            P.dma('pool', out[b, tok0:tok0 + 128, :], ot[:], reads=[ok], writes=[('out', b, tok0)])

                fr_prev = front(tiles[0])
                for ti in range(len(tiles)):
                    fr_next = front(tiles[ti + 1]) if ti + 1 < len(tiles) else None
                    tail(fr_prev)
                    fr_prev = fr_next
            P.barrier()

        def attn_phase_A(l, b):
            j = l // 3
            tiles = token_tiles(nb, b)
            W = wb_attn[j].rearrange("(c p) n -> p c n", p=128)
            with contextlib.ExitStack() as ph:
                hT = load_hT(ph, l, b, tiles)
                ropecos = ph.enter_context(sbt("ropecos", [128, LL], F32))
                ropesin = ph.enter_context(sbt("ropesin", [128, LL], F32))
                P.dma('sp', ropesin[:], rope_s[0], writes=['ropesin'])
                P.dma('sp', ropecos[:], rope_s[1], writes=['ropecos'])
                wts = Rot([ph.enter_context(sbt("a_w%d" % i, [128, 8, 512], BF16)) for i in range(2)], 'a_w')
                stg = Rot([ph.enter_context(sbt("a_s%d" % i, [128, 4, 512], BF16)) for i in range(3)], 'a_s')
                qbs = Rot([ph.enter_context(sbt("a_qb%d" % i, [128, 512], BF16)) for i in range(3)], 'a_qb')
                t1s = Rot([ph.enter_context(sbt("a_t1%d" % i, [128, 512], F32)) for i in range(2)], 'a_t1')
                t2s = Rot([ph.enter_context(sbt("a_t2%d" % i, [128, 512], F32)) for i in range(2)], 'a_t2')
                q32s = Rot([ph.enter_context(sbt("a_q32%d" % i, [128, 512], F32)) for i in range(3)], 'a_q32')
                hkeys = [('hT', t0) for (t0, _n, _r) in tiles]
                for cg in DBG.get('cgs', range(10)):
                    wt, wk = wts.next()
                    P.dma('sp', wt[:], W[:, :, cg * 512:(cg + 1) * 512], writes=[wk])
                    if cg == 5:
                        for (t0, nt, r) in tiles:
                            for tb in range(nt // 128):
                                tok0 = t0 + tb * 128
                                pt, pkey = next_ps()
                                for c in range(8):
                                    P.mm(pt[:], hT[:, c, tok0:tok0 + 128], wt[:, c, :], c == 0, c == 7,
                                         reads=[wk, ('hT', t0, c)], writes=[pkey])
                                sg, sgk = stg.next()
                                P.cp('act' if tb % 2 else 'dve', sg[:, 0, :], pt[:], reads=[pkey], writes=[sgk])
                                P.dma('pool', v_s[tok0:tok0 + 128, :], sg[:, 0, :], reads=[sgk], writes=[('v_s', tok0)])
                        continue
                    if cg < 4:
                        dst = qT_s[cg * 512:(cg + 1) * 512, :]
                        dk = 'qT_s'
                    elif cg == 4:
                        dst = kT_s
                        dk = 'kT_s'
                    else:
                        dst = zT_s[(cg - 6) * 512:(cg - 5) * 512, :]
                        dk = 'zT_s'
                    items = []
                    for (t0, nt, r) in tiles:
                        sg, sgk = stg.next()
                        for hh in range(4):
                            items.append((t0, nt, hh, sg, sgk))

                    def stage1(it):
                        t0, nt, hh, sg, sgk = it
                        pt, pkey = next_ps()
                        for c in range(8):
                            P.mm(pt[:, :nt], wt[:, c, hh * 128:(hh + 1) * 128], hT[:, c, t0:t0 + nt], c == 0, c == 7,
                                 reads=[wk, ('hT', t0, c)], writes=[pkey])
                        if cg >= 6:
                            P.act(sg[:, hh, :nt], pt[:, :nt], AF.Silu, reads=[pkey], writes=[sgk])
                            return None
                        if t0 < LC:
                            P.cp('act', sg[:, hh, :nt], pt[:, :nt], reads=[pkey], writes=[sgk])
                            return None
                        qb, qbk = qbs.next()
                        q32, q32k = q32s.next()
                        P.cp('act', q32[:, :nt], pt[:, :nt], reads=[pkey], writes=[q32k])
                        P.cp('act', qb[:, :nt], pt[:, :nt], reads=[pkey], writes=[qbk])
                        return (qb, qbk, q32, q32k)

                    def stage2(it, st):
                        t0, nt, hh, sg, sgk = it
                        if st is not None:
                            qb, qbk, q32, q32k = st
                            lt0 = t0 - LC
                            t1, t1k = t1s.next()
                            t2, t2k = t2s.next()
                            p2, p2k = next_ps()
                            P.mm(p2[:, :nt], ropeP[:], qb[:, :nt], True, True, reads=['ropeP', qbk], writes=[p2k])
                            P.tt('pool', t1[:, :nt], q32[:, :nt], ropecos[:, lt0:lt0 + nt], ALU.mult,
                                 reads=[q32k, 'ropecos'], writes=[t1k])
                            P.tt('dve', t2[:, :nt], p2[:, :nt], ropesin[:, lt0:lt0 + nt], ALU.mult,
                                 reads=[p2k, 'ropesin'], writes=[t2k])
                            P.tt('dve', sg[:, hh, :nt], t1[:, :nt], t2[:, :nt], ALU.add,
                                 reads=[t1k, t2k], writes=[sgk])
                        if hh == 3:
                            P.dma('pool', dst.rearrange("(h p) t -> p h t", p=128)[:, :, t0:t0 + nt], sg[:, :, :nt],
                                  reads=[sgk], writes=[(dk, cg, t0)])

                    st_prev = stage1(items[0])
                    for i_ in range(len(items)):
                        st_next = stage1(items[i_ + 1]) if i_ + 1 < len(items) else None
                        stage2(items[i_], st_prev)
                        st_prev = st_next
            P.barrier()

        def attn_phase_B(l, b, last):
            j = l // 3
            scale = 128 ** -0.5
            with contextlib.ExitStack() as ph:
                kT = ph.enter_context(sbt("b_k", [128, 4, LT], BF16))
                V = ph.enter_context(sbt("b_v", [128, LT // 128, 512], BF16))
                qgs = Rot([ph.enter_context(sbt("b_q%d" % i, [128, 4, LT], BF16)) for i in range(2)], 'b_q')
                ysts = Rot([ph.enter_context(sbt("b_y%d" % i, [128, 4, LT], BF16)) for i in range(2)], 'b_y')
                pts = Rot([ph.enter_context(sbt("b_p%d" % i, [128, 4, 128], BF16)) for i in range(10)], 'b_p')
                dns = Rot([ph.enter_context(sbt("b_d%d" % i, [128, 4, 128], F32)) for i in range(2)], 'b_d')
                P.dma('sp', kT[:], kT_s.rearrange("(h p) t -> p h t", p=128), writes=['b_k'])
                P.dma('sp', V[:], v_s.rearrange("(n p) c -> p n c", p=128), writes=['b_v'])
                qblocks = list(range(2, 18)) if last else list(range(18))
                for h in range(4):
                    qg, qk = qgs.next()
                    ys, yk = ysts.next()
                    P.dma('sp', qg[:], qT_s[h * 512:(h + 1) * 512, :].rearrange("(g p) t -> p g t", p=128),
                          writes=[qk])
                    if last:
                        P.memset('pool', ys[:, :, 0:LC], 0.0, writes=[yk])
                    steps = []
                    for qb in qblocks:
                        kbs = [(0, None), (1, None)]
                        if qb >= 2:
                            if qb - 1 >= 2:
                                kbs.append((qb - 1, mprev))
                            kbs.append((qb, None))
                            if qb + 1 <= 17:
                                kbs.append((qb + 1, mnext))
                        for i_, (kb, msk) in enumerate(kbs):
                            steps.append(dict(qb=qb, kb=kb, msk=msk, first=(i_ == 0), last=(i_ == len(kbs) - 1)))
                    acc = {}

                    def emit_S(st):
                        qb, kb, msk = st['qb'], st['kb'], st['msk']
                        ps_, psk = next_ps()
                        P.mm(ps_[:].rearrange("p (g t) -> p g t", g=4), kT[:, h, kb * 128:(kb + 1) * 128],
                             qg[:, :, qb * 128:(qb + 1) * 128], True, True, reads=['b_k', qk], writes=[psk])
                        pt_, ptk = pts.next()
                        P.act(pt_[:], ps_[:].rearrange("p (g t) -> p g t", g=4), AF.Exp, reads=[psk], writes=[ptk],
                              scale=scale)
                        if msk is not None:
                            P.tt('pool', pt_[:], pt_[:], msk[:].unsqueeze(1).to_broadcast([128, 4, 128]), ALU.mult,
                                 reads=[ptk, 'mprev', 'mnext'], writes=[ptk])
                        st['pt'] = (pt_, ptk)

                    def emit_PV(st):
                        qb, kb = st['qb'], st['kb']
                        pt_, ptk = st['pt']
                        if st['first']:
                            acc['po'] = next_ps()
                            acc['pd'] = next_ps()
                        po, pok = acc['po']
                        pd, pdk = acc['pd']
                        P.mm(po[:], V[:, kb, h * 128:(h + 1) * 128], pt_[:].rearrange("p g t -> p (g t)"),
                             st['first'], st['last'], reads=['b_v', ptk], writes=[pok])
                        P.mm(pd[:], onesbf[:], pt_[:].rearrange("p g t -> p (g t)"),
                             st['first'], st['last'], reads=['onesbf', ptk], writes=[pdk])
                        if st['last']:
                            dn, dnk = dns.next()
                            P.tt('dve', dn[:], pd[:].rearrange("p (g t) -> p g t", g=4),
                                 esink[:, j, h * 4:(h + 1) * 4].unsqueeze(2).to_broadcast([128, 4, 128]), ALU.add,
                                 reads=[pdk, 'esink'], writes=[dnk])
                            P.act(dn[:], dn[:], AF.Ln, reads=[dnk], writes=[dnk])
                            P.act(dn[:], dn[:], AF.Exp, reads=[dnk], writes=[dnk], scale=-1.0)
                            P.tt('dve', ys[:, :, qb * 128:(qb + 1) * 128], po[:].rearrange("p (g t) -> p g t", g=4), dn[:],
                                 ALU.mult, reads=[pok, dnk], writes=[yk])

                    LOOK = 2
                    for i_ in range(len(steps) + LOOK):
                        if i_ < len(steps):
                            emit_S(steps[i_])
                        if i_ - LOOK >= 0:
                            emit_PV(steps[i_ - LOOK])
                    P.dma('pool', yT_s[h * 512:(h + 1) * 512, :].rearrange("(g p) t -> p g t", p=128), ys[:],
                          reads=[yk], writes=[('yT_s', h)])
            P.barrier()


        hglb = galloc("hglb", [128, 16], F32)
        hgoml = galloc("hgoml", [128, 16], F32)
        hgng = galloc("hgng", [128, 16], F32)

        def hg_prepare(l):
            with contextlib.ExitStack() as ph:
                raw = ph.enter_context(sbt("hg_raw", [128, DEPTH, 16], F32))
                mx = ph.enter_context(sbt("hg_mx", [128, 16], F32))
                tot = ph.enter_context(sbt("hg_tot", [128, 16], F32))
                P.dma('sp', raw[:], hg_lb.rearrange("l (c p) -> p l c", p=128), writes=['hg_raw'],
                      allow_slow_non_contiguous=True)
                P.dma('sp', hgng[:], hg_norm_g[0].rearrange("(c p) -> p c", p=128), writes=['hgng'],
                      allow_slow_non_contiguous=True)
                P.tt('dve', mx[:], raw[:, 0, :], raw[:, 1, :], ALU.max, reads=['hg_raw'], writes=['hg_mx'])
                for i in range(2, DEPTH):
                    P.tt('dve', mx[:], mx[:], raw[:, i, :], ALU.max, reads=['hg_raw', 'hg_mx'], writes=['hg_mx'])
                P.tt('dve', raw[:], raw[:], mx[:].unsqueeze(1).to_broadcast([128, DEPTH, 16]), ALU.subtract,
                     reads=['hg_raw', 'hg_mx'], writes=['hg_raw'])
                P.act(raw[:], raw[:], AF.Exp, reads=['hg_raw'], writes=['hg_raw'])
                P.tt('dve', tot[:], raw[:, 0, :], raw[:, 1, :], ALU.add, reads=['hg_raw'], writes=['hg_tot'])
                for i in range(2, DEPTH):
                    P.tt('dve', tot[:], tot[:], raw[:, i, :], ALU.add, reads=['hg_raw', 'hg_tot'], writes=['hg_tot'])
                P.recip(tot[:], tot[:], reads=['hg_tot'], writes=['hg_tot'])
                P.memset('dve', hglb[:], 0.0, writes=['hglb'])
                for i in range(1, l + 1):
                    P.tt('dve', hglb[:], hglb[:], raw[:, i, :], ALU.add, reads=['hg_raw', 'hglb'], writes=['hglb'])
                P.tt('dve', hglb[:], hglb[:], tot[:], ALU.mult, reads=['hglb', 'hg_tot'], writes=['hglb'])
                P.ts('dve', hgoml[:], hglb[:], -1.0, 1.0, ALU.mult, ALU.add, reads=['hglb'], writes=['hgoml'])
            P.barrier()


        def hg_layer(l, b):
            tiles = token_tiles(nb, b)
            W = wb_hg.rearrange("(c p) n -> p c n", p=128)
            NBLK = LT // 128
            NCH = LT // 64
            with contextlib.ExitStack() as ph:
                hT = load_hT(ph, l, b, tiles)
                with contextlib.ExitStack() as ph2:
                    wzs = Rot([ph2.enter_context(sbt("g_wz%d" % i, [128, 8, 512], BF16)) for i in range(2)], 'g_wz')
                    zsg = Rot([ph2.enter_context(sbt("g_zs%d" % i, [128, 4, 512], BF16)) for i in range(3)], 'g_zs')
                    for cg in range(4):
                        wz, wzk = wzs.next()
                        P.dma('sp', wz[:], W[:, :, 8192 + cg * 512:8192 + (cg + 1) * 512], writes=[wzk])
                        for (t0, nt, r) in tiles:
                            sg, sgk = zsg.next()
                            for hh in range(4):
                                pt, pkey = next_ps()
                                for c in range(8):
                                    P.mm(pt[:, :nt], wz[:, c, hh * 128:(hh + 1) * 128], hT[:, c, t0:t0 + nt], c == 0, c == 7,
                                         reads=[wzk, ('hT', t0, c)], writes=[pkey])
                                P.act(sg[:, hh, :nt], pt[:, :nt], AF.Silu, reads=[pkey], writes=[sgk])
                            P.dma('pool', zT_s[cg * 512:(cg + 1) * 512, :].rearrange("(h p) t -> p h t", p=128)[:, :, t0:t0 + nt],
                                  sg[:, :, :nt], reads=[sgk], writes=[('zT_s', cg, t0)])
                    P.barrier()

                def A(name, shape, dt, n=1):
                    return [ph.enter_context(sbt("%s%d" % (name, i), list(shape), dt)) for i in range(n)]
                wts = Rot(A("g_w", [128, 4, 8, 128], BF16, 1), 'g_w')
                QT = A("g_QT", [128, LT], BF16, 2)
                KT = A("g_KT", [128, LT], BF16, 2)
                Ktok = A("g_Kt", [128, NBLK, 128], BF16, 2)
                Vtok = A("g_Vt", [128, NBLK, 128], BF16, 1)[0]
                O = A("g_O", [128, LT], F32, 1)
                S32a = A("g_S32a", [128, NCH + 1, 128], F32, 1)[0]
                Sbfa = A("g_Sbfa", [128, NCH, 128], BF16, 1)[0]
                atts = Rot(A("g_atts", [128, 4, 64], BF16, 6), 'g_atts')
                yst = Rot(A("g_y", [128, 512], BF16, 2), 'g_y')
                NCHN = 4
                t_e = Rot(A("g_te", [128, 512], F32, 2 * NCHN), 'g_te')
                t_q = Rot(A("g_tq", [128, 512], BF16, 4), 'g_tq')
                t_g1 = Rot(A("g_g1", [128, 512], F32, NCHN), 'g_g1')
                t_g2 = Rot(A("g_g2", [128, 512], F32, NCHN), 'g_g2')
                t_b = Rot(A("g_b", [128, 512], F32, NCHN), 'g_b')
                t_E = Rot(A("g_E", [128, 512], F32, NCHN), 'g_E')
                t_kd = Rot(A("g_kd", [128, 512], BF16, NCHN), 'g_kd')

                def tkey(c):
                    return 0 if c < 4 else LC + 512 * ((c - 4) // 8)

                batches = [tiles[0:2], tiles[2:4], tiles[4:5]]

                def emit_front(hd, wt, wk, batch):
                    fr = []
                    for (t0, nt, r) in batch:
                        blk0 = t0 // 128
                        nbk = nt // 128
                        pp = {}
                        for role in (0, 1, 2):
                            pt, pkey = next_ps()
                            for c in range(8):
                                P.mm(pt[:, :nt], wt[:, role, c, :], hT[:, c, t0:t0 + nt], c == 0, c == 7,
                                     reads=[wk, ('hT', t0, c)], writes=[pkey])
                            pp[role] = (pt, pkey)
                        pv, pvk = next_ps()
                        for tb in range(nbk):
                            for c in range(8):
                                P.mm(pv[:, tb * 128:(tb + 1) * 128], hT[:, c, t0 + tb * 128:t0 + (tb + 1) * 128],
                                     wt[:, 3, c, :], c == 0, c == 7, reads=[wk, ('hT', t0, c)], writes=[pvk])
                        qs, qsk = t_q.next()
                        P.cp('act', qs[:, :nt], pp[0][0][:, :nt], reads=[pp[0][1]], writes=[qsk])
                        es = []
                        for d in range(2):
                            e_, ek = t_e.next()
                            P.act(e_[:, :nt], pp[1 + d][0][:, :nt], AF.Exp, reads=[pp[1 + d][1]], writes=[ek], scale=-1.0)
                            es.append((e_, ek))
                        P.cp('act', Vtok[:, blk0:blk0 + nbk, :], pv[:, :nt].rearrange("p (b v) -> p b v", v=128),
                             reads=[pvk], writes=[('Vt', t0)])
                        fr.append((t0, nt, qs, qsk, es))
                    return fr

                def emit_chain(hd, fr):
                    ch = []
                    for (t0, nt, qs, qsk, es) in fr:
                        for d in range(2):
                            g1, g1k = t_g1.next()
                            g2, g2k = t_g2.next()
                            bt, bk = t_b.next()
                            Et, Ek = t_E.next()
                            kd, kdk = t_kd.next()
                            ch.append(dict(t0=t0, nt=nt, d=d, qs=qs, qsk=qsk, e=es[d][0], ek=es[d][1], g1=g1, g1k=g1k,
                                           g2=g2, g2k=g2k, bt=bt, bk=bk, Et=Et, Ek=Ek, kd=kd, kdk=kdk))
                    for c_ in ch:
                        nt = c_['nt']
                        P.act(c_['g1'][:, :nt], c_['e'][:, :nt], AF.Ln, reads=[c_['ek'], 'hglb'], writes=[c_['g1k']],
                              scale=hglb[:, hd:hd + 1], bias=1.0)
                        P.act(c_['g2'][:, :nt], c_['e'][:, :nt], AF.Ln, reads=[c_['ek']], writes=[c_['g2k']], bias=1.0)
                    for c_ in ch:
                        nt = c_['nt']
                        P.tt('dve', c_['g1'][:, :nt], c_['g1'][:, :nt], c_['g2'][:, :nt], ALU.subtract,
                             reads=[c_['g1k'], c_['g2k']], writes=[c_['g1k']])
                        if c_['d'] == 0:
                            rs = cst[:, C_HGRF:C_HGRF + nt]
                            P.op('dve', (lambda o_, a_, b_: (lambda e: e.tensor_tensor_scan(
                                out=o_, data0=a_, data1=b_, initial=0.0, op0=ALU.mult, op1=ALU.add)))(
                                c_['bt'][:, :nt], rs, c_['g1'][:, :nt]), reads=[c_['g1k'], 'cst'], writes=[c_['bk']])
                        else:
                            rs = cst[:, C_HGRR:C_HGRR + nt]
                            P.op('dve', (lambda o_, a_, b_: (lambda e: e.tensor_tensor_scan(
                                out=o_, data0=a_, data1=b_, initial=0.0, op0=ALU.mult, op1=ALU.add)))(
                                c_['bt'][:, :nt][:, ::-1], rs[:, ::-1], c_['g1'][:, :nt][:, ::-1]),
                                reads=[c_['g1k'], 'cst'], writes=[c_['bk']])
                    for c_ in ch:
                        nt = c_['nt']
                        P.tt('pool', c_['g2'][:, :nt], c_['g2'][:, :nt], c_['bt'][:, :nt], ALU.add,
                             reads=[c_['g2k'], c_['bk']], writes=[c_['g2k']])
                    for c_ in ch:
                        nt = c_['nt']
                        P.act(c_['Et'][:, :nt], c_['bt'][:, :nt], AF.Exp, reads=[c_['bk']], writes=[c_['Ek']])
                        P.act(c_['g1'][:, :nt], c_['g2'][:, :nt], AF.Exp, reads=[c_['g2k']], writes=[c_['g1k']], scale=-1.0)
                    for c_ in ch:
                        nt, t0, d = c_['nt'], c_['t0'], c_['d']
                        P.tt('dve', QT[d][:, t0:t0 + nt], c_['qs'][:, :nt], c_['Et'][:, :nt], ALU.mult,
                             reads=[c_['qsk'], c_['Ek']], writes=[('QT', d, t0)])
                    for c_ in ch:
                        nt, t0, d = c_['nt'], c_['t0'], c_['d']
                        c0 = t0 // 64
                        ncn = nt // 64
                        P.stt('dve', KT[d][:, t0:t0 + nt], c_['e'][:, :nt], hgoml[:, hd:hd + 1], c_['g1'][:, :nt],
                              ALU.mult, ALU.mult, reads=[c_['ek'], c_['g1k'], 'hgoml'], writes=[('KT', d, t0)])
                        src = c_['Et'][:, 63:nt:64] if d == 0 else c_['Et'][:, 0:nt:64]
                        P.cp('pool', dec[d][:, c0:c0 + ncn], src, reads=[c_['Ek']], writes=[('dec', d, t0)])
                        P.tt('pool', c_['kd'][:, :nt].rearrange("p (c t) -> p c t", t=64),
                             KT[d][:, t0:t0 + nt].rearrange("p (c t) -> p c t", t=64),
                             dec[d][:, c0:c0 + ncn].unsqueeze(2).to_broadcast([128, ncn, 64]), ALU.mult,
                             reads=[('KT', d, t0), ('dec', d, t0)], writes=[c_['kdk']])
                    trs = []
                    for c_ in ch:
                        nt, t0, d = c_['nt'], c_['t0'], c_['d']
                        nbk = nt // 128
                        ptr, ptrk = next_ps()
                        ptb = ptr[:].bitcast(BF16)
                        for tb in range(nbk):
                            P.tr(ptb[:, tb * 128:(tb + 1) * 128], c_['kd'][:, tb * 128:(tb + 1) * 128], identbf[:],
                                 reads=[c_['kdk'], 'identbf'], writes=[ptrk])
                        trs.append((ptb, ptrk))
                    for c_, (ptb, ptrk) in zip(ch, trs):
                        nt, t0, d = c_['nt'], c_['t0'], c_['d']
                        blk0 = t0 // 128
                        nbk = nt // 128
                        P.cp('act', Ktok[d][:, blk0:blk0 + nbk, :], ptb[:, :nt].rearrange("p (b k) -> p b k", k=128),
                             reads=[ptrk], writes=[('Kt', d, t0)])

                dec = A("g_dec", [128, NCH], F32, 2)
                for hd in range(16):
                    wt, wk = wts.next()
                    for role, cb in enumerate((0, 2048, 4096, 6144)):
                        P.dma('sp', wt[:, role], W[:, :, cb + hd * 128:cb + (hd + 1) * 128], writes=[wk])
                    fr_prev = emit_front(hd, wt, wk, batches[0])
                    for bi in range(len(batches)):
                        fr_next = emit_front(hd, wt, wk, batches[bi + 1]) if bi + 1 < len(batches) else None
                        emit_chain(hd, fr_prev)
                        fr_prev = fr_next
                    orders = [list(range(NCH)), [3, 2, 1, 0] + list(range(NCH - 1, 3, -1))]
                    groups = [(0, 4)] + [(4 + 8 * i_, 8) for i_ in range((NCH - 4) // 8)]
                    for d in range(2):
                        order = orders[d]
                        P.memset('pool', S32a[:, 0, :], 0.0, writes=[('S32a', 0)])
                        for r in range(NCH):
                            c = order[r]
                            blk = c // 2
                            pb = 64 * (c % 2)
                            t0k = tkey(c)
                            pu, puk = next_ps()
                            P.mm(pu[:, 0:128], Ktok[d][pb:pb + 64, blk, :], Vtok[pb:pb + 64, blk, :],
                                 True, True, reads=[('Kt', d, t0k), ('Vt', t0k)], writes=[puk])
                            P.stt('dve', S32a[:, r + 1, :], S32a[:, r, :], dec[d][:, c:c + 1], pu[:, 0:128],
                                  ALU.mult, ALU.add, reads=[puk, ('S32a', r), ('dec', d, t0k)], writes=[('S32a', r + 1)])
                        def emit_scores(gi):
                            r0, ns = groups[gi]
                            cs = [order[r_] for r_ in range(r0, r0 + ns)]
                            cfirst = min(cs)
                            t0k = tkey(cfirst)
                            nbk = ns // 2
                            pa, pak = next_ps()
                            for c in cs:
                                bl = (c - cfirst) // 2
                                pb = 64 * (c % 2)
                                P.mm(pa[pb:pb + 64, 64 * bl:64 * (bl + 1)], KT[d][:, 64 * c:64 * c + 64], QT[d][:, 64 * c:64 * c + 64],
                                     True, True, reads=[('KT', d, t0k), ('QT', d, t0k)], writes=[pak])
                            ats = []
                            for par in range(2):
                                mk = cst[:, C_HGM4 + 128 * d + 64 * par:C_HGM4 + 128 * d + 64 * par + 64]
                                at_, atk = atts.next()
                                P.tt('dve', at_[:, 0:nbk, :], pa[:, 0:64 * nbk].rearrange("p (b t) -> p b t", t=64),
                                     mk.unsqueeze(1).to_broadcast([128, nbk, 64]), ALU.mult, reads=[pak, 'cst'], writes=[atk])
                                ats.append((at_, atk))
                            P.cp('act', Sbfa[:, r0:r0 + ns, :], S32a[:, r0:r0 + ns, :],
                                 reads=[('S32a', r_) for r_ in range(r0, r0 + ns)], writes=[('Sbfa', r0)])
                            return (r0, ns, cs, cfirst, t0k, ats)

                        def emit_outputs(sc):
                            r0, ns, cs, cfirst, t0k, ats = sc
                            ntk = 64 * ns
                            po, pok = next_ps()
                            for ri, c in enumerate(cs):
                                bl = (c - cfirst) // 2
                                blk = c // 2
                                col = 64 * (c - cfirst)
                                at_, atk = ats[c % 2]
                                P.mm(po[:, col:col + 64], Vtok[:, blk, :], at_[:, bl, :], True, False,
                                     reads=[('Vt', t0k), atk], writes=[pok])
                                P.mm(po[:, col:col + 64], Sbfa[:, r0 + ri, :], QT[d][:, 64 * c:64 * c + 64], False, True,
                                     reads=[('Sbfa', r0), ('QT', d, t0k)], writes=[pok])
                            if d == 0:
                                P.cp('act', O[0][:, t0k:t0k + ntk], po[:, :ntk], reads=[pok], writes=[('O', 0, t0k)])
                            else:
                                P.tt('dve', O[0][:, t0k:t0k + ntk], po[:, :ntk], O[0][:, t0k:t0k + ntk], ALU.add,
                                     reads=[pok, ('O', 0, t0k)], writes=[('O', 0, t0k)])

                        sc_prev = emit_scores(0)
                        for gi in range(len(groups)):
                            sc_next = emit_scores(gi + 1) if gi + 1 < len(groups) else None
                            emit_outputs(sc_prev)
                            sc_prev = sc_next
                    nrm = []
                    for (t0, nt, r) in tiles:
                        sq, sqk = (t_g1 if len(nrm) % 2 == 0 else t_g2).next()
                        rt, rk = (t_b if len(nrm) % 2 == 0 else t_E).next()
                        nrm.append((t0, nt, sq, sqk, rt, rk))
                    for (t0, nt, sq, sqk, rt, rk) in nrm:
                        P.act(sq[:, :nt], O[0][:, t0:t0 + nt], AF.Square, reads=[('O', 0, t0)], writes=[sqk])
                    pns = []
                    for (t0, nt, sq, sqk, rt, rk) in nrm:
                        pn, pnk = next_ps()
                        P.mm(pn[:, :nt], ones32[:], sq[:, :nt], True, True, reads=['ones32', sqk], writes=[pnk])
                        pns.append((pn, pnk))
                    for (t0, nt, sq, sqk, rt, rk), (pn, pnk) in zip(nrm, pns):
                        P.act(rt[:, :nt], pn[:, :nt], AF.Ln, reads=[pnk], writes=[rk], scale=1.0 / 128.0, bias=EPS)
                    for (t0, nt, sq, sqk, rt, rk) in nrm:
                        P.act(rt[:, :nt], rt[:, :nt], AF.Exp, reads=[rk], writes=[rk], scale=-0.5)
                    for (t0, nt, sq, sqk, rt, rk) in nrm:
                        ys, ysk = yst.next()
                        P.stt('dve', ys[:, :nt], O[0][:, t0:t0 + nt], hgng[:, hd:hd + 1], rt[:, :nt], ALU.mult, ALU.mult,
                              reads=[('O', 0, t0), rk, 'hgng'], writes=[ysk])
                        P.dma('pool', yT_s[hd * 128:(hd + 1) * 128, t0:t0 + nt], ys[:, :nt], reads=[ysk],
                              writes=[('yT_s', hd, t0)])
            P.barrier()


        NK = LT // 8
        NKC = LC // 8
        A1 = [galloc("s5A1_%d" % d, [128, 2, 64], F32) for d in range(2)]
        A2 = [galloc("s5A2_%d" % d, [128, 2, 64], F32) for d in range(2)]
        glub = galloc("s5glub", [128, 16], F32)
        B1 = [galloc("s5B1_%d" % d, [128, 2, 64], F32) for d in range(2)]
        B2 = [galloc("s5B2_%d" % d, [128, 2, 64], F32) for d in range(2)]

        def s5_prepare():
            with contextlib.ExitStack() as ph:
                def A(name, shape, dt):
                    return ph.enter_context(sbt(name, list(shape), dt))
                seltmp = A("p_seltmp", [128, 8, 128], F32)
                Sel = A("p_sel", [128, 8, 8, 128], BF16)
                P.tt('dve', seltmp[:], cst[:, C_IM:C_IM + 1024].rearrange("p (b q) -> p b q", b=8),
                     cst[:, C_MM:C_MM + 128].unsqueeze(1).to_broadcast([128, 8, 128]), ALU.mult,
                     reads=['cst'], writes=['seltmp'])
                for a in range(8):
                    P.ts('dve', Sel[:, a, :, :], seltmp[:], cst[:, C_GM + a:C_GM + a + 1], None, ALU.mult,
                         reads=['seltmp', 'cst'], writes=['Sel'])
                P.dma('pool', sel_s, Sel[:].rearrange("p a b q -> p (a b q)"), reads=['Sel'], writes=['sel_s'])
                P.dma('sp', glub[:], s5_glu_b[0].rearrange("(c p) -> p c", p=128), writes=['glub'],
                      allow_slow_non_contiguous=True)
                dtab = A("p_dtab", [128, 128], F32)
                for i in range(8):
                    P.dma('sp', dtab[16 * i:16 * (i + 1), :], s5_d[0].rearrange("(g m) -> m g", m=16), writes=['dtab'],
                          allow_slow_non_contiguous=True)
                SH = [64, 128]
                sm = {}

                def T(name, shape=SH):
                    t = A("p_" + name, shape, F32)
                    sm[name] = t
                    return t
                ki = A("p_ki", SH, I32)
                tmpa = T("tmpa")
                tmpb = T("tmpb")
                per = []
                Ct = []
                for d in range(2):
                    nm = lambda x_: "%s%d" % (x_, d)
                    lr = T(nm("lr"))
                    li = T(nm("li"))
                    ls = T(nm("ls"))
                    P.dma('sp', lr[:], s5_lam_re[0, d].rearrange("g p -> p g"), writes=[nm("lr")], allow_slow_non_contiguous=True)
                    P.dma('sp', li[:], s5_lam_im[0, d].rearrange("g p -> p g"), writes=[nm("li")], allow_slow_non_contiguous=True)
                    P.dma('sp', ls[:], s5_log_step[0, d].partition_broadcast(64), writes=[nm("ls")])
                    P.act(ls[:], ls[:], AF.Exp, reads=[nm("ls")], writes=[nm("ls")])
                    ar = T(nm("ar"))
                    ang = T(nm("ang"))
                    P.tt('dve', ar[:], lr[:], ls[:], ALU.mult, reads=[nm("lr"), nm("ls")], writes=[nm("ar")])
                    P.tt('dve', ang[:], li[:], ls[:], ALU.mult, reads=[nm("li"), nm("ls")], writes=[nm("ang")])
                    mag = T(nm("mag"))
                    magi = T(nm("magi"))
                    P.act(mag[:], ar[:], AF.Exp, reads=[nm("ar")], writes=[nm("mag")])
                    P.act(magi[:], ar[:], AF.Exp, reads=[nm("ar")], writes=[nm("magi")], scale=-1.0)
                    sn = T(nm("sn"))
                    cs = T(nm("cs"))
                    for (dst, shift, dk) in ((sn, 0.0, nm("sn")), (cs, math.pi / 2, nm("cs"))):
                        P.ts('dve', tmpa[:], ang[:], shift, 1.0 / TWO_PI, ALU.add, ALU.mult, reads=[nm("ang")], writes=['tmpa'])
                        P.cp('dve', ki[:], tmpa[:], reads=['tmpa'], writes=['ki'])
                        P.cp('dve', tmpb[:], ki[:], reads=['ki'], writes=['tmpb'])
                        P.stt('dve', tmpa[:], tmpb[:], -TWO_PI, ang[:], ALU.mult, ALU.add, reads=['tmpb', nm("ang")], writes=['tmpa'])
                        P.ts('dve', tmpa[:], tmpa[:], shift, 3.14159, ALU.add, ALU.min, reads=['tmpa'], writes=['tmpa'])
                        P.ts('dve', tmpa[:], tmpa[:], -3.14159, None, ALU.max, reads=['tmpa'], writes=['tmpa'])
                        P.act(dst[:], tmpa[:], AF.Sin, reads=['tmpa'], writes=[dk])
                    a_re = T(nm("a_re"))
                    a_im = T(nm("a_im"))
                    i_re = T(nm("i_re"))
                    i_im = T(nm("i_im"))
                    P.tt('dve', a_re[:], mag[:], cs[:], ALU.mult, reads=[nm("mag"), nm("cs")], writes=[nm("a_re")])
                    P.tt('dve', a_im[:], mag[:], sn[:], ALU.mult, reads=[nm("mag"), nm("sn")], writes=[nm("a_im")])
                    P.tt('dve', i_re[:], magi[:], cs[:], ALU.mult, reads=[nm("magi"), nm("cs")], writes=[nm("i_re")])
                    P.tt('dve', i_im[:], magi[:], sn[:], ALU.mult, reads=[nm("magi"), nm("sn")], writes=[nm("i_im")])
                    P.ts('dve', i_im[:], i_im[:], -1.0, None, ALU.mult, reads=[nm("i_im")], writes=[nm("i_im")])
                    cf_re = T(nm("cf_re"))
                    cf_im = T(nm("cf_im"))
                    nr = mag
                    P.ts('dve', nr[:], a_re[:], -1.0, None, ALU.add, reads=[nm("a_re")], writes=[nm("mag")])
                    den = magi
                    P.tt('dve', den[:], lr[:], lr[:], ALU.mult, reads=[nm("lr")], writes=[nm("magi")])
                    P.tt('dve', tmpa[:], li[:], li[:], ALU.mult, reads=[nm("li")], writes=['tmpa'])
                    P.tt('dve', den[:], den[:], tmpa[:], ALU.add, reads=[nm("magi"), 'tmpa'], writes=[nm("magi")])
                    P.recip(den[:], den[:], reads=[nm("magi")], writes=[nm("magi")])
                    P.tt('dve', cf_re[:], nr[:], lr[:], ALU.mult, reads=[nm("mag"), nm("lr")], writes=[nm("cf_re")])
                    P.tt('dve', tmpa[:], a_im[:], li[:], ALU.mult, reads=[nm("a_im"), nm("li")], writes=['tmpa'])
                    P.tt('dve', cf_re[:], cf_re[:], tmpa[:], ALU.add, reads=[nm("cf_re"), 'tmpa'], writes=[nm("cf_re")])
                    P.tt('dve', cf_re[:], cf_re[:], den[:], ALU.mult, reads=[nm("cf_re"), nm("magi")], writes=[nm("cf_re")])
                    P.tt('dve', cf_im[:], a_im[:], lr[:], ALU.mult, reads=[nm("a_im"), nm("lr")], writes=[nm("cf_im")])
                    P.tt('dve', tmpa[:], nr[:], li[:], ALU.mult, reads=[nm("mag"), nm("li")], writes=['tmpa'])
                    P.tt('dve', cf_im[:], cf_im[:], tmpa[:], ALU.subtract, reads=[nm("cf_im"), 'tmpa'], writes=[nm("cf_im")])
                    P.tt('dve', cf_im[:], cf_im[:], den[:], ALU.mult, reads=[nm("cf_im"), nm("magi")], writes=[nm("cf_im")])
                    p_re, p_im = sn, cs
                    P.cp('dve', p_re[:], a_re[:], reads=[nm("a_re")], writes=[nm("sn")])
                    P.cp('dve', p_im[:], a_im[:], reads=[nm("a_im")], writes=[nm("cs")])
                    for _sq in range(3):
                        P.tt('dve', tmpa[:], p_re[:], p_re[:], ALU.mult, reads=[nm("sn")], writes=['tmpa'])
                        P.tt('dve', tmpb[:], p_im[:], p_im[:], ALU.mult, reads=[nm("cs")], writes=['tmpb'])
                        P.tt('dve', p_im[:], p_re[:], p_im[:], ALU.mult, reads=[nm("sn"), nm("cs")], writes=[nm("cs")])
                        P.ts('dve', p_im[:], p_im[:], 2.0, None, ALU.mult, reads=[nm("cs")], writes=[nm("cs")])
                        P.tt('dve', p_re[:], tmpa[:], tmpb[:], ALU.subtract, reads=['tmpa', 'tmpb'], writes=[nm("sn")])
                    P.dma('pool', a8_s[d, 0], p_re[:], reads=[nm("sn")], writes=[('a8s', d, 0)])
                    P.dma('pool', a8_s[d, 1], p_im[:], reads=[nm("cs")], writes=[('a8s', d, 1)])
                    for comp in range(2):
                        src = a8_s[d, comp].rearrange("p (pr h) -> h p pr", h=2)
                        for h in range(2):
                            P.dma('sp', A1[d][64 * h:64 * (h + 1), comp if False else 0, :] if False else
                                  (A1[d][64 * h:64 * (h + 1), 0, :] if comp == 0 else A2[d][64 * h:64 * (h + 1), 1, :]),
                                  src[h], reads=[('a8s', d, comp)], writes=[('A12', d, comp, h)],
                                  allow_slow_non_contiguous=True)
                    P.cp('dve', A1[d][:, 1, :], A1[d][:, 0, :], reads=[('A12', d, 0, 0), ('A12', d, 0, 1)], writes=[('A1', d)])
                    P.ts('dve', A2[d][:, 0, :], A2[d][:, 1, :], -1.0, None, ALU.mult,
                         reads=[('A12', d, 1, 0), ('A12', d, 1, 1)], writes=[('A2', d)])
                    bt1 = A("p_bt1_%d" % d, [128, 64], F32)
                    bt2 = A("p_bt2_%d" % d, [128, 64], F32)
                    P.tt('dve', bt1[:], A1[d][:, 0, :], A1[d][:, 0, :], ALU.mult, reads=[('A1', d)], writes=[('bt1', d)])
                    P.tt('dve', bt2[:], A2[d][:, 1, :], A2[d][:, 1, :], ALU.mult, reads=[('A2', d)], writes=[('bt2', d)])
                    P.tt('dve', B1[d][:, 0, :], bt1[:], bt2[:], ALU.subtract, reads=[('bt1', d), ('bt2', d)], writes=[('B1', d)])
                    P.cp('dve', B1[d][:, 1, :], B1[d][:, 0, :], reads=[('B1', d)], writes=[('B1', d)])
                    P.tt('dve', bt1[:], A1[d][:, 0, :], A2[d][:, 1, :], ALU.mult, reads=[('A1', d), ('A2', d), ('bt1', d)], writes=[('bt1', d)])
                    P.ts('dve', B2[d][:, 1, :], bt1[:], 2.0, None, ALU.mult, reads=[('bt1', d)], writes=[('B2', d)])
                    P.ts('dve', B2[d][:, 0, :], bt1[:], -2.0, None, ALU.mult, reads=[('bt1', d)], writes=[('B2', d)])
                    per.append(dict(a_re=a_re, a_im=a_im, i_re=i_re, i_im=i_im, cf_re=cf_re, cf_im=cf_im,
                                    k=[nm("a_re"), nm("a_im"), nm("i_re"), nm("i_im"), nm("cf_re"), nm("cf_im")]))
                    ctd = []
                    for comp, srcC in enumerate((s5_c_re, s5_c_im)):
                        cn = A("p_cn%d%d" % (d, comp), [128, 1024], F32)
                        ct = A("p_ct%d%d" % (d, comp), [64, 128, 16], F32)
                        P.dma('sp', cn[:], srcC[0, d].rearrange("g n p -> g (n p)"), writes=[('cn', d, comp)])
                        for n4 in range(4):
                            pt, pkey = next_ps()
                            for nn in range(4):
                                n = n4 * 4 + nn
                                P.tr(pt[0:64, nn * 128:(nn + 1) * 128], cn[:, n * 64:(n + 1) * 64], ident32,
                                     reads=[('cn', d, comp), 'cst'], writes=[pkey])
                            P.cp('act', ct[:, :, n4 * 4:(n4 + 1) * 4].rearrange("p g n -> p n g"),
                                 pt[0:64, :].rearrange("p (n g) -> p n g", n=4), reads=[pkey], writes=[('ct', d, comp)])
                        ctd.append(ct)
                    Ct.append(ctd)

                GQ = 16
                Bt = [A("p_B%d" % c_, [64, GQ, 16], F32) for c_ in range(2)]
                Xt = [A("p_X%d" % c_, [64, GQ, 8, 16], F32) for c_ in range(2)]
                Vt = [A("p_V%d" % c_, [64, GQ, 8, 16], F32) for c_ in range(2)]
                Vc = [[A("p_Vc%d%d" % (a_, c_), [64, GQ, 16], F32) for c_ in range(2)] for a_ in range(2)]
                Wc = [[A("p_Wc%d%d" % (a_, c_), [64, GQ, 16], F32) for c_ in range(2)] for a_ in range(2)]
                Wb = [A("p_Wb%d" % c_, [64, GQ, 8, 16], BF16) for c_ in range(2)]
                cm = [A("p_cm%d" % c_, [64, GQ, 16], F32) for c_ in range(4)]
                Wacc = A("p_Wacc", [128, GQ, 128], F32)
                Wtmp = A("p_Wtmp", [128, 4, 128], F32)
                Wib = A("p_Wib", [128, GQ, 128], BF16)
                WSb = A("p_WSb", [128, 8, 64], BF16)
                uid = [0]

                def cmul(eng, cs, o_re, o_im, x_re, x_im, y_re, y_im, rk, wk):
                    kk = [('cm', cs, i_) for i_ in range(4)]
                    c_ = cmsets[cs]
                    P.tt(eng, c_[0][:], x_re, y_re, ALU.mult, reads=rk, writes=[kk[0]])
                    P.tt(eng, c_[1][:], x_im, y_im, ALU.mult, reads=rk, writes=[kk[1]])
                    P.tt(eng, c_[2][:], x_re, y_im, ALU.mult, reads=rk, writes=[kk[2]])
                    P.tt(eng, c_[3][:], x_im, y_re, ALU.mult, reads=rk, writes=[kk[3]])
                    P.tt(eng, o_re, c_[0][:], c_[1][:], ALU.subtract, reads=[kk[0], kk[1]], writes=wk)
                    P.tt(eng, o_im, c_[2][:], c_[3][:], ALU.add, reads=[kk[2], kk[3]], writes=wk)

                cmsets = [cm] + [[A("p_cm%d_%d" % (a_, c_), [64, GQ, 16], F32) for c_ in range(4)] for a_ in range(2)]

                for gq in range(128 // GQ):
                    g0 = gq * GQ
                    for d in range(2):
                        pd = per[d]

                        def bc(t):
                            return t[:, g0:g0 + GQ].unsqueeze(2).to_broadcast([64, GQ, 16])
                        P.dma('sp', Bt[0][:], s5_b_re[0, d, g0:g0 + GQ].rearrange("g p m -> p g m"), writes=['Bt0'])
                        P.dma('sp', Bt[1][:], s5_b_im[0, d, g0:g0 + GQ].rearrange("g p m -> p g m"), writes=['Bt1'])
                        cre = Ct[d][0][:, g0:g0 + GQ, :]
                        cim = Ct[d][1][:, g0:g0 + GQ, :]

                        def xs(tau):
                            return (7 - tau) if d == 0 else tau

                        def xstep(tau):
                            if tau == 0:
                                cmul('dve', 0, Xt[0][:, :, xs(0), :], Xt[1][:, :, xs(0), :], Bt[0][:], Bt[1][:],
                                     bc(pd['cf_re']), bc(pd['cf_im']), ['Bt0', 'Bt1'] + pd['k'], [('X', xs(0))])
                            else:
                                cmul('dve', 0, Xt[0][:, :, xs(tau), :], Xt[1][:, :, xs(tau), :],
                                     Xt[0][:, :, xs(tau - 1), :], Xt[1][:, :, xs(tau - 1), :], bc(pd['a_re']), bc(pd['a_im']),
                                     [('X', xs(tau - 1))] + pd['k'], [('X', xs(tau))])

                        vorder = list(range(7, -1, -1)) if d == 0 else list(range(8))

                        def vstep(idx):
                            jj = vorder[idx]
                            cur = Vc[idx % 2]
                            prev = Vc[(idx - 1) % 2]
                            if idx == 0:
                                P.cp('pool', cur[0][:], cre, reads=[('ct', d, 0)], writes=[('Vc', idx % 2)])
                                P.cp('pool', cur[1][:], cim, reads=[('ct', d, 1)], writes=[('Vc', idx % 2)])
                            else:
                                cmul('pool', 1, cur[0][:], cur[1][:], prev[0][:], prev[1][:], bc(pd['i_re']), bc(pd['i_im']),
                                     [('Vc', (idx - 1) % 2)] + pd['k'], [('Vc', idx % 2)])
                            P.cp('act', Vt[0][:, :, jj, :], cur[0][:], reads=[('Vc', idx % 2)], writes=[('V', jj)])
                            P.act(Vt[1][:, :, jj, :], cur[1][:], AF.Copy, reads=[('Vc', idx % 2)], writes=[('V', jj)], scale=-1.0)

                        def wstep(i_):
                            e_ = i_ + 1
                            jj = (e_ - 1) if d == 0 else (8 - e_)
                            cur = Wc[e_ % 2]
                            prev = Wc[(e_ - 1) % 2]
                            if e_ == 1:
                                cmul('dve', 2, cur[0][:], cur[1][:], cre, cim, bc(pd['a_re']), bc(pd['a_im']),
                                     [('ct', d, 0), ('ct', d, 1)] + pd['k'], [('Wc', e_ % 2)])
                            else:
                                cmul('dve', 2, cur[0][:], cur[1][:], prev[0][:], prev[1][:], bc(pd['a_re']), bc(pd['a_im']),
                                     [('Wc', (e_ - 1) % 2)] + pd['k'], [('Wc', e_ % 2)])
                            P.cp('act', Wb[0][:, :, jj, :], cur[0][:], reads=[('Wc', e_ % 2)], writes=[('Wb', 0)])
                            P.act(Wb[1][:, :, jj, :], cur[1][:], AF.Copy, reads=[('Wc', e_ % 2)], writes=[('Wb', 1)], scale=-1.0)

                        for i_ in range(8):
                            xstep(i_)
                            vstep(i_)
                            wstep(i_)
                        for comp in range(2):
                            P.dma('pool', Winter_s[d, comp, g0:g0 + GQ].rearrange("g p jn -> p g jn"),
                                  Wb[comp][:].rearrange("p g j n -> p g (j n)"), reads=[('Wb', comp)],
                                  writes=[('Winter_s', d, comp, gq)])
                        xkeys = [('X', i_) for i_ in range(8)]
                        vkeys = [('V', i_) for i_ in range(8)]
                        for comp in range(2):
                            for g8 in range(GQ // 8):
                                pt, pkey = next_ps()
                                for gg in range(8):
                                    g = g8 * 8 + gg
                                    P.tr(pt[:, gg * 64:(gg + 1) * 64], Xt[comp][:, g, :, :].rearrange("p s m -> p (s m)"),
                                         ident32[0:64, 0:64], reads=xkeys + ['cst'], writes=[pkey])
                                P.cp('act', WSb[:], pt[:].rearrange("p (g q) -> p g q", g=8), reads=[pkey], writes=['WSb'])
                                P.dma('pool', WS_s[d, comp, g0 + g8 * 8:g0 + (g8 + 1) * 8].rearrange("g q p -> q g p"),
                                      WSb[:], reads=['WSb'], writes=[('WS_s', d, comp, gq, g8)])
                        for g4 in range(GQ // 4):
                            pt, pkey = next_ps()
                            for gg in range(4):
                                g = g4 * 4 + gg
                                P.mm(pt[:, gg * 128:(gg + 1) * 128], Xt[0][:, g, :, :].rearrange("p s m -> p (s m)"),
                                     Vt[0][:, g, :, :].rearrange("p j n -> p (j n)"), True, False,
                                     reads=xkeys + vkeys, writes=[pkey])
                                P.mm(pt[:, gg * 128:(gg + 1) * 128], Xt[1][:, g, :, :].rearrange("p s m -> p (s m)"),
                                     Vt[1][:, g, :, :].rearrange("p j n -> p (j n)"), False, True,
                                     reads=xkeys + vkeys, writes=[pkey])
                            msk = cst[:, C_MF:C_MF + 128] if d == 0 else cst[:, C_MR:C_MR + 128]
                            mb = msk.unsqueeze(1).to_broadcast([128, 4, 128])
                            if d == 0:
                                P.tt('dve', Wacc[:, g4 * 4:(g4 + 1) * 4, :], pt[:].rearrange("p (g q) -> p g q", g=4), mb,
                                     ALU.mult, reads=[pkey, 'cst'], writes=[('Wacc', g4)])
                            else:
                                P.tt('dve', Wtmp[:], pt[:].rearrange("p (g q) -> p g q", g=4), mb,
                                     ALU.mult, reads=[pkey, 'cst'], writes=['Wtmp'])
                                P.tt('pool', Wacc[:, g4 * 4:(g4 + 1) * 4, :], Wacc[:, g4 * 4:(g4 + 1) * 4, :], Wtmp[:],
                                     ALU.add, reads=['Wtmp', ('Wacc', g4)], writes=[('Wacc', g4)])
                    for g in range(GQ):
                        P.stt('dve', Wib[:, g, :], ident32, dtab[:, g0 + g:g0 + g + 1], Wacc[:, g, :], ALU.mult, ALU.add,
                              reads=['cst', 'dtab', ('Wacc', g // 4)], writes=['Wib'])
                    P.dma('pool', Wintra_s[g0:g0 + GQ].rearrange("g q r -> q g r"), Wib[:], reads=['Wib'],
                          writes=[('Wintra_s', gq)])
            P.barrier()

        def s5_phase_A(l, b):
            tiles = token_tiles(nb, b)
            W = wb_s5.rearrange("(c p) n -> p c n", p=128)
            with contextlib.ExitStack() as ph:
                hT = load_hT(ph, l, b, tiles)
                wts = Rot([ph.enter_context(sbt("s_w%d" % i, [128, 8, 512], BF16)) for i in range(2)], 's_w')
                stg = Rot([ph.enter_context(sbt("s_s%d" % i, [128, 4, 512], BF16)) for i in range(3)], 's_s')
                for cg in range(8):
                    wt, wk = wts.next()
                    P.dma('sp', wt[:], W[:, :, cg * 512:(cg + 1) * 512], writes=[wk])
                    for (t0, nt, r) in tiles:
                        sg, sgk = stg.next()
                        for hh in range(4):
                            pt, pkey = next_ps()
                            for c in range(8):
                                P.mm(pt[:, :nt], wt[:, c, hh * 128:(hh + 1) * 128], hT[:, c, t0:t0 + nt], c == 0, c == 7,
                                     reads=[wk, ('hT', t0, c)], writes=[pkey])
                            if cg >= 4:
                                P.act(sg[:, hh, :nt], pt[:, :nt], AF.Silu, reads=[pkey], writes=[sgk])
                            else:
                                P.cp('act' if hh % 2 else 'dve', sg[:, hh, :nt], pt[:, :nt], reads=[pkey], writes=[sgk])
                        dst = uT_s[cg * 512:(cg + 1) * 512, :] if cg < 4 else zT_s[(cg - 4) * 512:(cg - 3) * 512, :]
                        P.dma('pool', dst.rearrange("(h p) t -> p h t", p=128)[:, :, t0:t0 + nt], sg[:, :, :nt],
                              reads=[sgk], writes=[('s5A', cg, t0)])
            P.barrier()

        def s5_phase_B(l, b):
            GB = 32
            NP = GB // 2
            NBLK = 128 // GB
            with contextlib.ExitStack() as ph:
                def A(name, shape, dt, n=1):
                    return [ph.enter_context(sbt("%s%d" % (name, i), list(shape), dt)) for i in range(n)]
                Us = A("b5_U", [128, GB, NK], BF16, 2)
                Sel = A("b5_sel", [128, 8, 8, 128], BF16)[0]
                P.dma('sp', Sel[:].rearrange("p a b q -> p (a b q)"), sel_s, writes=['Sel'])
                X = A("b5_X", [128, 2, NP, NK + 2], F32, 2)
                SW = 8
                slb = [A("b5_sl%d" % d, [128, 2, NP, SW], F32, 2) for d in range(2)]
                Xb = A("b5_Xb", [128, 2, NP, NK], BF16, 2)
                uts = Rot(A("b5_u", [128, LT], BF16, 2), 'b5_u')
                wss = Rot(A("b5_ws", [128, 4, 64], BF16, 3), 'b5_ws')
                wis = Rot(A("b5_wi", [128, 128], BF16, 3), 'b5_wi')
                wns = Rot(A("b5_wn", [128, 4, 128], BF16, 2), 'b5_wn')
                gys = A("b5_gy", [128, NK], BF16, 8)
                gst = Rot(A("b5_gs", [128, 512], BF16, 1), 'b5_gs')
                tm = [A("b5_t%d" % d, [128, 2, NP, 2], F32, 2) for d in range(2)]

                def emit_us(gb):
                    gbase = gb * GB
                    U = Us[gb % 2]
                    for d in range(2):
                        P.memset('pool', X[d][:, :, :, 0:2], 0.0, writes=[('X', d)])
                    wsprev = None
                    for ccl in range(GB // 8):
                        cc = gb * (GB // 8) + ccl
                        ut, utk = uts.next()
                        P.dma('sp', ut[:], uT_s[cc * 128:(cc + 1) * 128, :], writes=[utk])
                        for gl in range(8):
                            g = cc * 8 + gl
                            gloc = g - gbase
                            pl = gloc // 2
                            pU, pUk = next_ps()
                            for i in range(8):
                                P.mm(pU[:, :NK], Sel[:, gl, i, :], ut[:, i:LT:8], i == 0, i == 7,
                                     reads=['Sel', utk], writes=[pUk])
                            P.cp('act', U[:, gloc, :], pU[:, :NK], reads=[pUk], writes=[('U', gb % 2, gloc)])
                            ws, wsk = wss.next()
                            P.dma('sp', ws[:], WS_s[:, :, g].rearrange("d c q p -> q (d c) p"), writes=[wsk])
                            if g % 2 == 0:
                                wsprev = (ws, wsk)
                                continue
                            for d in range(2):
                                for comp in range(2):
                                    pS, pSk = next_ps()
                                    P.mm(pS[0:64, :NK], wsprev[0][:, d * 2 + comp, :], U[:, gloc - 1, :], True, True,
                                         reads=[wsprev[1], ('U', gb % 2, gloc - 1)], writes=[pSk])
                                    P.mm(pS[64:128, :NK], ws[:, d * 2 + comp, :], U[:, gloc, :], True, True,
                                         reads=[wsk, ('U', gb % 2, gloc)], writes=[pSk])
                                    if d == 0:
                                        P.cp('act', X[0][:, comp, pl, 2:NK + 2], pS[:, :NK], reads=[pSk], writes=[('X', 0)])
                                    else:
                                        P.cp('act', X[1][:, comp, pl, NKC + 1:1:-1], pS[:, 0:NKC], reads=[pSk], writes=[('X', 1)])
                                        P.cp('act', X[1][:, comp, pl, NK + 1:NKC + 1:-1], pS[:, NKC:NK], reads=[pSk], writes=[('X', 1)])

                def emit_scan(gb):
                    p0 = (gb * GB) // 2
                    for d, eng in ((0, 'dve'), (1, 'pool')):
                        c1 = NK + 2
                        while c1 > 2:
                            c0 = max(2, c1 - SW)
                            w = c1 - c0
                            a1b = A1[d][:, :, p0:p0 + NP].unsqueeze(3).to_broadcast([128, 2, NP, w])
                            a2b = A2[d][:, :, p0:p0 + NP].unsqueeze(3).to_broadcast([128, 2, NP, w])
                            P.tt(eng, slb[d][0][:, :, :, 0:w], a1b, X[d][:, :, :, c0 - 1:c1 - 1], ALU.mult,
                                 reads=[('X', d), ('A1', d)], writes=[('slb', d, 0)])
                            P.tt(eng, slb[d][1][:, :, :, 0:w], a2b, X[d][:, ::-1, :, c0 - 1:c1 - 1], ALU.mult,
                                 reads=[('X', d), ('A2', d)], writes=[('slb', d, 1)])
                            P.tt(eng, X[d][:, :, :, c0:c1], X[d][:, :, :, c0:c1], slb[d][0][:, :, :, 0:w], ALU.add,
                                 reads=[('slb', d, 0)], writes=[('X', d)])
                            P.tt(eng, X[d][:, :, :, c0:c1], X[d][:, :, :, c0:c1], slb[d][1][:, :, :, 0:w], ALU.add,
                                 reads=[('slb', d, 1)], writes=[('X', d)])
                            c1 = c0
                    for c in range(2, NK + 2, 2):
                        for d, eng in ((0, 'dve'), (1, 'pool')):
                            xp = X[d][:, :, :, c - 2:c]
                            xps = X[d][:, ::-1, :, c - 2:c]
                            xk = X[d][:, :, :, c:c + 2]
                            b1b = B1[d][:, :, p0:p0 + NP].unsqueeze(3).to_broadcast([128, 2, NP, 2])
                            b2b = B2[d][:, :, p0:p0 + NP].unsqueeze(3).to_broadcast([128, 2, NP, 2])
                            P.tt(eng, tm[d][0][:], b1b, xp, ALU.mult, reads=[('X', d), ('B1', d)], writes=[('tm', d, 0)])
                            P.tt(eng, tm[d][1][:], b2b, xps, ALU.mult, reads=[('X', d), ('B2', d)], writes=[('tm', d, 1)])
                            P.tt(eng, xk, xk, tm[d][0][:], ALU.add, reads=[('tm', d, 0)], writes=[('X', d)])
                            P.tt(eng, xk, xk, tm[d][1][:], ALU.add, reads=[('tm', d, 1)], writes=[('X', d)])

                def emit_xb(gb):
                    P.cp('act', Xb[0][:], X[0][:, :, :, 1:NK + 1], reads=[('X', 0)], writes=[('Xb', 0)])
                    P.cp('act', Xb[1][:, :, :, 0:NKC], X[1][:, :, :, NKC:0:-1], reads=[('X', 1)], writes=[('Xb', 1)])
                    P.cp('act', Xb[1][:, :, :, NKC:NK], X[1][:, :, :, NK:NKC:-1], reads=[('X', 1)], writes=[('Xb', 1)])

                def emit_y(gb):
                    gbase = gb * GB
                    U = Us[gb % 2]
                    for ccl in range(GB // 8):
                        cc = gb * (GB // 8) + ccl
                        wn = None
                        for gl in range(8):
                            g = cc * 8 + gl
                            gloc = g - gbase
                            pl = gloc // 2
                            hf = g % 2
                            wi, wik = wis.next()
                            P.dma('sp', wi[:], Wintra_s[g], writes=[wik])
                            if hf == 0:
                                wn, wnk = wns.next()
                                P.dma('sp', wn[:], Winter_s[:, :, g:g + 2].rearrange("d c g p n -> (g p) (d c) n"), writes=[wnk])
                            pY, pYk = next_ps()
                            P.mm(pY[:, :NK], wi[:], U[:, gloc, :], True, False, reads=[wik, ('U', gb % 2, gloc)], writes=[pYk])
                            for d in range(2):
                                for comp in range(2):
                                    P.mm(pY[:, :NK], wn[64 * hf:64 * (hf + 1), d * 2 + comp, :],
                                         Xb[d][64 * hf:64 * (hf + 1), comp, pl, :], False, (d == 1 and comp == 1),
                                         reads=[wnk, ('Xb', d)], writes=[pYk])
                            P.act(gys[gl][:], pY[:, :NK], AF.Gelu_apprx_tanh, reads=[pYk], writes=[('gy', gl)])
                        for bk in range((LT + 511) // 512):
                            ncol = min(64, NK - 64 * bk)
                            pL, pLk = next_ps()
                            for j in range(8):
                                for gl in range(8):
                                    P.mm(pL[:, j:8 * ncol:8], Sel[:, j, gl, :], gys[gl][:, 64 * bk:64 * bk + ncol],
                                         gl == 0, gl == 7, reads=['Sel', ('gy', gl)], writes=[pLk])
                            gs, gsk = gst.next()
                            P.cp('act', gs[:, :8 * ncol], pL[:, :8 * ncol], reads=[pLk], writes=[gsk])
                            P.dma('sp', gT_s[cc * 128:(cc + 1) * 128, 512 * bk:512 * bk + 8 * ncol], gs[:, :8 * ncol],
                                  reads=[gsk], writes=[('gT_s', cc, bk)])

                for gb in range(NBLK + 1):
                    if gb < NBLK:
                        emit_us(gb)
                        emit_scan(gb)
                    if gb > 0:
                        emit_y(gb - 1)
                    if gb < NBLK:
                        emit_xb(gb)
            P.barrier()

        def s5_phase_G(l, b):
            tiles = token_tiles(nb, b)
            with contextlib.ExitStack() as ph:
                gT = ph.enter_context(sbt("g5_g", [128, 16, LT], BF16))
                gws = Rot([ph.enter_context(sbt("g5_w%d" % i, [128, 16, 128], BF16)) for i in range(2)], 'g5_w')
                sgs = Rot([ph.enter_context(sbt("g5_s%d" % i, [128, 512], F32)) for i in range(2)], 'g5_s')
                yss = Rot([ph.enter_context(sbt("g5_y%d" % i, [128, 512], BF16)) for i in range(2)], 'g5_y')
                for kc in range(16):
                    P.dma('sp', gT[:, kc, :], gT_s[kc * 128:(kc + 1) * 128, :], writes=[('gT', kc)])
                gkeys = [('gT', kc) for kc in range(16)]
                Wg = wb_glu.rearrange("(k p) n -> p k n", p=128)
                for oc in range(16):
                    gw, gwk = gws.next()
                    P.dma('sp', gw[:], Wg[:, :, oc * 128:(oc + 1) * 128], writes=[gwk])
                    for (t0, nt, r) in tiles:
                        pG, pGk = next_ps()
                        for kc in range(16):
                            P.mm(pG[:, :nt], gw[:, kc, :], gT[:, kc, t0:t0 + nt], kc == 0, kc == 15,
                                 reads=[gwk, ('gT', kc)], writes=[pGk])
                        sg, sgk = sgs.next()
                        P.act(sg[:, :nt], pG[:, :nt], AF.Sigmoid, reads=[pGk, 'glub'], writes=[sgk], bias=glub[:, oc:oc + 1])
                        ys, ysk = yss.next()
                        P.tt('dve', ys[:, :nt], gT[:, oc, t0:t0 + nt], sg[:, :nt], ALU.mult, reads=[('gT', oc), sgk], writes=[ysk])
                        P.dma('pool', yT_s[oc * 128:(oc + 1) * 128, t0:t0 + nt], ys[:, :nt], reads=[ysk],
                              writes=[('yT_s', oc, t0)])
            P.barrier()

        for l in layers:
            kind = l % 3
            last = (l == DEPTH - 1)
            for b in range(nb):
                if kind == 0:
                    attn_phase_A(l, b)
                    if STOP[0] == 'A':
                        return finish()
                    attn_phase_B(l, b, last)
                    if STOP[0] == 'B':
                        return finish()
                elif kind == 2:
                    if b == 0:
                        hg_prepare(l)
                    hg_layer(l, b)
                    if STOP[0] == 'B':
                        return finish()
                else:
                    if b == 0:
                        s5_prepare()
                        if STOP[0] == 'prep':
                            return finish()
                    s5_phase_A(l, b)
                    if STOP[0] == 'A':
                        return finish()
                    s5_phase_B(l, b)
                    if STOP[0] == 'B1':
                        return finish()
                    s5_phase_G(l, b)
                    if STOP[0] == 'B':
                        return finish()
                phase_C(l, b, last)
        return finish()


INPUT_NAMES = ['x', 'c', 'ctx', 'c_ctx', 'ada_w', 'ada_b', 'ln_g', 'ln_b', 'w_out', 'attn_w_in', 'attn_sink',
               's5_w_in', 's5_lam_re', 's5_lam_im', 's5_log_step', 's5_b_re', 's5_b_im', 's5_c_re', 's5_c_im',
               's5_d', 's5_glu_w', 's5_glu_b', 'hg_w_in', 'hg_lb', 'hg_norm_g']


def kernel(**inputs):
    B = inputs['x'].shape[0]
    nb = B // NCORES
    nc = build(nb)
    consts = make_consts()
    in_maps = []
    for core in range(NCORES):
        m = {}
        for k in INPUT_NAMES:
            a = np.ascontiguousarray(np.asarray(inputs[k], dtype=np.float32))
            if k in ('x', 'c', 'ctx'):
                a = np.ascontiguousarray(a[core * nb:(core + 1) * nb])
            m[k] = a
        m['consts'] = consts
        m['posc'] = make_pos()
        in_maps.append(m)
    res = run_bass_kernel_spmd(nc, in_maps, core_ids=list(range(NCORES)))
    return np.concatenate([r['out'] for r in res.results], axis=0).astype(np.float32)
```

```python
import contextlib
import math
import numpy as np
import concourse.bass as bass
import concourse.mybir as mybir
from concourse.bass_utils import run_bass_kernel_spmd

F32 = mybir.dt.float32
BF16 = mybir.dt.bfloat16
I32 = mybir.dt.int32
ALU = mybir.AluOpType
AF = mybir.ActivationFunctionType

D = 1024
E = 2048
LC = 256
LL = 2048
LT = LC + LL
DEPTH = 4
ALPHA = (2.0 * DEPTH) ** 0.25
EPS = 1e-5
NCORES = 8
STOP = [None]
DBG = {}
TWO_PI = 2.0 * math.pi


class Prog:
    NSLOT = 8
    SAME_ENGINE_SYNC = True

    def __init__(self, nc, stack):
        self.nc = nc
        self.eng = {'pe': nc.tensor, 'act': nc.scalar, 'dve': nc.vector,
                    'pool': nc.gpsimd, 'sp': nc.sync}
        self.ops = {e: [] for e in self.eng}
        self.sem = {e: stack.enter_context(nc.semaphore('s_' + e)) for e in self.eng}
        self.cnt = {e: 0 for e in self.eng}
        self.dsem = {}
        self.dcnt = {}
        self.dnext = {}
        for q in ('sp', 'pool'):
            self.dsem[q] = [stack.enter_context(nc.semaphore('d_%s%d' % (q, i)))
                            for i in range(self.NSLOT)]
            self.dcnt[q] = [0] * self.NSLOT
            self.dnext[q] = 0
        self.lastw = {}
        self.readers = {}
        self.waited = {e: {} for e in self.eng}
        self.pending = {e: [] for e in self.eng}
        self.nops = 0

    def _deps(self, e, reads, writes, is_dma=False):
        toks = []
        for r in reads:
            w = self.lastw.get(r)
            if w is not None:
                toks.append(w)
        for r in writes:
            w = self.lastw.get(r)
            if w is not None:
                toks.append(w)
            toks.extend(self.readers.get(r, ()))
        need = {}
        for (te, sem, val, tdma) in toks:
            if te == e and not tdma and not is_dma and (e == 'pe' or not self.SAME_ENGINE_SYNC):
                continue
            k = id(sem)
            if need.get(k, (None, 0))[1] < val:
                need[k] = (sem, val)
        waits = self.pending[e]
        self.pending[e] = []
        wd = self.waited[e]
        for k, (sem, val) in need.items():
            if wd.get(k, 0) >= val:
                continue
            wd[k] = val
            waits.append((sem, val))
        return waits

    def _commit(self, tok, reads, writes):
        for r in writes:
            self.lastw[r] = tok
            self.readers[r] = []
        for r in reads:
            if r in writes:
                continue
            self.readers.setdefault(r, []).append(tok)

    def op(self, e, fn, reads=(), writes=()):
        waits = self._deps(e, reads, writes)
        self.cnt[e] += 1
        tok = (e, self.sem[e], self.cnt[e], False)
        self.ops[e].append((waits, fn, self.sem[e], 1))
        self._commit(tok, reads, writes)
        self.nops += 1

    def dma(self, q, out, in_, reads=(), writes=(), **kw):
        waits = self._deps(q, reads, writes, is_dma=True)
        slot = self.dnext[q]
        self.dnext[q] = (slot + 1) % self.NSLOT
        sem = self.dsem[q][slot]
        prev = self.dcnt[q][slot]
        if prev > 0 and self.waited[q].get(id(sem), 0) < prev:
            self.waited[q][id(sem)] = prev
            waits.append((sem, prev))
        self.dcnt[q][slot] = prev + 16
        tok = (q, sem, prev + 16, True)
        self.ops[q].append((waits, lambda eng: eng.dma_start(out=out, in_=in_, **kw), sem, 16))
        self._commit(tok, reads, writes)
        self.nops += 1

    def _all_tokens(self):
        fin = []
        for e in self.eng:
            if self.cnt[e] > 0:
                fin.append((self.sem[e], self.cnt[e]))
        for q in self.dsem:
            for s, c in zip(self.dsem[q], self.dcnt[q]):
                if c > 0:
                    fin.append((s, c))
        return fin

    def barrier(self):
        fin = self._all_tokens()
        for e in self.eng:
            wd = self.waited[e]
            for (s, v) in fin:
                if s is self.sem[e]:
                    continue
                if wd.get(id(s), 0) >= v:
                    continue
                wd[id(s)] = v
                self.pending[e].append((s, v))
        self.lastw = {}
        self.readers = {}

    def emit(self):
        nc = self.nc
        fin = self._all_tokens()
        ops = self.ops
        pending = self.pending
        with nc.Block() as block:
            def mk(ename):
                def body(eng):
                    for (waits, fn, sem, inc) in ops[ename]:
                        for (s, v) in waits:
                            eng.wait_ge(s, v)
                        fn(eng).then_inc(sem, inc)
                    for (s, v) in pending[ename]:
                        eng.wait_ge(s, v)
                    if ename == 'sp':
                        for (s, v) in fin:
                            eng.wait_ge(s, v)
                return body
            block.sync(mk('sp'))
            block.tensor(mk('pe'))
            block.scalar(mk('act'))
            block.vector(mk('dve'))
            block.gpsimd(mk('pool'))

    def mm(self, out, lhsT, rhs, start, stop, reads=(), writes=()):
        self.op('pe', lambda e: e.matmul(out, lhsT, rhs, start=start, stop=stop), reads, writes)

    def tr(self, out, in_, ident, reads=(), writes=()):
        self.op('pe', lambda e: e.transpose(out, in_, ident), reads, writes)

    def act(self, out, in_, func, reads=(), writes=(), **kw):
        self.op('act', lambda e: e.activation(out=out, in_=in_, func=func, **kw), reads, writes)

    def tt(self, eng, out, in0, in1, op, reads=(), writes=()):
        self.op(eng, lambda e: e.tensor_tensor(out=out, in0=in0, in1=in1, op=op), reads, writes)

    def ts(self, eng, out, in0, s1, s2, op0, op1=None, reads=(), writes=()):
        if op1 is None:
            self.op(eng, lambda e: e.tensor_scalar(out=out, in0=in0, scalar1=s1, scalar2=None, op0=op0),
                    reads, writes)
        else:
            self.op(eng, lambda e: e.tensor_scalar(out=out, in0=in0, scalar1=s1, scalar2=s2, op0=op0, op1=op1),
                    reads, writes)

    def stt(self, eng, out, in0, scalar, in1, op0, op1, reads=(), writes=()):
        self.op(eng, lambda e: e.scalar_tensor_tensor(out=out, in0=in0, scalar=scalar, in1=in1, op0=op0, op1=op1),
                reads, writes)

    def cp(self, eng, out, in_, reads=(), writes=()):
        if eng == 'act':
            self.op('act', lambda e: e.copy(out=out, in_=in_), reads, writes)
        else:
            self.op(eng, lambda e: e.tensor_copy(out=out, in_=in_), reads, writes)

    def memset(self, eng, ap, val, writes=()):
        self.op(eng, lambda e: e.memset(ap, val), (), writes)

    def recip(self, out, in_, reads=(), writes=()):
        self.op('dve', lambda e: e.reciprocal(out=out, in_=in_), reads, writes)


class Rot:
    def __init__(self, tiles, name):
        self.tiles = tiles
        self.name = name
        self.i = 0

    def next(self):
        k = self.i % len(self.tiles)
        self.i += 1
        return self.tiles[k], (self.name, k)


C_IDENT = 0
C_MPREV = 128
C_MNEXT = 256
C_ROPEP = 384
C_POS = 512
C_FIDX = C_POS + 0
C_HGRF = C_FIDX + 1
C_HGRR = C_HGRF + 512
C_HGTF = C_HGRR + 512
C_HGTR = C_HGTF + 64
C_HGM4 = C_HGTR + 64
C_GM = C_HGM4 + 256
C_MM = C_GM + 8
C_IM = C_MM + 128
C_MF = C_IM + 1024
C_MR = C_MF + 128
C_NCOL = C_MR + 128


def make_consts():
    c = np.zeros((128, C_NCOL), np.float32)
    c[:, C_IDENT:C_IDENT + 128] = np.eye(128, dtype=np.float32)
    kk = np.arange(128)[:, None]
    qq = np.arange(128)[None, :]
    c[:, C_MPREV:C_MPREV + 128] = (qq <= kk)
    c[:, C_MNEXT:C_MNEXT + 128] = (kk <= qq)
    pm = np.zeros((128, 128), np.float32)
    for base in (0, 64):
        for f in range(32):
            pm[base + f + 32, base + f] = -1.0
            pm[base + f, base + f + 32] = 1.0
    c[:, C_ROPEP:C_ROPEP + 128] = pm
    t = np.arange(2048)
    c[:, C_FIDX] = np.arange(128) % 32
    tt = np.arange(512)
    c[:, C_HGRF:C_HGRF + 512] = (tt % 64 != 0)[None, :]
    c[:, C_HGRR:C_HGRR + 512] = (tt % 64 != 63)[None, :]
    s = (np.arange(128) % 64)[:, None]
    t64 = np.arange(64)[None, :]
    c[:, C_HGTF:C_HGTF + 64] = (s <= t64)
    c[:, C_HGTR:C_HGTR + 64] = (s >= t64)
    pp_ = np.arange(128)[:, None]
    lo = (pp_ < 64)
    hi = (pp_ >= 64)
    c[:, C_HGM4 + 0:C_HGM4 + 64] = (s <= t64) * lo
    c[:, C_HGM4 + 64:C_HGM4 + 128] = (s <= t64) * hi
    c[:, C_HGM4 + 128:C_HGM4 + 192] = (s >= t64) * lo
    c[:, C_HGM4 + 192:C_HGM4 + 256] = (s >= t64) * hi
    p = np.arange(128)[:, None]
    q = np.arange(128)[None, :]
    c[:, C_GM:C_GM + 8] = (p // 16 == np.arange(8)[None, :])
    c[:, C_MM:C_MM + 128] = (q % 16 == p % 16)
    im = (np.arange(128)[None, :] // 16 == np.arange(8)[:, None]).astype(np.float32)
    c[:, C_IM:C_IM + 1024] = im.reshape(1, 1024)
    c[:, C_MF:C_MF + 128] = (q // 16 >= p // 16)
    c[:, C_MR:C_MR + 128] = (p // 16 >= q // 16)
    return c


def make_pos():
    t = np.arange(2048)
    c = np.zeros((128, 2048), np.float32)
    c[:64] = (t // 64)[None, :]
    c[64:] = (t % 64)[None, :]
    return c


def token_tiles(nb, b, with_ctx=True):
    tl = []
    if with_ctx:
        tl.append((0, LC, nb))
    for i in range(LL // 512):
        tl.append((LC + 512 * i, 512, b))
    return tl


def build(nb, layers=(0, 1, 2, 3), final_out=True, debug=False):
    nc = bass.Bass("TRN2", target_bir_lowering=False)
    NR = nb + 1 + ((nb + 1) % 2)
    _uid = [0]

    def sbt(name, shape, dt):
        _uid[0] += 1
        return nc.sbuf_tensor("%s_u%d" % (name, _uid[0]), shape, dt)

    def din(name, shape, dt=F32):
        return nc.dram_tensor(name, list(shape), dt, kind="ExternalInput").ap()

    def dscr(name, shape, dt):
        return nc.dram_tensor(name, list(shape), dt, kind="Internal").ap()

    x_in = din("x", [nb, LL, D])
    c_in = din("c", [nb, D])
    ctx_in = din("ctx", [nb, LC, D])
    cctx_in = din("c_ctx", [D])
    ada_w = din("ada_w", [DEPTH, D, 3 * D])
    ada_b = din("ada_b", [DEPTH, 3 * D])
    ln_g = din("ln_g", [DEPTH, D])
    ln_b = din("ln_b", [DEPTH, D])
    w_out = din("w_out", [DEPTH, E, D])
    attn_w_in = din("attn_w_in", [2, D, 5120])
    attn_sink = din("attn_sink", [2, 16])
    s5_w_in = din("s5_w_in", [1, D, 4096])
    s5_lam_re = din("s5_lam_re", [1, 2, 128, 64])
    s5_lam_im = din("s5_lam_im", [1, 2, 128, 64])
    s5_log_step = din("s5_log_step", [1, 2, 128])
    s5_b_re = din("s5_b_re", [1, 2, 128, 64, 16])
    s5_b_im = din("s5_b_im", [1, 2, 128, 64, 16])
    s5_c_re = din("s5_c_re", [1, 2, 128, 16, 64])
    s5_c_im = din("s5_c_im", [1, 2, 128, 16, 64])
    s5_d = din("s5_d", [1, E])
    s5_glu_w = din("s5_glu_w", [1, E, E])
    s5_glu_b = din("s5_glu_b", [1, E])
    hg_w_in = din("hg_w_in", [1, D, 10240])
    hg_lb = din("hg_lb", [DEPTH, E])
    hg_norm_g = din("hg_norm_g", [1, E])
    consts = din("consts", [128, C_NCOL])
    posc = din("posc", [128, LL])
    out = nc.dram_tensor("out", [nb, LL, D], F32, kind="ExternalOutput").ap()
    dbg = nc.dram_tensor("dbg", [nb, D, LT], F32, kind="ExternalOutput").ap() if debug else None
    dbgy = nc.dram_tensor("dbgy", [E, LT], BF16, kind="ExternalOutput").ap() if debug else None
    dbgz = nc.dram_tensor("dbgz", [E, LT], BF16, kind="ExternalOutput").ap() if debug else None
    dbgws = nc.dram_tensor("dbgws", [2, 2, 128, 128, 64], BF16, kind="ExternalOutput").ap() if debug else None
    dbgwn = nc.dram_tensor("dbgwn", [2, 2, 128, 64, 128], BF16, kind="ExternalOutput").ap() if debug else None
    dbgwi = nc.dram_tensor("dbgwi", [128, 128, 128], BF16, kind="ExternalOutput").ap() if debug else None
    dbgg = nc.dram_tensor("dbgg", [E, LT], BF16, kind="ExternalOutput").ap() if debug else None

    xT_s = dscr("xT_s", [nb, D, LT], F32)
    wb_attn = dscr("wb_attn", [2, D, 5120], BF16)
    wb_s5 = dscr("wb_s5", [D, 4096], BF16)
    wb_glu = dscr("wb_glu", [E, E], BF16)
    wb_hg = dscr("wb_hg", [D, 10240], BF16)
    wb_out = dscr("wb_out", [DEPTH, E, D], BF16)
    qT_s = dscr("qT_s", [E, LT], BF16)
    kT_s = dscr("kT_s", [512, LT], BF16)
    v_s = dscr("v_s", [LT, 512], BF16)
    zT_s = dscr("zT_s", [E, LT], BF16)
    yT_s = dscr("yT_s", [E, LT], BF16)
    uT_s = dscr("uT_s", [E, LT], BF16)
    rope_s = dscr("rope_s", [2, 128, LL], F32)
    sel_s = dscr("sel_s", [128, 8192], BF16)
    gT_s = dscr("gT_s", [E, LT], BF16)
    WS_s = dscr("WS_s", [2, 2, 128, 128, 64], BF16)
    Winter_s = dscr("Winter_s", [2, 2, 128, 64, 128], BF16)
    Wintra_s = dscr("Wintra_s", [128, 128, 128], BF16)
    a8_s = dscr("a8_s", [2, 2, 64, 128], F32)

    with contextlib.ExitStack() as gs:
        P = Prog(nc, gs)

        def galloc(name, shape, dt):
            return gs.enter_context(sbt(name, list(shape), dt))

        def finish():
            if debug:
                for b in range(nb):
                    P.dma('sp', dbg[b], xT_s[b], writes=[('dbg', b)])
                P.dma('sp', dbgy, yT_s, writes=['dbgy'])
                P.dma('sp', dbgz, zT_s, writes=['dbgz'])
                if 1 in [l_ % 3 for l_ in layers]:
                    for d_ in range(2):
                        for c_ in range(2):
                            P.dma('sp', dbgws[d_, c_], WS_s[d_, c_], writes=[('dbgws', d_, c_)])
                            P.dma('sp', dbgwn[d_, c_], Winter_s[d_, c_], writes=[('dbgwn', d_, c_)])
                    P.dma('sp', dbgwi, Wintra_s, writes=['dbgwi'])
                    P.dma('sp', dbgg, gT_s, writes=['dbgg'])
            P.emit()
            return nc

        cst = galloc("cst", [128, C_NCOL], F32)
        ident32 = cst[:, C_IDENT:C_IDENT + 128]
        ones32 = galloc("ones32", [128, 128], F32)
        onesbf = galloc("onesbf", [128, 128], BF16)
        identbf = galloc("identbf", [128, 128], BF16)
        ropeP = galloc("ropeP", [128, 128], BF16)
        mprev = galloc("mprev", [128, 128], BF16)
        mnext = galloc("mnext", [128, 128], BF16)
        modT = galloc("modT", [128, DEPTH, 24, NR], F32)
        lng = galloc("lng", [128, DEPTH, 8], F32)
        lnb = galloc("lnb", [128, DEPTH, 8], F32)
        esink = galloc("esink", [128, 2, 16], F32)

        psum = [gs.enter_context(nc.psum_tensor("ps%d" % i, [128, 512], F32)) for i in range(8)]

        P.dma('sp', cst[:], consts, writes=['cst'])
        P.memset('dve', ones32[:], 1.0, writes=['ones32'])
        P.memset('dve', onesbf[:], 1.0, writes=['onesbf'])
        P.cp('dve', identbf[:], ident32, reads=['cst'], writes=['identbf'])
        P.cp('dve', ropeP[:], cst[:, C_ROPEP:C_ROPEP + 128], reads=['cst'], writes=['ropeP'])
        P.cp('dve', mprev[:], cst[:, C_MPREV:C_MPREV + 128], reads=['cst'], writes=['mprev'])
        P.cp('dve', mnext[:], cst[:, C_MNEXT:C_MNEXT + 128], reads=['cst'], writes=['mnext'])
        P.dma('sp', lng[:], ln_g.rearrange("l (c p) -> p l c", p=128), writes=['lng'],
              allow_slow_non_contiguous=True)
        P.dma('sp', lnb[:], ln_b.rearrange("l (c p) -> p l c", p=128), writes=['lnb'],
              allow_slow_non_contiguous=True)
        P.dma('sp', esink[:].rearrange("p a h -> p (a h)"),
              attn_sink.rearrange("a h -> (a h)").partition_broadcast(128), writes=['esink'])
        P.act(esink[:], esink[:], AF.Exp, reads=['esink'], writes=['esink'])

        def conv_weights(pairs):
            with contextlib.ExitStack() as ph:
                n = 3
                srcs = Rot([ph.enter_context(sbt("cw_s%d" % i, [128, 2048], F32)) for i in range(n)], 'cw_s')
                dsts = Rot([ph.enter_context(sbt("cw_d%d" % i, [128, 2048], BF16)) for i in range(n)], 'cw_d')
                engs = ['dve', 'pool', 'act']
                k = 0
                for (src, dst, dkey) in pairs:
                    R, Cc = src.shape
                    for r0 in range(0, R, 128):
                        for c0 in range(0, Cc, 2048):
                            cw = min(2048, Cc - c0)
                            st, sk = srcs.next()
                            dt_, dk = dsts.next()
                            P.dma('sp', st[:, :cw], src[r0:r0 + 128, c0:c0 + cw], writes=[sk])
                            P.cp(engs[k % 3], dt_[:, :cw], st[:, :cw], reads=[sk], writes=[dk])
                            P.dma('pool', dst[r0:r0 + 128, c0:c0 + cw], dt_[:, :cw], reads=[dk], writes=[('wbw', k)])
                            k += 1
            P.barrier()

        pairs = []
        used_kinds = set(l % 3 for l in layers)
        for l in layers:
            pairs.append((w_out[l], wb_out[l], 'wb'))
            if l % 3 == 0:
                pairs.append((attn_w_in[l // 3], wb_attn[l // 3], 'wb'))
        if 1 in used_kinds:
            pairs.append((s5_w_in[0], wb_s5, 'wb'))
            pairs.append((s5_glu_w[0], wb_glu, 'wb'))
        if 2 in used_kinds:
            pairs.append((hg_w_in[0], wb_hg, 'wb'))
        if STOP[0] == 'init':
            return finish()
        if DBG.get('dmacast', False):
            kcw = 0
            for (src, dst, dkey) in pairs:
                R, Cc = src.shape
                for r0 in range(0, R, 128):
                    P.dma('pool', dst[r0:r0 + 128, :], src[r0:r0 + 128, :], writes=[('wbw', kcw)])
                    kcw += 1
        else:
            conv_weights(pairs)
        if STOP[0] == 'conv':
            return finish()

        def to_feature_major():
            with contextlib.ExitStack() as ph:
                xin = Rot([ph.enter_context(sbt("tf_i%d" % i, [128, D], F32)) for i in range(3)], 'tf_i')
                xst = Rot([ph.enter_context(sbt("tf_o%d" % i, [128, 8, 512], F32)) for i in range(2)], 'tf_o')
                pk = 0
                for b in range(nb):
                    for (t0, nt, _r) in token_tiles(nb, b):
                        so, sok = xst.next()
                        for tb in range(nt // 128):
                            xi, xik = xin.next()
                            if t0 < LC:
                                src = ctx_in[b, tb * 128:(tb + 1) * 128, :]
                            else:
                                src = x_in[b, t0 - LC + tb * 128:t0 - LC + (tb + 1) * 128, :]
                            P.dma('sp', xi[:], src, writes=[xik])
                            for half in range(2):
                                pt = psum[pk % 8]
                                pkey = ('ps', pk % 8)
                                pk += 1
                                for cc in range(4):
                                    c = half * 4 + cc
                                    P.tr(pt[:, cc * 128:(cc + 1) * 128], xi[:, c * 128:(c + 1) * 128], ident32,
                                         reads=[xik, 'cst'], writes=[pkey])
                                P.cp('act' if half else 'dve',
                                     so[:, half * 4:(half + 1) * 4, tb * 128:(tb + 1) * 128],
                                     pt[:].rearrange("p (c t) -> p c t", c=4), reads=[pkey], writes=[sok])
                        P.dma('pool', xT_s[b].rearrange("(c p) t -> p c t", p=128)[:, :, t0:t0 + nt],
                              so[:, :, :nt], reads=[sok], writes=[('xT', b, t0)])
            P.barrier()

        to_feature_major()
        if STOP[0] == 'tfm':
            return finish()

        def compute_mod():
            with contextlib.ExitStack() as ph:
                cT = ph.enter_context(sbt("cT", [128, 8, NR], F32))
                P.memset('dve', cT[:], 0.0, writes=['cT'])
                adab = ph.enter_context(sbt("adab", [128, DEPTH, 24], F32))
                wts = Rot([ph.enter_context(sbt("adaw%d" % i, [128, 8, 1024], F32)) for i in range(2)], 'adaw')
                for r in range(nb):
                    P.dma('sp', cT[:, :, r:r + 1], c_in[r].rearrange("(c p o) -> p c o", p=128, o=1),
                          writes=['cT'], allow_slow_non_contiguous=True)
                P.dma('sp', cT[:, :, nb:nb + 1], cctx_in.rearrange("(c p o) -> p c o", p=128, o=1),
                      writes=['cT'], allow_slow_non_contiguous=True)
                P.dma('sp', adab[:], ada_b.rearrange("l (o p) -> p l o", p=128), writes=['adab'],
                      allow_slow_non_contiguous=True)
                P.act(cT[:], cT[:], AF.Silu, reads=['cT'], writes=['cT'])
                for l in layers:
                    for part in range(3):
                        wt, wk = wts.next()
                        P.dma('sp', wt[:], ada_w[l].rearrange("(c p) n -> p c n", p=128)[:, :, part * 1024:(part + 1) * 1024],
                              writes=[wk])
                        pt = psum[part]
                        pkey = ('ps', part)
                        for oc in range(8):
                            for kc in range(8):
                                P.mm(pt[:, oc * NR:(oc + 1) * NR], wt[:, kc, oc * 128:(oc + 1) * 128],
                                     cT[:, kc, :], kc == 0, kc == 7, reads=[wk, 'cT'], writes=[pkey])
                        P.tt('dve', modT[:, l, part * 8:(part + 1) * 8, :],
                             pt[:, :8 * NR].rearrange("p (o r) -> p o r", o=8),
                             adab[:, l, part * 8:(part + 1) * 8].unsqueeze(2).to_broadcast([128, 8, NR]),
                             ALU.add, reads=[pkey, 'adab'], writes=['modT'])
                    P.ts('dve', modT[:, l, 8:16, :], modT[:, l, 8:16, :], 1.0, None, ALU.add,
                         reads=['modT'], writes=['modT'])
                    P.ts('dve', modT[:, l, 16:24, :], modT[:, l, 16:24, :], 1.0 / ALPHA, None, ALU.mult,
                         reads=['modT'], writes=['modT'])
            P.barrier()

        compute_mod()
        if STOP[0] == 'mod':
            return finish()


        def rope_tables():
            with contextlib.ExitStack() as ph:
                ropecos = ph.enter_context(sbt("ropecos", [128, LL], F32))
                ropesin = ph.enter_context(sbt("ropesin", [128, LL], F32))
                invf = ph.enter_context(sbt("invf", [128, 1], F32))
                ang = ph.enter_context(sbt("ang", [128, LL], F32))
                a2 = ph.enter_context(sbt("a2", [128, LL], F32))
                ki = ph.enter_context(sbt("ki", [128, LL], I32))
                kf = ph.enter_context(sbt("kf", [128, LL], F32))
                P.act(invf[:], cst[:, C_FIDX:C_FIDX + 1], AF.Exp, reads=['cst'], writes=['invf'],
                      scale=-math.log(10000.0) / 32.0)
                P.dma('sp', a2[:], posc, writes=['a2'])
                P.ts('dve', ang[:], a2[:], invf[:, 0:1], None, ALU.mult,
                     reads=['a2', 'invf'], writes=['ang'])
                for (dst, shift, dkey) in ((ropesin, 0.0, 'ropesin'), (ropecos, math.pi / 2, 'ropecos')):
                    P.ts('dve', a2[:], ang[:], shift, 1.0 / TWO_PI, ALU.add, ALU.mult, reads=['ang'], writes=['a2'])
                    P.cp('dve', ki[:], a2[:], reads=['a2'], writes=['ki'])
                    P.cp('dve', kf[:], ki[:], reads=['ki'], writes=['kf'])
                    P.stt('dve', a2[:], kf[:], -TWO_PI, ang[:], ALU.mult, ALU.add, reads=['kf', 'ang'], writes=['a2'])
                    P.ts('dve', a2[:], a2[:], shift, 3.14159, ALU.add, ALU.min, reads=['a2'], writes=['a2'])
                    P.ts('dve', a2[:], a2[:], -3.14159, None, ALU.max, reads=['a2'], writes=['a2'])
                    P.act(dst[:], a2[:], AF.Sin, reads=['a2'], writes=[dkey])
                    P.dma('pool', rope_s[0 if dkey == 'ropesin' else 1], dst[:], reads=[dkey], writes=[('rope_s', dkey)])
            P.barrier()

        if 0 in used_kinds:
            rope_tables()
        if STOP[0] == 'rope':
            return finish()

        def load_hT(ph, l, b, tiles):
            hT = ph.enter_context(sbt("hT", [128, 8, LT], BF16))
            with contextlib.ExitStack() as ph2:
                xin = Rot([ph2.enter_context(sbt("hx%d" % i, [128, 8, 512], F32)) for i in range(2)], 'hx')
                for ti, (t0, nt, r) in enumerate(tiles):
                    xt, xk = xin.next()
                    P.dma('sp', xt[:, :, :nt], xT_s[b].rearrange("(c p) t -> p c t", p=128)[:, :, t0:t0 + nt],
                          reads=[('xT', b, t0)], writes=[xk])
                    for c in range(8):
                        if c % 2 == 0:
                            P.act(hT[:, c, t0:t0 + nt], xt[:, c, :nt], AF.Identity, reads=[xk, 'modT'],
                                  writes=[('hT', t0, c)], scale=modT[:, l, 8 + c, r:r + 1], bias=modT[:, l, c, r:r + 1])
                        else:
                            P.ts('dve', hT[:, c, t0:t0 + nt], xt[:, c, :nt], modT[:, l, 8 + c, r:r + 1],
                                 modT[:, l, c, r:r + 1], ALU.mult, ALU.add, reads=[xk, 'modT'], writes=[('hT', t0, c)])
                P.barrier()
            return hT

        pcount = [0]

        def next_ps():
            k = pcount[0] % 8
            pcount[0] += 1
            return psum[k], ('ps', k)

        def phase_C(l, b, last):
            tiles = token_tiles(nb, b, with_ctx=not last)
            with contextlib.ExitStack() as ph:
                wo = ph.enter_context(sbt("wo", [128, 16, D], BF16))
                P.dma('sp', wo[:], wb_out[l].rearrange("(k p) n -> p k n", p=128), writes=['wo'])
                yts = Rot([ph.enter_context(sbt("c_y%d" % i, [128, 16, 512], BF16)) for i in range(2)], 'c_y')
                zts = Rot([ph.enter_context(sbt("c_z%d" % i, [128, 16, 512], BF16)) for i in range(1 if last else 2)], 'c_z')
                xts = Rot([ph.enter_context(sbt("c_x%d" % i, [128, 8, 512], F32)) for i in range(2)], 'c_x')
                tps = Rot([ph.enter_context(sbt("c_t%d" % i, [128, 8, 512], F32)) for i in range(2)], 'c_t')
                sqs = Rot([ph.enter_context(sbt("c_q%d" % i, [128, 8, 512], F32)) for i in range(1)], 'c_q')
                sts = Rot([ph.enter_context(sbt("c_s%d" % i, [128, 3, 512], F32)) for i in range(1)], 'c_s')
                ots = Rot([ph.enter_context(sbt("c_o%d" % i, [128, D], F32)) for i in range(2)], 'c_o') if last else None

                def front(tile):
                    t0, nt, r = tile
                    yt, yk = yts.next()
                    zt, zk = zts.next()
                    xt, xk = xts.next()
                    tp, tk = tps.next()
                    P.dma('sp', yt[:, :, :nt], yT_s.rearrange("(k p) t -> p k t", p=128)[:, :, t0:t0 + nt],
                          writes=[yk, (yk, 0), (yk, 1)])
                    P.dma('sp', zt[:, :, :nt], zT_s.rearrange("(k p) t -> p k t", p=128)[:, :, t0:t0 + nt],
                          writes=[zk])
                    P.dma('sp', xt[:, :, :nt], xT_s[b].rearrange("(c p) t -> p c t", p=128)[:, :, t0:t0 + nt],
                          reads=[('xT', b, t0)], writes=[xk])
                    P.tt('dve', yt[:, 0:10, :nt], yt[:, 0:10, :nt], zt[:, 0:10, :nt], ALU.mult, reads=[yk, zk], writes=[(yk, 0)])
                    P.tt('pool', yt[:, 10:16, :nt], yt[:, 10:16, :nt], zt[:, 10:16, :nt], ALU.mult, reads=[yk, zk], writes=[(yk, 1)])
                    for oc in range(8):
                        pt, pkey = next_ps()
                        for kc in range(16):
                            P.mm(pt[:, :nt], wo[:, kc, oc * 128:(oc + 1) * 128], yt[:, kc, :nt], kc == 0, kc == 15,
                                 reads=['wo', (yk, 0 if kc < 10 else 1)], writes=[pkey])
                        P.stt('dve', tp[:, oc, :nt], pt[:, :nt], modT[:, l, 16 + oc, r:r + 1], xt[:, oc, :nt],
                              ALU.mult, ALU.add, reads=[pkey, xk, 'modT'], writes=[(tk, oc)])
                    return (tile, xt, xk, tp, tk, yk)

                def tail(fr):
                    (t0, nt, r), xt, xk, tp, tk, yk = fr
                    sq, qk = sqs.next()
                    st, sk = sts.next()
                    tks = [(tk, oc) for oc in range(8)]
                    P.act(sq[:, :, :nt], tp[:, :, :nt], AF.Square, reads=tks, writes=[qk])
                    pm, pmk = next_ps()
                    for oc in range(8):
                        P.mm(pm[:, :nt], ones32[:], tp[:, oc, :nt], oc == 0, oc == 7, reads=[(tk, oc), 'ones32'], writes=[pmk])
                    pq, pqk = next_ps()
                    for oc in range(8):
                        P.mm(pq[:, :nt], ones32[:], sq[:, oc, :nt], oc == 0, oc == 7, reads=[qk, 'ones32'], writes=[pqk])
                    mean = st[:, 0, :nt]
                    msq = st[:, 1, :nt]
                    rstd = st[:, 2, :nt]
                    P.act(mean, pm[:, :nt], AF.Copy, reads=[pmk], writes=[sk], scale=1.0 / D)
                    P.tt('dve', msq, mean, mean, ALU.mult, reads=[sk], writes=[sk])
                    P.stt('dve', msq, pq[:, :nt], 1.0 / D, msq, ALU.mult, ALU.subtract, reads=[pqk, sk], writes=[sk])
                    P.act(msq, msq, AF.Ln, reads=[sk], writes=[sk], bias=EPS / (ALPHA * ALPHA))
                    P.act(rstd, msq, AF.Exp, reads=[sk], writes=[sk], scale=-0.5)
                    for (eng, c0, c1) in (('dve', 0, 5), ('pool', 5, 8)):
                        kk = [(tk, oc) for oc in range(c0, c1)]
                        P.tt(eng, tp[:, c0:c1, :nt], tp[:, c0:c1, :nt], mean.unsqueeze(1).to_broadcast([128, c1 - c0, nt]),
                             ALU.subtract, reads=kk + [sk], writes=kk)
                        P.tt(eng, tp[:, c0:c1, :nt], tp[:, c0:c1, :nt], rstd.unsqueeze(1).to_broadcast([128, c1 - c0, nt]),
                             ALU.mult, reads=kk + [sk], writes=kk)
                    for oc in range(8):
                        P.act(xt[:, oc, :nt], tp[:, oc, :nt], AF.Identity, reads=[(tk, oc), 'lng', 'lnb'], writes=[xk],
                              scale=lng[:, l, oc:oc + 1], bias=lnb[:, l, oc:oc + 1])
                    if not (last and final_out):
                        P.dma('pool', xT_s[b].rearrange("(c p) t -> p c t", p=128)[:, :, t0:t0 + nt], xt[:, :, :nt],
                              reads=[xk], writes=[('xT', b, t0)])
                    if last and final_out:
                        for tb in range(nt // 128):
                            ot, ok = ots.next()
                            for half in range(2):
                                pt, pkey = next_ps()
                                for cc in range(4):
                                    c = half * 4 + cc
                                    P.tr(pt[:, cc * 128:(cc + 1) * 128], xt[:, c, tb * 128:(tb + 1) * 128], ident32,
                                         reads=[xk, 'cst'], writes=[pkey])
                                P.cp('act' if half else 'dve', ot[:, half * 512:(half + 1) * 512], pt[:],
                                     reads=[pkey], writes=[ok])
                            tok0 = t0 - LC + tb * 128
                            P.dma('pool', out[b, tok0:tok0 + 128, :], ot[:], reads=[ok], writes=[('out', b, tok0)])

                fr_prev = front(tiles[0])
                for ti in range(len(tiles)):
                    fr_next = front(tiles[ti + 1]) if ti + 1 < len(tiles) else None
                    tail(fr_prev)
                    fr_prev = fr_next
            P.barrier()

        def attn_phase_A(l, b):
            j = l // 3
            tiles = token_tiles(nb, b)
            W = wb_attn[j].rearrange("(c p) n -> p c n", p=128)
            with contextlib.ExitStack() as ph:
                hT = load_hT(ph, l, b, tiles)
                ropecos = ph.enter_context(sbt("ropecos", [128, LL], F32))
                ropesin = ph.enter_context(sbt("ropesin", [128, LL], F32))
                P.dma('sp', ropesin[:], rope_s[0], writes=['ropesin'])
                P.dma('sp', ropecos[:], rope_s[1], writes=['ropecos'])
                wts = Rot([ph.enter_context(sbt("a_w%d" % i, [128, 8, 512], BF16)) for i in range(2)], 'a_w')
                stg = Rot([ph.enter_context(sbt("a_s%d" % i, [128, 4, 512], BF16)) for i in range(3)], 'a_s')
                qbs = Rot([ph.enter_context(sbt("a_qb%d" % i, [128, 512], BF16)) for i in range(3)], 'a_qb')
                t1s = Rot([ph.enter_context(sbt("a_t1%d" % i, [128, 512], F32)) for i in range(2)], 'a_t1')
                t2s = Rot([ph.enter_context(sbt("a_t2%d" % i, [128, 512], F32)) for i in range(2)], 'a_t2')
                q32s = Rot([ph.enter_context(sbt("a_q32%d" % i, [128, 512], F32)) for i in range(3)], 'a_q32')
                hkeys = [('hT', t0) for (t0, _n, _r) in tiles]
                for cg in DBG.get('cgs', range(10)):
                    wt, wk = wts.next()
                    P.dma('sp', wt[:], W[:, :, cg * 512:(cg + 1) * 512], writes=[wk])
                    if cg == 5:
                        for (t0, nt, r) in tiles:
                            for tb in range(nt // 128):
                                tok0 = t0 + tb * 128
                                pt, pkey = next_ps()
                                for c in range(8):
                                    P.mm(pt[:], hT[:, c, tok0:tok0 + 128], wt[:, c, :], c == 0, c == 7,
                                         reads=[wk, ('hT', t0, c)], writes=[pkey])
                                sg, sgk = stg.next()
                                P.cp('act' if tb % 2 else 'dve', sg[:, 0, :], pt[:], reads=[pkey], writes=[sgk])
                                P.dma('pool', v_s[tok0:tok0 + 128, :], sg[:, 0, :], reads=[sgk], writes=[('v_s', tok0)])
                        continue
                    if cg < 4:
                        dst = qT_s[cg * 512:(cg + 1) * 512, :]
                        dk = 'qT_s'
                    elif cg == 4:
                        dst = kT_s
                        dk = 'kT_s'
                    else:
                        dst = zT_s[(cg - 6) * 512:(cg - 5) * 512, :]
                        dk = 'zT_s'
                    items = []
                    for (t0, nt, r) in tiles:
                        sg, sgk = stg.next()
                        for hh in range(4):
                            items.append((t0, nt, hh, sg, sgk))

                    def stage1(it):
                        t0, nt, hh, sg, sgk = it
                        pt, pkey = next_ps()
                        for c in range(8):
                            P.mm(pt[:, :nt], wt[:, c, hh * 128:(hh + 1) * 128], hT[:, c, t0:t0 + nt], c == 0, c == 7,
                                 reads=[wk, ('hT', t0, c)], writes=[pkey])
                        if cg >= 6:
                            P.act(sg[:, hh, :nt], pt[:, :nt], AF.Silu, reads=[pkey], writes=[sgk])
                            return None
                        if t0 < LC:
                            P.cp('act', sg[:, hh, :nt], pt[:, :nt], reads=[pkey], writes=[sgk])
                            return None
                        qb, qbk = qbs.next()
                        q32, q32k = q32s.next()
                        P.cp('act', q32[:, :nt], pt[:, :nt], reads=[pkey], writes=[q32k])
                        P.cp('act', qb[:, :nt], pt[:, :nt], reads=[pkey], writes=[qbk])
                        return (qb, qbk, q32, q32k)

                    def stage2(it, st):
                        t0, nt, hh, sg, sgk = it
                        if st is not None:
                            qb, qbk, q32, q32k = st
                            lt0 = t0 - LC
                            t1, t1k = t1s.next()
                            t2, t2k = t2s.next()
                            p2, p2k = next_ps()
                            P.mm(p2[:, :nt], ropeP[:], qb[:, :nt], True, True, reads=['ropeP', qbk], writes=[p2k])
                            P.tt('pool', t1[:, :nt], q32[:, :nt], ropecos[:, lt0:lt0 + nt], ALU.mult,
                                 reads=[q32k, 'ropecos'], writes=[t1k])
                            P.tt('dve', t2[:, :nt], p2[:, :nt], ropesin[:, lt0:lt0 + nt], ALU.mult,
                                 reads=[p2k, 'ropesin'], writes=[t2k])
                            P.tt('dve', sg[:, hh, :nt], t1[:, :nt], t2[:, :nt], ALU.add,
                                 reads=[t1k, t2k], writes=[sgk])
                        if hh == 3:
                            P.dma('pool', dst.rearrange("(h p) t -> p h t", p=128)[:, :, t0:t0 + nt], sg[:, :, :nt],
                                  reads=[sgk], writes=[(dk, cg, t0)])

                    st_prev = stage1(items[0])
                    for i_ in range(len(items)):
                        st_next = stage1(items[i_ + 1]) if i_ + 1 < len(items) else None
                        stage2(items[i_], st_prev)
                        st_prev = st_next
            P.barrier()

        def attn_phase_B(l, b, last):
            j = l // 3
            scale = 128 ** -0.5
            with contextlib.ExitStack() as ph:
                kT = ph.enter_context(sbt("b_k", [128, 4, LT], BF16))
                V = ph.enter_context(sbt("b_v", [128, LT // 128, 512], BF16))
                qgs = Rot([ph.enter_context(sbt("b_q%d" % i, [128, 4, LT], BF16)) for i in range(2)], 'b_q')
                ysts = Rot([ph.enter_context(sbt("b_y%d" % i, [128, 4, LT], BF16)) for i in range(2)], 'b_y')
                pts = Rot([ph.enter_context(sbt("b_p%d" % i, [128, 4, 128], BF16)) for i in range(10)], 'b_p')
                dns = Rot([ph.enter_context(sbt("b_d%d" % i, [128, 4, 128], F32)) for i in range(2)], 'b_d')
                P.dma('sp', kT[:], kT_s.rearrange("(h p) t -> p h t", p=128), writes=['b_k'])
                P.dma('sp', V[:], v_s.rearrange("(n p) c -> p n c", p=128), writes=['b_v'])
                qblocks = list(range(2, 18)) if last else list(range(18))
                for h in range(4):
                    qg, qk = qgs.next()
                    ys, yk = ysts.next()
                    P.dma('sp', qg[:], qT_s[h * 512:(h + 1) * 512, :].rearrange("(g p) t -> p g t", p=128),
                          writes=[qk])
                    if last:
                        P.memset('pool', ys[:, :, 0:LC], 0.0, writes=[yk])
                    steps = []
                    for qb in qblocks:
                        kbs = [(0, None), (1, None)]
                        if qb >= 2:
                            if qb - 1 >= 2:
                                kbs.append((qb - 1, mprev))
                            kbs.append((qb, None))
                            if qb + 1 <= 17:
                                kbs.append((qb + 1, mnext))
                        for i_, (kb, msk) in enumerate(kbs):
                            steps.append(dict(qb=qb, kb=kb, msk=msk, first=(i_ == 0), last=(i_ == len(kbs) - 1)))
                    acc = {}

                    def emit_S(st):
                        qb, kb, msk = st['qb'], st['kb'], st['msk']
                        ps_, psk = next_ps()
                        P.mm(ps_[:].rearrange("p (g t) -> p g t", g=4), kT[:, h, kb * 128:(kb + 1) * 128],
                             qg[:, :, qb * 128:(qb + 1) * 128], True, True, reads=['b_k', qk], writes=[psk])
                        pt_, ptk = pts.next()
                        P.act(pt_[:], ps_[:].rearrange("p (g t) -> p g t", g=4), AF.Exp, reads=[psk], writes=[ptk],
                              scale=scale)
                        if msk is not None:
                            P.tt('pool', pt_[:], pt_[:], msk[:].unsqueeze(1).to_broadcast([128, 4, 128]), ALU.mult,
                                 reads=[ptk, 'mprev', 'mnext'], writes=[ptk])
                        st['pt'] = (pt_, ptk)

                    def emit_PV(st):
                        qb, kb = st['qb'], st['kb']
                        pt_, ptk = st['pt']
                        if st['first']:
                            acc['po'] = next_ps()
                            acc['pd'] = next_ps()
                        po, pok = acc['po']
                        pd, pdk = acc['pd']
                        P.mm(po[:], V[:, kb, h * 128:(h + 1) * 128], pt_[:].rearrange("p g t -> p (g t)"),
                             st['first'], st['last'], reads=['b_v', ptk], writes=[pok])
                        P.mm(pd[:], onesbf[:], pt_[:].rearrange("p g t -> p (g t)"),
                             st['first'], st['last'], reads=['onesbf', ptk], writes=[pdk])
                        if st['last']:
                            dn, dnk = dns.next()
                            P.tt('dve', dn[:], pd[:].rearrange("p (g t) -> p g t", g=4),
                                 esink[:, j, h * 4:(h + 1) * 4].unsqueeze(2).to_broadcast([128, 4, 128]), ALU.add,
                                 reads=[pdk, 'esink'], writes=[dnk])
                            P.act(dn[:], dn[:], AF.Ln, reads=[dnk], writes=[dnk])
                            P.act(dn[:], dn[:], AF.Exp, reads=[dnk], writes=[dnk], scale=-1.0)
                            P.tt('dve', ys[:, :, qb * 128:(qb + 1) * 128], po[:].rearrange("p (g t) -> p g t", g=4), dn[:],
                                 ALU.mult, reads=[pok, dnk], writes=[yk])

                    LOOK = 2
                    for i_ in range(len(steps) + LOOK):
                        if i_ < len(steps):
                            emit_S(steps[i_])
                        if i_ - LOOK >= 0:
                            emit_PV(steps[i_ - LOOK])
                    P.dma('pool', yT_s[h * 512:(h + 1) * 512, :].rearrange("(g p) t -> p g t", p=128), ys[:],
                          reads=[yk], writes=[('yT_s', h)])
            P.barrier()


        hglb = galloc("hglb", [128, 16], F32)
        hgoml = galloc("hgoml", [128, 16], F32)
        hgng = galloc("hgng", [128, 16], F32)

        def hg_prepare(l):
            with contextlib.ExitStack() as ph:
                raw = ph.enter_context(sbt("hg_raw", [128, DEPTH, 16], F32))
                mx = ph.enter_context(sbt("hg_mx", [128, 16], F32))
                tot = ph.enter_context(sbt("hg_tot", [128, 16], F32))
                P.dma('sp', raw[:], hg_lb.rearrange("l (c p) -> p l c", p=128), writes=['hg_raw'],
                      allow_slow_non_contiguous=True)
                P.dma('sp', hgng[:], hg_norm_g[0].rearrange("(c p) -> p c", p=128), writes=['hgng'],
                      allow_slow_non_contiguous=True)
                P.tt('dve', mx[:], raw[:, 0, :], raw[:, 1, :], ALU.max, reads=['hg_raw'], writes=['hg_mx'])
                for i in range(2, DEPTH):
                    P.tt('dve', mx[:], mx[:], raw[:, i, :], ALU.max, reads=['hg_raw', 'hg_mx'], writes=['hg_mx'])
                P.tt('dve', raw[:], raw[:], mx[:].unsqueeze(1).to_broadcast([128, DEPTH, 16]), ALU.subtract,
                     reads=['hg_raw', 'hg_mx'], writes=['hg_raw'])
                P.act(raw[:], raw[:], AF.Exp, reads=['hg_raw'], writes=['hg_raw'])
                P.tt('dve', tot[:], raw[:, 0, :], raw[:, 1, :], ALU.add, reads=['hg_raw'], writes=['hg_tot'])
                for i in range(2, DEPTH):
                    P.tt('dve', tot[:], tot[:], raw[:, i, :], ALU.add, reads=['hg_raw', 'hg_tot'], writes=['hg_tot'])
                P.recip(tot[:], tot[:], reads=['hg_tot'], writes=['hg_tot'])
                P.memset('dve', hglb[:], 0.0, writes=['hglb'])
                for i in range(1, l + 1):
                    P.tt('dve', hglb[:], hglb[:], raw[:, i, :], ALU.add, reads=['hg_raw', 'hglb'], writes=['hglb'])
                P.tt('dve', hglb[:], hglb[:], tot[:], ALU.mult, reads=['hglb', 'hg_tot'], writes=['hglb'])
                P.ts('dve', hgoml[:], hglb[:], -1.0, 1.0, ALU.mult, ALU.add, reads=['hglb'], writes=['hgoml'])
            P.barrier()


        def hg_layer(l, b):
            tiles = token_tiles(nb, b)
            W = wb_hg.rearrange("(c p) n -> p c n", p=128)
            NBLK = LT // 128
            NCH = LT // 64
            with contextlib.ExitStack() as ph:
                hT = load_hT(ph, l, b, tiles)
                with contextlib.ExitStack() as ph2:
                    wzs = Rot([ph2.enter_context(sbt("g_wz%d" % i, [128, 8, 512], BF16)) for i in range(2)], 'g_wz')
                    zsg = Rot([ph2.enter_context(sbt("g_zs%d" % i, [128, 4, 512], BF16)) for i in range(3)], 'g_zs')
                    for cg in range(4):
                        wz, wzk = wzs.next()
                        P.dma('sp', wz[:], W[:, :, 8192 + cg * 512:8192 + (cg + 1) * 512], writes=[wzk])
                        for (t0, nt, r) in tiles:
                            sg, sgk = zsg.next()
                            for hh in range(4):
                                pt, pkey = next_ps()
                                for c in range(8):
                                    P.mm(pt[:, :nt], wz[:, c, hh * 128:(hh + 1) * 128], hT[:, c, t0:t0 + nt], c == 0, c == 7,
                                         reads=[wzk, ('hT', t0, c)], writes=[pkey])
                                P.act(sg[:, hh, :nt], pt[:, :nt], AF.Silu, reads=[pkey], writes=[sgk])
                            P.dma('pool', zT_s[cg * 512:(cg + 1) * 512, :].rearrange("(h p) t -> p h t", p=128)[:, :, t0:t0 + nt],
                                  sg[:, :, :nt], reads=[sgk], writes=[('zT_s', cg, t0)])
                    P.barrier()

                def A(name, shape, dt, n=1):
                    return [ph.enter_context(sbt("%s%d" % (name, i), list(shape), dt)) for i in range(n)]
                wts = Rot(A("g_w", [128, 4, 8, 128], BF16, 1), 'g_w')
                QT = A("g_QT", [128, LT], BF16, 2)
                KT = A("g_KT", [128, LT], BF16, 2)
                Ktok = A("g_Kt", [128, NBLK, 128], BF16, 2)
                Vtok = A("g_Vt", [128, NBLK, 128], BF16, 1)[0]
                O = A("g_O", [128, LT], F32, 1)
                S32a = A("g_S32a", [128, NCH + 1, 128], F32, 1)[0]
                Sbfa = A("g_Sbfa", [128, NCH, 128], BF16, 1)[0]
                atts = Rot(A("g_atts", [128, 4, 64], BF16, 6), 'g_atts')
                yst = Rot(A("g_y", [128, 512], BF16, 2), 'g_y')
                NCHN = 4
                t_e = Rot(A("g_te", [128, 512], F32, 2 * NCHN), 'g_te')
                t_q = Rot(A("g_tq", [128, 512], BF16, 4), 'g_tq')
                t_g1 = Rot(A("g_g1", [128, 512], F32, NCHN), 'g_g1')
                t_g2 = Rot(A("g_g2", [128, 512], F32, NCHN), 'g_g2')
                t_b = Rot(A("g_b", [128, 512], F32, NCHN), 'g_b')
                t_E = Rot(A("g_E", [128, 512], F32, NCHN), 'g_E')
                t_kd = Rot(A("g_kd", [128, 512], BF16, NCHN), 'g_kd')

                def tkey(c):
                    return 0 if c < 4 else LC + 512 * ((c - 4) // 8)

                batches = [tiles[0:2], tiles[2:4], tiles[4:5]]

                def emit_front(hd, wt, wk, batch):
                    fr = []
                    for (t0, nt, r) in batch:
                        blk0 = t0 // 128
                        nbk = nt // 128
                        pp = {}
                        for role in (0, 1, 2):
                            pt, pkey = next_ps()
                            for c in range(8):
                                P.mm(pt[:, :nt], wt[:, role, c, :], hT[:, c, t0:t0 + nt], c == 0, c == 7,
                                     reads=[wk, ('hT', t0, c)], writes=[pkey])
                            pp[role] = (pt, pkey)
                        pv, pvk = next_ps()
                        for tb in range(nbk):
                            for c in range(8):
                                P.mm(pv[:, tb * 128:(tb + 1) * 128], hT[:, c, t0 + tb * 128:t0 + (tb + 1) * 128],
                                     wt[:, 3, c, :], c == 0, c == 7, reads=[wk, ('hT', t0, c)], writes=[pvk])
                        qs, qsk = t_q.next()
                        P.cp('act', qs[:, :nt], pp[0][0][:, :nt], reads=[pp[0][1]], writes=[qsk])
                        es = []
                        for d in range(2):
                            e_, ek = t_e.next()
                            P.act(e_[:, :nt], pp[1 + d][0][:, :nt], AF.Exp, reads=[pp[1 + d][1]], writes=[ek], scale=-1.0)
                            es.append((e_, ek))
                        P.cp('act', Vtok[:, blk0:blk0 + nbk, :], pv[:, :nt].rearrange("p (b v) -> p b v", v=128),
                             reads=[pvk], writes=[('Vt', t0)])
                        fr.append((t0, nt, qs, qsk, es))
                    return fr

                def emit_chain(hd, fr):
                    ch = []
                    for (t0, nt, qs, qsk, es) in fr:
                        for d in range(2):
                            g1, g1k = t_g1.next()
                            g2, g2k = t_g2.next()
                            bt, bk = t_b.next()
                            Et, Ek = t_E.next()
                            kd, kdk = t_kd.next()
                            ch.append(dict(t0=t0, nt=nt, d=d, qs=qs, qsk=qsk, e=es[d][0], ek=es[d][1], g1=g1, g1k=g1k,
                                           g2=g2, g2k=g2k, bt=bt, bk=bk, Et=Et, Ek=Ek, kd=kd, kdk=kdk))
                    for c_ in ch:
                        nt = c_['nt']
                        P.act(c_['g1'][:, :nt], c_['e'][:, :nt], AF.Ln, reads=[c_['ek'], 'hglb'], writes=[c_['g1k']],
                              scale=hglb[:, hd:hd + 1], bias=1.0)
                        P.act(c_['g2'][:, :nt], c_['e'][:, :nt], AF.Ln, reads=[c_['ek']], writes=[c_['g2k']], bias=1.0)
                    for c_ in ch:
                        nt = c_['nt']
                        P.tt('dve', c_['g1'][:, :nt], c_['g1'][:, :nt], c_['g2'][:, :nt], ALU.subtract,
                             reads=[c_['g1k'], c_['g2k']], writes=[c_['g1k']])
                        if c_['d'] == 0:
                            rs = cst[:, C_HGRF:C_HGRF + nt]
                            P.op('dve', (lambda o_, a_, b_: (lambda e: e.tensor_tensor_scan(
                                out=o_, data0=a_, data1=b_, initial=0.0, op0=ALU.mult, op1=ALU.add)))(
                                c_['bt'][:, :nt], rs, c_['g1'][:, :nt]), reads=[c_['g1k'], 'cst'], writes=[c_['bk']])
                        else:
                            rs = cst[:, C_HGRR:C_HGRR + nt]
                            P.op('dve', (lambda o_, a_, b_: (lambda e: e.tensor_tensor_scan(
                                out=o_, data0=a_, data1=b_, initial=0.0, op0=ALU.mult, op1=ALU.add)))(
                                c_['bt'][:, :nt][:, ::-1], rs[:, ::-1], c_['g1'][:, :nt][:, ::-1]),
                                reads=[c_['g1k'], 'cst'], writes=[c_['bk']])
                    for c_ in ch:
                        nt = c_['nt']
                        P.tt('pool', c_['g2'][:, :nt], c_['g2'][:, :nt], c_['bt'][:, :nt], ALU.add,
                             reads=[c_['g2k'], c_['bk']], writes=[c_['g2k']])
                    for c_ in ch:
                        nt = c_['nt']
                        P.act(c_['Et'][:, :nt], c_['bt'][:, :nt], AF.Exp, reads=[c_['bk']], writes=[c_['Ek']])
                        P.act(c_['g1'][:, :nt], c_['g2'][:, :nt], AF.Exp, reads=[c_['g2k']], writes=[c_['g1k']], scale=-1.0)
                    for c_ in ch:
                        nt, t0, d = c_['nt'], c_['t0'], c_['d']
                        P.tt('dve', QT[d][:, t0:t0 + nt], c_['qs'][:, :nt], c_['Et'][:, :nt], ALU.mult,
                             reads=[c_['qsk'], c_['Ek']], writes=[('QT', d, t0)])
                    for c_ in ch:
                        nt, t0, d = c_['nt'], c_['t0'], c_['d']
                        c0 = t0 // 64
                        ncn = nt // 64
                        P.stt('dve', KT[d][:, t0:t0 + nt], c_['e'][:, :nt], hgoml[:, hd:hd + 1], c_['g1'][:, :nt],
                              ALU.mult, ALU.mult, reads=[c_['ek'], c_['g1k'], 'hgoml'], writes=[('KT', d, t0)])
                        src = c_['Et'][:, 63:nt:64] if d == 0 else c_['Et'][:, 0:nt:64]
                        P.cp('pool', dec[d][:, c0:c0 + ncn], src, reads=[c_['Ek']], writes=[('dec', d, t0)])
                        P.tt('pool', c_['kd'][:, :nt].rearrange("p (c t) -> p c t", t=64),
                             KT[d][:, t0:t0 + nt].rearrange("p (c t) -> p c t", t=64),
                             dec[d][:, c0:c0 + ncn].unsqueeze(2).to_broadcast([128, ncn, 64]), ALU.mult,
                             reads=[('KT', d, t0), ('dec', d, t0)], writes=[c_['kdk']])
                    trs = []
                    for c_ in ch:
                        nt, t0, d = c_['nt'], c_['t0'], c_['d']
                        nbk = nt // 128
                        ptr, ptrk = next_ps()
                        ptb = ptr[:].bitcast(BF16)
                        for tb in range(nbk):
                            P.tr(ptb[:, tb * 128:(tb + 1) * 128], c_['kd'][:, tb * 128:(tb + 1) * 128], identbf[:],
                                 reads=[c_['kdk'], 'identbf'], writes=[ptrk])
                        trs.append((ptb, ptrk))
                    for c_, (ptb, ptrk) in zip(ch, trs):
                        nt, t0, d = c_['nt'], c_['t0'], c_['d']
                        blk0 = t0 // 128
                        nbk = nt // 128
                        P.cp('act', Ktok[d][:, blk0:blk0 + nbk, :], ptb[:, :nt].rearrange("p (b k) -> p b k", k=128),
                             reads=[ptrk], writes=[('Kt', d, t0)])

                dec = A("g_dec", [128, NCH], F32, 2)
                for hd in range(16):
                    wt, wk = wts.next()
                    for role, cb in enumerate((0, 2048, 4096, 6144)):
                        P.dma('sp', wt[:, role], W[:, :, cb + hd * 128:cb + (hd + 1) * 128], writes=[wk])
                    fr_prev = emit_front(hd, wt, wk, batches[0])
                    for bi in range(len(batches)):
                        fr_next = emit_front(hd, wt, wk, batches[bi + 1]) if bi + 1 < len(batches) else None
                        emit_chain(hd, fr_prev)
                        fr_prev = fr_next
                    orders = [list(range(NCH)), [3, 2, 1, 0] + list(range(NCH - 1, 3, -1))]
                    groups = [(0, 4)] + [(4 + 8 * i_, 8) for i_ in range((NCH - 4) // 8)]
                    for d in range(2):
                        order = orders[d]
                        P.memset('pool', S32a[:, 0, :], 0.0, writes=[('S32a', 0)])
                        def emit_scan_steps(r_lo, r_n):
                          for r in range(r_lo, r_lo + r_n):
                            c = order[r]
                            blk = c // 2
                            pb = 64 * (c % 2)
                            t0k = tkey(c)
                            pu, puk = next_ps()
                            P.mm(pu[:, 0:128], Ktok[d][pb:pb + 64, blk, :], Vtok[pb:pb + 64, blk, :],
                                 True, True, reads=[('Kt', d, t0k), ('Vt', t0k)], writes=[puk])
                            P.stt('dve', S32a[:, r + 1, :], S32a[:, r, :], dec[d][:, c:c + 1], pu[:, 0:128],
                                  ALU.mult, ALU.add, reads=[puk, ('S32a', r), ('dec', d, t0k)], writes=[('S32a', r + 1)])
                        def emit_scores(gi):
                            r0, ns = groups[gi]
                            cs = [order[r_] for r_ in range(r0, r0 + ns)]
                            cfirst = min(cs)
                            t0k = tkey(cfirst)
                            nbk = ns // 2
                            pa, pak = next_ps()
                            for c in cs:
                                bl = (c - cfirst) // 2
                                pb = 64 * (c % 2)
                                P.mm(pa[pb:pb + 64, 64 * bl:64 * (bl + 1)], KT[d][:, 64 * c:64 * c + 64], QT[d][:, 64 * c:64 * c + 64],
                                     True, True, reads=[('KT', d, t0k), ('QT', d, t0k)], writes=[pak])
                            ats = []
                            for par in range(2):
                                mk = cst[:, C_HGM4 + 128 * d + 64 * par:C_HGM4 + 128 * d + 64 * par + 64]
                                at_, atk = atts.next()
                                P.tt('dve', at_[:, 0:nbk, :], pa[:, 0:64 * nbk].rearrange("p (b t) -> p b t", t=64),
                                     mk.unsqueeze(1).to_broadcast([128, nbk, 64]), ALU.mult, reads=[pak, 'cst'], writes=[atk])
                                ats.append((at_, atk))
                            P.cp('act', Sbfa[:, r0:r0 + ns, :], S32a[:, r0:r0 + ns, :],
                                 reads=[('S32a', r_) for r_ in range(r0, r0 + ns)], writes=[('Sbfa', r0)])
                            return (r0, ns, cs, cfirst, t0k, ats)

                        def emit_outputs(sc):
                            r0, ns, cs, cfirst, t0k, ats = sc
                            ntk = 64 * ns
                            po, pok = next_ps()
                            for ri, c in enumerate(cs):
                                bl = (c - cfirst) // 2
                                blk = c // 2
                                col = 64 * (c - cfirst)
                                at_, atk = ats[c % 2]
                                P.mm(po[:, col:col + 64], Vtok[:, blk, :], at_[:, bl, :], True, False,
                                     reads=[('Vt', t0k), atk], writes=[pok])
                                P.mm(po[:, col:col + 64], Sbfa[:, r0 + ri, :], QT[d][:, 64 * c:64 * c + 64], False, True,
                                     reads=[('Sbfa', r0), ('QT', d, t0k)], writes=[pok])
                            if d == 0:
                                P.cp('act', O[0][:, t0k:t0k + ntk], po[:, :ntk], reads=[pok], writes=[('O', 0, t0k)])
                            else:
                                P.tt('dve', O[0][:, t0k:t0k + ntk], po[:, :ntk], O[0][:, t0k:t0k + ntk], ALU.add,
                                     reads=[pok, ('O', 0, t0k)], writes=[('O', 0, t0k)])

                        emit_scan_steps(*groups[0])
                        sc_prev = emit_scores(0)
                        for gi in range(len(groups)):
                            sc_next = None
                            if gi + 1 < len(groups):
                                emit_scan_steps(*groups[gi + 1])
                                sc_next = emit_scores(gi + 1)
                            emit_outputs(sc_prev)
                            sc_prev = sc_next
                    nrm = []
                    for (t0, nt, r) in tiles:
                        sq, sqk = (t_g1 if len(nrm) % 2 == 0 else t_g2).next()
                        rt, rk = (t_b if len(nrm) % 2 == 0 else t_E).next()
                        nrm.append((t0, nt, sq, sqk, rt, rk))
                    for (t0, nt, sq, sqk, rt, rk) in nrm:
                        P.act(sq[:, :nt], O[0][:, t0:t0 + nt], AF.Square, reads=[('O', 0, t0)], writes=[sqk])
                    pns = []
                    for (t0, nt, sq, sqk, rt, rk) in nrm:
                        pn, pnk = next_ps()
                        P.mm(pn[:, :nt], ones32[:], sq[:, :nt], True, True, reads=['ones32', sqk], writes=[pnk])
                        pns.append((pn, pnk))
                    for (t0, nt, sq, sqk, rt, rk), (pn, pnk) in zip(nrm, pns):
                        P.act(rt[:, :nt], pn[:, :nt], AF.Ln, reads=[pnk], writes=[rk], scale=1.0 / 128.0, bias=EPS)
                    for (t0, nt, sq, sqk, rt, rk) in nrm:
                        P.act(rt[:, :nt], rt[:, :nt], AF.Exp, reads=[rk], writes=[rk], scale=-0.5)
                    for (t0, nt, sq, sqk, rt, rk) in nrm:
                        ys, ysk = yst.next()
                        P.stt('dve', ys[:, :nt], O[0][:, t0:t0 + nt], hgng[:, hd:hd + 1], rt[:, :nt], ALU.mult, ALU.mult,
                              reads=[('O', 0, t0), rk, 'hgng'], writes=[ysk])
                        P.dma('pool', yT_s[hd * 128:(hd + 1) * 128, t0:t0 + nt], ys[:, :nt], reads=[ysk],
                              writes=[('yT_s', hd, t0)])
            P.barrier()


        NK = LT // 8
        NKC = LC // 8
        A1 = [galloc("s5A1_%d" % d, [128, 2, 64], F32) for d in range(2)]
        A2 = [galloc("s5A2_%d" % d, [128, 2, 64], F32) for d in range(2)]
        glub = galloc("s5glub", [128, 16], F32)
        B1 = [galloc("s5B1_%d" % d, [128, 2, 64], F32) for d in range(2)]
        B2 = [galloc("s5B2_%d" % d, [128, 2, 64], F32) for d in range(2)]

        def s5_prepare():
            with contextlib.ExitStack() as ph:
                def A(name, shape, dt):
                    return ph.enter_context(sbt(name, list(shape), dt))
                seltmp = A("p_seltmp", [128, 8, 128], F32)
                Sel = A("p_sel", [128, 8, 8, 128], BF16)
                P.tt('dve', seltmp[:], cst[:, C_IM:C_IM + 1024].rearrange("p (b q) -> p b q", b=8),
                     cst[:, C_MM:C_MM + 128].unsqueeze(1).to_broadcast([128, 8, 128]), ALU.mult,
                     reads=['cst'], writes=['seltmp'])
                for a in range(8):
                    P.ts('dve', Sel[:, a, :, :], seltmp[:], cst[:, C_GM + a:C_GM + a + 1], None, ALU.mult,
                         reads=['seltmp', 'cst'], writes=['Sel'])
                P.dma('pool', sel_s, Sel[:].rearrange("p a b q -> p (a b q)"), reads=['Sel'], writes=['sel_s'])
                P.dma('sp', glub[:], s5_glu_b[0].rearrange("(c p) -> p c", p=128), writes=['glub'],
                      allow_slow_non_contiguous=True)
                dtab = A("p_dtab", [128, 128], F32)
                for i in range(8):
                    P.dma('sp', dtab[16 * i:16 * (i + 1), :], s5_d[0].rearrange("(g m) -> m g", m=16), writes=['dtab'],
                          allow_slow_non_contiguous=True)
                SH = [64, 128]
                sm = {}

                def T(name, shape=SH):
                    t = A("p_" + name, shape, F32)
                    sm[name] = t
                    return t
                ki = A("p_ki", SH, I32)
                tmpa = T("tmpa")
                tmpb = T("tmpb")
                per = []
                Ct = []
                for d in range(2):
                    nm = lambda x_: "%s%d" % (x_, d)
                    lr = T(nm("lr"))
                    li = T(nm("li"))
                    ls = T(nm("ls"))
                    P.dma('sp', lr[:], s5_lam_re[0, d].rearrange("g p -> p g"), writes=[nm("lr")], allow_slow_non_contiguous=True)
                    P.dma('sp', li[:], s5_lam_im[0, d].rearrange("g p -> p g"), writes=[nm("li")], allow_slow_non_contiguous=True)
                    P.dma('sp', ls[:], s5_log_step[0, d].partition_broadcast(64), writes=[nm("ls")])
                    P.act(ls[:], ls[:], AF.Exp, reads=[nm("ls")], writes=[nm("ls")])
                    ar = T(nm("ar"))
                    ang = T(nm("ang"))
                    P.tt('dve', ar[:], lr[:], ls[:], ALU.mult, reads=[nm("lr"), nm("ls")], writes=[nm("ar")])
                    P.tt('dve', ang[:], li[:], ls[:], ALU.mult, reads=[nm("li"), nm("ls")], writes=[nm("ang")])
                    mag = T(nm("mag"))
                    magi = T(nm("magi"))
                    P.act(mag[:], ar[:], AF.Exp, reads=[nm("ar")], writes=[nm("mag")])
                    P.act(magi[:], ar[:], AF.Exp, reads=[nm("ar")], writes=[nm("magi")], scale=-1.0)
                    sn = T(nm("sn"))
                    cs = T(nm("cs"))
                    for (dst, shift, dk) in ((sn, 0.0, nm("sn")), (cs, math.pi / 2, nm("cs"))):
                        P.ts('dve', tmpa[:], ang[:], shift, 1.0 / TWO_PI, ALU.add, ALU.mult, reads=[nm("ang")], writes=['tmpa'])
                        P.cp('dve', ki[:], tmpa[:], reads=['tmpa'], writes=['ki'])
                        P.cp('dve', tmpb[:], ki[:], reads=['ki'], writes=['tmpb'])
                        P.stt('dve', tmpa[:], tmpb[:], -TWO_PI, ang[:], ALU.mult, ALU.add, reads=['tmpb', nm("ang")], writes=['tmpa'])
                        P.ts('dve', tmpa[:], tmpa[:], shift, 3.14159, ALU.add, ALU.min, reads=['tmpa'], writes=['tmpa'])
                        P.ts('dve', tmpa[:], tmpa[:], -3.14159, None, ALU.max, reads=['tmpa'], writes=['tmpa'])
                        P.act(dst[:], tmpa[:], AF.Sin, reads=['tmpa'], writes=[dk])
                    a_re = T(nm("a_re"))
                    a_im = T(nm("a_im"))
                    i_re = T(nm("i_re"))
                    i_im = T(nm("i_im"))
                    P.tt('dve', a_re[:], mag[:], cs[:], ALU.mult, reads=[nm("mag"), nm("cs")], writes=[nm("a_re")])
                    P.tt('dve', a_im[:], mag[:], sn[:], ALU.mult, reads=[nm("mag"), nm("sn")], writes=[nm("a_im")])
                    P.tt('dve', i_re[:], magi[:], cs[:], ALU.mult, reads=[nm("magi"), nm("cs")], writes=[nm("i_re")])
                    P.tt('dve', i_im[:], magi[:], sn[:], ALU.mult, reads=[nm("magi"), nm("sn")], writes=[nm("i_im")])
                    P.ts('dve', i_im[:], i_im[:], -1.0, None, ALU.mult, reads=[nm("i_im")], writes=[nm("i_im")])
                    cf_re = T(nm("cf_re"))
                    cf_im = T(nm("cf_im"))
                    nr = mag
                    P.ts('dve', nr[:], a_re[:], -1.0, None, ALU.add, reads=[nm("a_re")], writes=[nm("mag")])
                    den = magi
                    P.tt('dve', den[:], lr[:], lr[:], ALU.mult, reads=[nm("lr")], writes=[nm("magi")])
                    P.tt('dve', tmpa[:], li[:], li[:], ALU.mult, reads=[nm("li")], writes=['tmpa'])
                    P.tt('dve', den[:], den[:], tmpa[:], ALU.add, reads=[nm("magi"), 'tmpa'], writes=[nm("magi")])
                    P.recip(den[:], den[:], reads=[nm("magi")], writes=[nm("magi")])
                    P.tt('dve', cf_re[:], nr[:], lr[:], ALU.mult, reads=[nm("mag"), nm("lr")], writes=[nm("cf_re")])
                    P.tt('dve', tmpa[:], a_im[:], li[:], ALU.mult, reads=[nm("a_im"), nm("li")], writes=['tmpa'])
                    P.tt('dve', cf_re[:], cf_re[:], tmpa[:], ALU.add, reads=[nm("cf_re"), 'tmpa'], writes=[nm("cf_re")])
                    P.tt('dve', cf_re[:], cf_re[:], den[:], ALU.mult, reads=[nm("cf_re"), nm("magi")], writes=[nm("cf_re")])
                    P.tt('dve', cf_im[:], a_im[:], lr[:], ALU.mult, reads=[nm("a_im"), nm("lr")], writes=[nm("cf_im")])
                    P.tt('dve', tmpa[:], nr[:], li[:], ALU.mult, reads=[nm("mag"), nm("li")], writes=['tmpa'])
                    P.tt('dve', cf_im[:], cf_im[:], tmpa[:], ALU.subtract, reads=[nm("cf_im"), 'tmpa'], writes=[nm("cf_im")])
                    P.tt('dve', cf_im[:], cf_im[:], den[:], ALU.mult, reads=[nm("cf_im"), nm("magi")], writes=[nm("cf_im")])
                    p_re, p_im = sn, cs
                    P.cp('dve', p_re[:], a_re[:], reads=[nm("a_re")], writes=[nm("sn")])
                    P.cp('dve', p_im[:], a_im[:], reads=[nm("a_im")], writes=[nm("cs")])
                    for _sq in range(3):
                        P.tt('dve', tmpa[:], p_re[:], p_re[:], ALU.mult, reads=[nm("sn")], writes=['tmpa'])
                        P.tt('dve', tmpb[:], p_im[:], p_im[:], ALU.mult, reads=[nm("cs")], writes=['tmpb'])
                        P.tt('dve', p_im[:], p_re[:], p_im[:], ALU.mult, reads=[nm("sn"), nm("cs")], writes=[nm("cs")])
                        P.ts('dve', p_im[:], p_im[:], 2.0, None, ALU.mult, reads=[nm("cs")], writes=[nm("cs")])
                        P.tt('dve', p_re[:], tmpa[:], tmpb[:], ALU.subtract, reads=['tmpa', 'tmpb'], writes=[nm("sn")])
                    P.dma('pool', a8_s[d, 0], p_re[:], reads=[nm("sn")], writes=[('a8s', d, 0)])
                    P.dma('pool', a8_s[d, 1], p_im[:], reads=[nm("cs")], writes=[('a8s', d, 1)])
                    for comp in range(2):
                        src = a8_s[d, comp].rearrange("p (pr h) -> h p pr", h=2)
                        for h in range(2):
                            P.dma('sp', A1[d][64 * h:64 * (h + 1), comp if False else 0, :] if False else
                                  (A1[d][64 * h:64 * (h + 1), 0, :] if comp == 0 else A2[d][64 * h:64 * (h + 1), 1, :]),
                                  src[h], reads=[('a8s', d, comp)], writes=[('A12', d, comp, h)],
                                  allow_slow_non_contiguous=True)
                    P.cp('dve', A1[d][:, 1, :], A1[d][:, 0, :], reads=[('A12', d, 0, 0), ('A12', d, 0, 1)], writes=[('A1', d)])
                    P.ts('dve', A2[d][:, 0, :], A2[d][:, 1, :], -1.0, None, ALU.mult,
                         reads=[('A12', d, 1, 0), ('A12', d, 1, 1)], writes=[('A2', d)])
                    bt1 = A("p_bt1_%d" % d, [128, 64], F32)
                    bt2 = A("p_bt2_%d" % d, [128, 64], F32)
                    P.tt('dve', bt1[:], A1[d][:, 0, :], A1[d][:, 0, :], ALU.mult, reads=[('A1', d)], writes=[('bt1', d)])
                    P.tt('dve', bt2[:], A2[d][:, 1, :], A2[d][:, 1, :], ALU.mult, reads=[('A2', d)], writes=[('bt2', d)])
                    P.tt('dve', B1[d][:, 0, :], bt1[:], bt2[:], ALU.subtract, reads=[('bt1', d), ('bt2', d)], writes=[('B1', d)])
                    P.cp('dve', B1[d][:, 1, :], B1[d][:, 0, :], reads=[('B1', d)], writes=[('B1', d)])
                    P.tt('dve', bt1[:], A1[d][:, 0, :], A2[d][:, 1, :], ALU.mult, reads=[('A1', d), ('A2', d), ('bt1', d)], writes=[('bt1', d)])
                    P.ts('dve', B2[d][:, 1, :], bt1[:], 2.0, None, ALU.mult, reads=[('bt1', d)], writes=[('B2', d)])
                    P.ts('dve', B2[d][:, 0, :], bt1[:], -2.0, None, ALU.mult, reads=[('bt1', d)], writes=[('B2', d)])
                    per.append(dict(a_re=a_re, a_im=a_im, i_re=i_re, i_im=i_im, cf_re=cf_re, cf_im=cf_im,
                                    k=[nm("a_re"), nm("a_im"), nm("i_re"), nm("i_im"), nm("cf_re"), nm("cf_im")]))
                    ctd = []
                    for comp, srcC in enumerate((s5_c_re, s5_c_im)):
                        cn = A("p_cn%d%d" % (d, comp), [128, 1024], F32)
                        ct = A("p_ct%d%d" % (d, comp), [64, 128, 16], F32)
                        P.dma('sp', cn[:], srcC[0, d].rearrange("g n p -> g (n p)"), writes=[('cn', d, comp)])
                        for n4 in range(4):
                            pt, pkey = next_ps()
                            for nn in range(4):
                                n = n4 * 4 + nn
                                P.tr(pt[0:64, nn * 128:(nn + 1) * 128], cn[:, n * 64:(n + 1) * 64], ident32,
                                     reads=[('cn', d, comp), 'cst'], writes=[pkey])
                            P.cp('act', ct[:, :, n4 * 4:(n4 + 1) * 4].rearrange("p g n -> p n g"),
                                 pt[0:64, :].rearrange("p (n g) -> p n g", n=4), reads=[pkey], writes=[('ct', d, comp)])
                        ctd.append(ct)
                    Ct.append(ctd)

                GQ = 16
                Bt = [A("p_B%d" % c_, [64, GQ, 16], F32) for c_ in range(2)]
                Xt = [A("p_X%d" % c_, [64, GQ, 8, 16], F32) for c_ in range(2)]
                Vt = [A("p_V%d" % c_, [64, GQ, 8, 16], F32) for c_ in range(2)]
                Vc = [[A("p_Vc%d%d" % (a_, c_), [64, GQ, 16], F32) for c_ in range(2)] for a_ in range(2)]
                Wc = [[A("p_Wc%d%d" % (a_, c_), [64, GQ, 16], F32) for c_ in range(2)] for a_ in range(2)]
                Wb = [A("p_Wb%d" % c_, [64, GQ, 8, 16], BF16) for c_ in range(2)]
                cm = [A("p_cm%d" % c_, [64, GQ, 16], F32) for c_ in range(4)]
                Wacc = A("p_Wacc", [128, GQ, 128], F32)
                Wtmp = A("p_Wtmp", [128, 4, 128], F32)
                Wib = A("p_Wib", [128, GQ, 128], BF16)
                WSb = A("p_WSb", [128, 8, 64], BF16)
                uid = [0]

                def cmul(eng, cs, o_re, o_im, x_re, x_im, y_re, y_im, rk, wk):
                    kk = [('cm', cs, i_) for i_ in range(4)]
                    c_ = cmsets[cs]
                    P.tt(eng, c_[0][:], x_re, y_re, ALU.mult, reads=rk, writes=[kk[0]])
                    P.tt(eng, c_[1][:], x_im, y_im, ALU.mult, reads=rk, writes=[kk[1]])
                    P.tt(eng, c_[2][:], x_re, y_im, ALU.mult, reads=rk, writes=[kk[2]])
                    P.tt(eng, c_[3][:], x_im, y_re, ALU.mult, reads=rk, writes=[kk[3]])
                    P.tt(eng, o_re, c_[0][:], c_[1][:], ALU.subtract, reads=[kk[0], kk[1]], writes=wk)
                    P.tt(eng, o_im, c_[2][:], c_[3][:], ALU.add, reads=[kk[2], kk[3]], writes=wk)

                cmsets = [cm] + [[A("p_cm%d_%d" % (a_, c_), [64, GQ, 16], F32) for c_ in range(4)] for a_ in range(2)]

                for gq in range(128 // GQ):
                    g0 = gq * GQ
                    for d in range(2):
                        pd = per[d]

                        def bc(t):
                            return t[:, g0:g0 + GQ].unsqueeze(2).to_broadcast([64, GQ, 16])
                        P.dma('sp', Bt[0][:], s5_b_re[0, d, g0:g0 + GQ].rearrange("g p m -> p g m"), writes=['Bt0'])
                        P.dma('sp', Bt[1][:], s5_b_im[0, d, g0:g0 + GQ].rearrange("g p m -> p g m"), writes=['Bt1'])
                        cre = Ct[d][0][:, g0:g0 + GQ, :]
                        cim = Ct[d][1][:, g0:g0 + GQ, :]

                        def xs(tau):
                            return (7 - tau) if d == 0 else tau

                        def xstep(tau):
                            if tau == 0:
                                cmul('dve', 0, Xt[0][:, :, xs(0), :], Xt[1][:, :, xs(0), :], Bt[0][:], Bt[1][:],
                                     bc(pd['cf_re']), bc(pd['cf_im']), ['Bt0', 'Bt1'] + pd['k'], [('X', xs(0))])
                            else:
                                cmul('dve', 0, Xt[0][:, :, xs(tau), :], Xt[1][:, :, xs(tau), :],
                                     Xt[0][:, :, xs(tau - 1), :], Xt[1][:, :, xs(tau - 1), :], bc(pd['a_re']), bc(pd['a_im']),
                                     [('X', xs(tau - 1))] + pd['k'], [('X', xs(tau))])

                        vorder = list(range(7, -1, -1)) if d == 0 else list(range(8))

                        def vstep(idx):
                            jj = vorder[idx]
                            cur = Vc[idx % 2]
                            prev = Vc[(idx - 1) % 2]
                            if idx == 0:
                                P.cp('pool', cur[0][:], cre, reads=[('ct', d, 0)], writes=[('Vc', idx % 2)])
                                P.cp('pool', cur[1][:], cim, reads=[('ct', d, 1)], writes=[('Vc', idx % 2)])
                            else:
                                cmul('pool', 1, cur[0][:], cur[1][:], prev[0][:], prev[1][:], bc(pd['i_re']), bc(pd['i_im']),
                                     [('Vc', (idx - 1) % 2)] + pd['k'], [('Vc', idx % 2)])
                            P.cp('act', Vt[0][:, :, jj, :], cur[0][:], reads=[('Vc', idx % 2)], writes=[('V', jj)])
                            P.act(Vt[1][:, :, jj, :], cur[1][:], AF.Copy, reads=[('Vc', idx % 2)], writes=[('V', jj)], scale=-1.0)

                        def wstep(i_):
                            e_ = i_ + 1
                            jj = (e_ - 1) if d == 0 else (8 - e_)
                            cur = Wc[e_ % 2]
                            prev = Wc[(e_ - 1) % 2]
                            if e_ == 1:
                                cmul('dve', 2, cur[0][:], cur[1][:], cre, cim, bc(pd['a_re']), bc(pd['a_im']),
                                     [('ct', d, 0), ('ct', d, 1)] + pd['k'], [('Wc', e_ % 2)])
                            else:
                                cmul('dve', 2, cur[0][:], cur[1][:], prev[0][:], prev[1][:], bc(pd['a_re']), bc(pd['a_im']),
                                     [('Wc', (e_ - 1) % 2)] + pd['k'], [('Wc', e_ % 2)])
                            P.cp('act', Wb[0][:, :, jj, :], cur[0][:], reads=[('Wc', e_ % 2)], writes=[('Wb', 0)])
                            P.act(Wb[1][:, :, jj, :], cur[1][:], AF.Copy, reads=[('Wc', e_ % 2)], writes=[('Wb', 1)], scale=-1.0)

                        for i_ in range(8):
                            xstep(i_)
                            vstep(i_)
                            wstep(i_)
                        for comp in range(2):
                            P.dma('pool', Winter_s[d, comp, g0:g0 + GQ].rearrange("g p jn -> p g jn"),
                                  Wb[comp][:].rearrange("p g j n -> p g (j n)"), reads=[('Wb', comp)],
                                  writes=[('Winter_s', d, comp, gq)])
                        xkeys = [('X', i_) for i_ in range(8)]
                        vkeys = [('V', i_) for i_ in range(8)]
                        for comp in range(2):
                            for g8 in range(GQ // 8):
                                pt, pkey = next_ps()
                                for gg in range(8):
                                    g = g8 * 8 + gg
                                    P.tr(pt[:, gg * 64:(gg + 1) * 64], Xt[comp][:, g, :, :].rearrange("p s m -> p (s m)"),
                                         ident32[0:64, 0:64], reads=xkeys + ['cst'], writes=[pkey])
                                P.cp('act', WSb[:], pt[:].rearrange("p (g q) -> p g q", g=8), reads=[pkey], writes=['WSb'])
                                P.dma('pool', WS_s[d, comp, g0 + g8 * 8:g0 + (g8 + 1) * 8].rearrange("g q p -> q g p"),
                                      WSb[:], reads=['WSb'], writes=[('WS_s', d, comp, gq, g8)])
                        for g4 in range(GQ // 4):
                            pt, pkey = next_ps()
                            for gg in range(4):
                                g = g4 * 4 + gg
                                P.mm(pt[:, gg * 128:(gg + 1) * 128], Xt[0][:, g, :, :].rearrange("p s m -> p (s m)"),
                                     Vt[0][:, g, :, :].rearrange("p j n -> p (j n)"), True, False,
                                     reads=xkeys + vkeys, writes=[pkey])
                                P.mm(pt[:, gg * 128:(gg + 1) * 128], Xt[1][:, g, :, :].rearrange("p s m -> p (s m)"),
                                     Vt[1][:, g, :, :].rearrange("p j n -> p (j n)"), False, True,
                                     reads=xkeys + vkeys, writes=[pkey])
                            msk = cst[:, C_MF:C_MF + 128] if d == 0 else cst[:, C_MR:C_MR + 128]
                            mb = msk.unsqueeze(1).to_broadcast([128, 4, 128])
                            if d == 0:
                                P.tt('dve', Wacc[:, g4 * 4:(g4 + 1) * 4, :], pt[:].rearrange("p (g q) -> p g q", g=4), mb,
                                     ALU.mult, reads=[pkey, 'cst'], writes=[('Wacc', g4)])
                            else:
                                P.tt('dve', Wtmp[:], pt[:].rearrange("p (g q) -> p g q", g=4), mb,
                                     ALU.mult, reads=[pkey, 'cst'], writes=['Wtmp'])
                                P.tt('pool', Wacc[:, g4 * 4:(g4 + 1) * 4, :], Wacc[:, g4 * 4:(g4 + 1) * 4, :], Wtmp[:],
                                     ALU.add, reads=['Wtmp', ('Wacc', g4)], writes=[('Wacc', g4)])
                    for g in range(GQ):
                        P.stt('dve', Wib[:, g, :], ident32, dtab[:, g0 + g:g0 + g + 1], Wacc[:, g, :], ALU.mult, ALU.add,
                              reads=['cst', 'dtab', ('Wacc', g // 4)], writes=['Wib'])
                    P.dma('pool', Wintra_s[g0:g0 + GQ].rearrange("g q r -> q g r"), Wib[:], reads=['Wib'],
                          writes=[('Wintra_s', gq)])
            P.barrier()

        def s5_phase_A(l, b):
            tiles = token_tiles(nb, b)
            W = wb_s5.rearrange("(c p) n -> p c n", p=128)
            with contextlib.ExitStack() as ph:
                hT = load_hT(ph, l, b, tiles)
                wts = Rot([ph.enter_context(sbt("s_w%d" % i, [128, 8, 512], BF16)) for i in range(2)], 's_w')
                stg = Rot([ph.enter_context(sbt("s_s%d" % i, [128, 4, 512], BF16)) for i in range(3)], 's_s')
                for cg in range(8):
                    wt, wk = wts.next()
                    P.dma('sp', wt[:], W[:, :, cg * 512:(cg + 1) * 512], writes=[wk])
                    for (t0, nt, r) in tiles:
                        sg, sgk = stg.next()
                        for hh in range(4):
                            pt, pkey = next_ps()
                            for c in range(8):
                                P.mm(pt[:, :nt], wt[:, c, hh * 128:(hh + 1) * 128], hT[:, c, t0:t0 + nt], c == 0, c == 7,
                                     reads=[wk, ('hT', t0, c)], writes=[pkey])
                            if cg >= 4:
                                P.act(sg[:, hh, :nt], pt[:, :nt], AF.Silu, reads=[pkey], writes=[sgk])
                            else:
                                P.cp('act' if hh % 2 else 'dve', sg[:, hh, :nt], pt[:, :nt], reads=[pkey], writes=[sgk])
                        dst = uT_s[cg * 512:(cg + 1) * 512, :] if cg < 4 else zT_s[(cg - 4) * 512:(cg - 3) * 512, :]
                        P.dma('pool', dst.rearrange("(h p) t -> p h t", p=128)[:, :, t0:t0 + nt], sg[:, :, :nt],
                              reads=[sgk], writes=[('s5A', cg, t0)])
            P.barrier()

        def s5_phase_B(l, b):
            GB = 32
            NP = GB // 2
            NBLK = 128 // GB
            with contextlib.ExitStack() as ph:
                def A(name, shape, dt, n=1):
                    return [ph.enter_context(sbt("%s%d" % (name, i), list(shape), dt)) for i in range(n)]
                Us = A("b5_U", [128, GB, NK], BF16, 2)
                Sel = A("b5_sel", [128, 8, 8, 128], BF16)[0]
                P.dma('sp', Sel[:].rearrange("p a b q -> p (a b q)"), sel_s, writes=['Sel'])
                X = A("b5_X", [128, 2, NP, NK + 2], F32, 2)
                SW = 8
                slb = [A("b5_sl%d" % d, [128, 2, NP, SW], F32, 2) for d in range(2)]
                Xb = A("b5_Xb", [128, 2, NP, NK], BF16, 2)
                uts = Rot(A("b5_u", [128, LT], BF16, 2), 'b5_u')
                wss = Rot(A("b5_ws", [128, 4, 64], BF16, 3), 'b5_ws')
                wis = Rot(A("b5_wi", [128, 128], BF16, 3), 'b5_wi')
                wns = Rot(A("b5_wn", [128, 4, 128], BF16, 2), 'b5_wn')
                gys = A("b5_gy", [128, NK], BF16, 8)
                gst = Rot(A("b5_gs", [128, 512], BF16, 1), 'b5_gs')
                tm = [A("b5_t%d" % d, [128, 2, NP, 2], F32, 2) for d in range(2)]

                def emit_us(gb):
                    gbase = gb * GB
                    U = Us[gb % 2]
                    for d in range(2):
                        P.memset('pool', X[d][:, :, :, 0:2], 0.0, writes=[('X', d)])
                    wsprev = None
                    for ccl in range(GB // 8):
                        cc = gb * (GB // 8) + ccl
                        ut, utk = uts.next()
                        P.dma('sp', ut[:], uT_s[cc * 128:(cc + 1) * 128, :], writes=[utk])
                        for gl in range(8):
                            g = cc * 8 + gl
                            gloc = g - gbase
                            pl = gloc // 2
                            pU, pUk = next_ps()
                            for i in range(8):
                                P.mm(pU[:, :NK], Sel[:, gl, i, :], ut[:, i:LT:8], i == 0, i == 7,
                                     reads=['Sel', utk], writes=[pUk])
                            P.cp('act', U[:, gloc, :], pU[:, :NK], reads=[pUk], writes=[('U', gb % 2, gloc)])
                            ws, wsk = wss.next()
                            P.dma('sp', ws[:], WS_s[:, :, g].rearrange("d c q p -> q (d c) p"), writes=[wsk])
                            if g % 2 == 0:
                                wsprev = (ws, wsk)
                                continue
                            for d in range(2):
                                for comp in range(2):
                                    pS, pSk = next_ps()
                                    P.mm(pS[0:64, :NK], wsprev[0][:, d * 2 + comp, :], U[:, gloc - 1, :], True, True,
                                         reads=[wsprev[1], ('U', gb % 2, gloc - 1)], writes=[pSk])
                                    P.mm(pS[64:128, :NK], ws[:, d * 2 + comp, :], U[:, gloc, :], True, True,
                                         reads=[wsk, ('U', gb % 2, gloc)], writes=[pSk])
                                    if d == 0:
                                        P.cp('act', X[0][:, comp, pl, 2:NK + 2], pS[:, :NK], reads=[pSk], writes=[('X', 0)])
                                    else:
                                        P.cp('act', X[1][:, comp, pl, NKC + 1:1:-1], pS[:, 0:NKC], reads=[pSk], writes=[('X', 1)])
                                        P.cp('act', X[1][:, comp, pl, NK + 1:NKC + 1:-1], pS[:, NKC:NK], reads=[pSk], writes=[('X', 1)])

                def emit_scan(gb):
                    p0 = (gb * GB) // 2
                    for d, eng in ((0, 'dve'), (1, 'pool')):
                        c1 = NK + 2
                        while c1 > 2:
                            c0 = max(2, c1 - SW)
                            w = c1 - c0
                            a1b = A1[d][:, :, p0:p0 + NP].unsqueeze(3).to_broadcast([128, 2, NP, w])
                            a2b = A2[d][:, :, p0:p0 + NP].unsqueeze(3).to_broadcast([128, 2, NP, w])
                            P.tt(eng, slb[d][0][:, :, :, 0:w], a1b, X[d][:, :, :, c0 - 1:c1 - 1], ALU.mult,
                                 reads=[('X', d), ('A1', d)], writes=[('slb', d, 0)])
                            P.tt(eng, slb[d][1][:, :, :, 0:w], a2b, X[d][:, ::-1, :, c0 - 1:c1 - 1], ALU.mult,
                                 reads=[('X', d), ('A2', d)], writes=[('slb', d, 1)])
                            P.tt(eng, X[d][:, :, :, c0:c1], X[d][:, :, :, c0:c1], slb[d][0][:, :, :, 0:w], ALU.add,
                                 reads=[('slb', d, 0)], writes=[('X', d)])
                            P.tt(eng, X[d][:, :, :, c0:c1], X[d][:, :, :, c0:c1], slb[d][1][:, :, :, 0:w], ALU.add,
                                 reads=[('slb', d, 1)], writes=[('X', d)])
                            c1 = c0
                    for c in range(2, NK + 2, 2):
                        for d, eng in ((0, 'dve'), (1, 'pool')):
                            xp = X[d][:, :, :, c - 2:c]
                            xps = X[d][:, ::-1, :, c - 2:c]
                            xk = X[d][:, :, :, c:c + 2]
                            b1b = B1[d][:, :, p0:p0 + NP].unsqueeze(3).to_broadcast([128, 2, NP, 2])
                            b2b = B2[d][:, :, p0:p0 + NP].unsqueeze(3).to_broadcast([128, 2, NP, 2])
                            P.tt(eng, tm[d][0][:], b1b, xp, ALU.mult, reads=[('X', d), ('B1', d)], writes=[('tm', d, 0)])
                            P.tt(eng, tm[d][1][:], b2b, xps, ALU.mult, reads=[('X', d), ('B2', d)], writes=[('tm', d, 1)])
                            P.tt(eng, xk, xk, tm[d][0][:], ALU.add, reads=[('tm', d, 0)], writes=[('X', d)])
                            P.tt(eng, xk, xk, tm[d][1][:], ALU.add, reads=[('tm', d, 1)], writes=[('X', d)])

                def emit_xb(gb):
                    P.cp('act', Xb[0][:], X[0][:, :, :, 1:NK + 1], reads=[('X', 0)], writes=[('Xb', 0)])
                    P.cp('act', Xb[1][:, :, :, 0:NKC], X[1][:, :, :, NKC:0:-1], reads=[('X', 1)], writes=[('Xb', 1)])
                    P.cp('act', Xb[1][:, :, :, NKC:NK], X[1][:, :, :, NK:NKC:-1], reads=[('X', 1)], writes=[('Xb', 1)])

                def emit_y(gb):
                    gbase = gb * GB
                    U = Us[gb % 2]
                    for ccl in range(GB // 8):
                        cc = gb * (GB // 8) + ccl
                        wn = None
                        for gl in range(8):
                            g = cc * 8 + gl
                            gloc = g - gbase
                            pl = gloc // 2
                            hf = g % 2
                            wi, wik = wis.next()
                            P.dma('sp', wi[:], Wintra_s[g], writes=[wik])
                            if hf == 0:
                                wn, wnk = wns.next()
                                P.dma('sp', wn[:], Winter_s[:, :, g:g + 2].rearrange("d c g p n -> (g p) (d c) n"), writes=[wnk])
                            pY, pYk = next_ps()
                            P.mm(pY[:, :NK], wi[:], U[:, gloc, :], True, False, reads=[wik, ('U', gb % 2, gloc)], writes=[pYk])
                            for d in range(2):
                                for comp in range(2):
                                    P.mm(pY[:, :NK], wn[64 * hf:64 * (hf + 1), d * 2 + comp, :],
                                         Xb[d][64 * hf:64 * (hf + 1), comp, pl, :], False, (d == 1 and comp == 1),
                                         reads=[wnk, ('Xb', d)], writes=[pYk])
                            P.act(gys[gl][:], pY[:, :NK], AF.Gelu_apprx_tanh, reads=[pYk], writes=[('gy', gl)])
                        for bk in range((LT + 511) // 512):
                            ncol = min(64, NK - 64 * bk)
                            pL, pLk = next_ps()
                            for j in range(8):
                                for gl in range(8):
                                    P.mm(pL[:, j:8 * ncol:8], Sel[:, j, gl, :], gys[gl][:, 64 * bk:64 * bk + ncol],
                                         gl == 0, gl == 7, reads=['Sel', ('gy', gl)], writes=[pLk])
                            gs, gsk = gst.next()
                            P.cp('act', gs[:, :8 * ncol], pL[:, :8 * ncol], reads=[pLk], writes=[gsk])
                            P.dma('sp', gT_s[cc * 128:(cc + 1) * 128, 512 * bk:512 * bk + 8 * ncol], gs[:, :8 * ncol],
                                  reads=[gsk], writes=[('gT_s', cc, bk)])

                for gb in range(NBLK + 1):
                    if gb < NBLK:
                        emit_us(gb)
                        emit_scan(gb)
                    if gb > 0:
                        emit_y(gb - 1)
                    if gb < NBLK:
                        emit_xb(gb)
            P.barrier()

        def s5_phase_G(l, b):
            tiles = token_tiles(nb, b)
            with contextlib.ExitStack() as ph:
                gT = ph.enter_context(sbt("g5_g", [128, 16, LT], BF16))
                gws = Rot([ph.enter_context(sbt("g5_w%d" % i, [128, 16, 128], BF16)) for i in range(2)], 'g5_w')
                sgs = Rot([ph.enter_context(sbt("g5_s%d" % i, [128, 512], F32)) for i in range(2)], 'g5_s')
                yss = Rot([ph.enter_context(sbt("g5_y%d" % i, [128, 512], BF16)) for i in range(2)], 'g5_y')
                for kc in range(16):
                    P.dma('sp', gT[:, kc, :], gT_s[kc * 128:(kc + 1) * 128, :], writes=[('gT', kc)])
                gkeys = [('gT', kc) for kc in range(16)]
                Wg = wb_glu.rearrange("(k p) n -> p k n", p=128)
                for oc in range(16):
                    gw, gwk = gws.next()
                    P.dma('sp', gw[:], Wg[:, :, oc * 128:(oc + 1) * 128], writes=[gwk])
                    for (t0, nt, r) in tiles:
                        pG, pGk = next_ps()
                        for kc in range(16):
                            P.mm(pG[:, :nt], gw[:, kc, :], gT[:, kc, t0:t0 + nt], kc == 0, kc == 15,
                                 reads=[gwk, ('gT', kc)], writes=[pGk])
                        sg, sgk = sgs.next()
                        P.act(sg[:, :nt], pG[:, :nt], AF.Sigmoid, reads=[pGk, 'glub'], writes=[sgk], bias=glub[:, oc:oc + 1])
                        ys, ysk = yss.next()
                        P.tt('dve', ys[:, :nt], gT[:, oc, t0:t0 + nt], sg[:, :nt], ALU.mult, reads=[('gT', oc), sgk], writes=[ysk])
                        P.dma('pool', yT_s[oc * 128:(oc + 1) * 128, t0:t0 + nt], ys[:, :nt], reads=[ysk],
                              writes=[('yT_s', oc, t0)])
            P.barrier()

        for l in layers:
            kind = l % 3
            last = (l == DEPTH - 1)
            for b in range(nb):
                if kind == 0:
                    attn_phase_A(l, b)
                    if STOP[0] == 'A':
                        return finish()
                    attn_phase_B(l, b, last)
                    if STOP[0] == 'B':
                        return finish()
                elif kind == 2:
                    if b == 0:
                        hg_prepare(l)
                    hg_layer(l, b)
                    if STOP[0] == 'B':
                        return finish()
                else:
                    if b == 0:
                        s5_prepare()
                        if STOP[0] == 'prep':
                            return finish()
                    s5_phase_A(l, b)
                    if STOP[0] == 'A':
                        return finish()
                    s5_phase_B(l, b)
                    if STOP[0] == 'B1':
                        return finish()
                    s5_phase_G(l, b)
                    if STOP[0] == 'B':
                        return finish()
                phase_C(l, b, last)
        return finish()


INPUT_NAMES = ['x', 'c', 'ctx', 'c_ctx', 'ada_w', 'ada_b', 'ln_g', 'ln_b', 'w_out', 'attn_w_in', 'attn_sink',
               's5_w_in', 's5_lam_re', 's5_lam_im', 's5_log_step', 's5_b_re', 's5_b_im', 's5_c_re', 's5_c_im',
               's5_d', 's5_glu_w', 's5_glu_b', 'hg_w_in', 'hg_lb', 'hg_norm_g']


def kernel(**inputs):
    B = inputs['x'].shape[0]
    nb = B // NCORES
    nc = build(nb)
    consts = make_consts()
    in_maps = []
    for core in range(NCORES):
        m = {}
        for k in INPUT_NAMES:
            a = np.ascontiguousarray(np.asarray(inputs[k], dtype=np.float32))
            if k in ('x', 'c', 'ctx'):
                a = np.ascontiguousarray(a[core * nb:(core + 1) * nb])
            m[k] = a
        m['consts'] = consts
        m['posc'] = make_pos()
        in_maps.append(m)
    res = run_bass_kernel_spmd(nc, in_maps, core_ids=list(range(NCORES)))
    return np.concatenate([r['out'] for r in res.results], axis=0).astype(np.float32)
```
